# Optimizing a Trainium2 kernel written in Bass

```python
import math
import jax, jax.numpy as jnp
from jax import lax
import numpy as np

D_MODEL = 1024
BATCH = 4
SEQ = 8192
DEPTH = 2

HEAD_DIM = 64
SWA_HEADS = 6
SWA_KV_HEADS = 2
SWA_GROUP = SWA_HEADS // SWA_KV_HEADS
SWA_WINDOW = 128
DIL_WINDOWS = (128, 512, 2048)
DIL_RATES = (1, 4, 16)
DIL_GROUPS = 3
DIL_HEADS_PER_GROUP = 2
DIL_HEADS = DIL_GROUPS * DIL_HEADS_PER_GROUP
MEM_HEADS = 4
MEM_LEN = 256
N_BUCKETS = 32
MAX_DISTANCE = 2048
SELF_HEADS = SWA_HEADS + DIL_HEADS
FFN_DIM = 2816
BLOCK = 128
N_BRANCH = 3
N_SANDWICH = 6
EPS = 1e-6

A_Q = SWA_HEADS * HEAD_DIM
A_KV = SWA_KV_HEADS * HEAD_DIM
B_W = DIL_HEADS * HEAD_DIM
C_W = MEM_HEADS * HEAD_DIM
P_IN = A_Q + 2 * A_KV + 3 * B_W + C_W

kernel_name = "hybrid_swa_dilated_memory_macaron_block"


def rms_norm(x, gain):
    x32 = x.astype(jnp.float32)
    y = x32 * lax.rsqrt(jnp.mean(x32 * x32, axis=-1, keepdims=True) + EPS)
    return (y * gain.astype(jnp.float32)).astype(x.dtype)


def swiglu(x, w_in, w_out):
    a, b = jnp.split(x @ w_in, 2, axis=-1)
    return (jax.nn.silu(a) * b) @ w_out


def t5_bucket(dist):
    max_exact = N_BUCKETS // 2
    d = jnp.maximum(dist, 1).astype(jnp.float32)
    large = max_exact + (jnp.log(d / max_exact) / math.log(MAX_DISTANCE / max_exact)
                         * (N_BUCKETS - max_exact)).astype(jnp.int32)
    large = jnp.minimum(large, N_BUCKETS - 1)
    return jnp.where(dist < max_exact, dist, large)


def band_distance():
    row = jnp.arange(BLOCK)[:, None]
    col = jnp.arange(2 * BLOCK)[None, :]
    return jnp.maximum(row + BLOCK - col, 0)


def band_mask(n_blocks, max_dist):
    row = jnp.arange(BLOCK)[:, None]
    col = jnp.arange(2 * BLOCK)[None, :]
    dist = row + BLOCK - col
    key_pos = jnp.arange(n_blocks)[:, None, None] * BLOCK - BLOCK + col[None]
    return ((dist >= 0) & (dist <= max_dist))[None] & (key_pos >= 0)


def key_windows(t):
    n, l = t.shape[:2]
    nb = l // BLOCK
    tp = jnp.pad(t, ((0, 0), (BLOCK, 0), (0, 0), (0, 0))).reshape(n, nb + 1, BLOCK, *t.shape[2:])
    return jnp.concatenate([tp[:, :-1], tp[:, 1:]], axis=2)


def banded_attention(q, k, v, bias, max_dist, sink):
    n, l, hk, g, hd = q.shape
    nb = l // BLOCK
    qb = q.reshape(n, nb, BLOCK, hk, g, hd)
    kw = key_windows(k)
    vw = key_windows(v)
    s = jnp.einsum('nbqhgd,nbkhd->nbhgqk', qb, kw,
                   preferred_element_type=jnp.float32) * (hd ** -0.5) + bias
    mask = band_mask(nb, max_dist)[None, :, None, None]
    s = jnp.where(mask, s, -jnp.inf)
    m = jnp.max(s, axis=-1)
    if sink is not None:
        sink32 = sink.astype(jnp.float32)[None, None, :, :, None]
        m = jnp.maximum(m, sink32)
    p = jnp.exp(s - m[..., None])
    denom = jnp.sum(p, axis=-1)
    if sink is not None:
        denom = denom + jnp.exp(sink32 - m)
    o = jnp.einsum('nbhgqk,nbkhd->nbqhgd', p, vw.astype(jnp.float32))
    o = o / jnp.moveaxis(denom, -1, 2)[..., None]
    lse = jnp.moveaxis(m + jnp.log(denom), -1, 2)
    return o.reshape(n, l, hk, g, hd), lse.reshape(n, l, hk, g)


def dilated_group(q, k, v, rate, window, bias):
    b, s, h, hd = q.shape
    l = s // rate
    lp = -(-l // BLOCK) * BLOCK

    def to_sub(t):
        t = t.reshape(b, l, rate, h, hd).transpose(0, 2, 1, 3, 4).reshape(b * rate, l, h, hd)
        return jnp.pad(t, ((0, 0), (0, lp - l), (0, 0), (0, 0)))

    o, lse = banded_attention(to_sub(q)[:, :, :, None], to_sub(k), to_sub(v),
                              bias, window // rate, None)
    o = o[:, :l, :, 0].reshape(b, rate, l, h, hd).transpose(0, 2, 1, 3, 4).reshape(b, s, h, hd)
    lse = lse[:, :l, :, 0].reshape(b, rate, l, h).transpose(0, 2, 1, 3).reshape(b, s, h)
    return o, lse


def mixing_sublayer(h, mem, bias_a, bias_b, mem_gain, w_in, sinks, w_mem_kv,
                    w_gate, b_gate, w_br_a, w_br_b, w_br_c, w_o):
    b, s, _ = h.shape
    z = h @ w_in
    qa, ka, va, qb, kb, vb, qc = jnp.split(
        z, np.cumsum([A_Q, A_KV, A_KV, B_W, B_W, B_W])[:].tolist(), axis=-1)

    oa, _ = banded_attention(
        qa.reshape(b, s, SWA_KV_HEADS, SWA_GROUP, HEAD_DIM),
        ka.reshape(b, s, SWA_KV_HEADS, HEAD_DIM),
        va.reshape(b, s, SWA_KV_HEADS, HEAD_DIM),
        bias_a, SWA_WINDOW - 1, sinks.reshape(SWA_KV_HEADS, SWA_GROUP))
    oa = oa.reshape(b, s, A_Q).astype(h.dtype)

    qb = qb.reshape(b, s, DIL_GROUPS, DIL_HEADS_PER_GROUP, HEAD_DIM)
    kb = kb.reshape(b, s, DIL_GROUPS, DIL_HEADS_PER_GROUP, HEAD_DIM)
    vb = vb.reshape(b, s, DIL_GROUPS, DIL_HEADS_PER_GROUP, HEAD_DIM)
    outs, lses = [], []
    for gi in range(DIL_GROUPS):
        o_g, l_g = dilated_group(qb[:, :, gi], kb[:, :, gi], vb[:, :, gi],
                                 DIL_RATES[gi], DIL_WINDOWS[gi], bias_b[gi])
        outs.append(o_g)
        lses.append(l_g)
    o_stack = jnp.stack(outs, axis=2)
    alpha = jax.nn.softmax(jnp.stack(lses, axis=2), axis=2)
    ob = (o_stack * alpha[..., None]).reshape(b, s, B_W).astype(h.dtype)

    mh = rms_norm(mem, mem_gain)
    kc, vc = jnp.split(mh @ w_mem_kv, 2, axis=-1)
    kc = kc.reshape(b, MEM_LEN, MEM_HEADS, HEAD_DIM)
    vc = vc.reshape(b, MEM_LEN, MEM_HEADS, HEAD_DIM)
    qc = qc.reshape(b, s, MEM_HEADS, HEAD_DIM)
    sc = jnp.einsum('bshd,bmhd->bhsm', qc, kc,
                    preferred_element_type=jnp.float32) * (HEAD_DIM ** -0.5)
    pc = jax.nn.softmax(sc, axis=-1)
    oc = jnp.einsum('bhsm,bmhd->bshd', pc, vc.astype(jnp.float32))
    oc = oc.reshape(b, s, C_W).astype(h.dtype)

    gates = jax.nn.sigmoid(h @ w_gate + b_gate.reshape(-1)).reshape(b, s, N_BRANCH, D_MODEL)
    merged = (gates[:, :, 0] * (oa @ w_br_a)
              + gates[:, :, 1] * (ob @ w_br_b)
              + gates[:, :, 2] * (oc @ w_br_c))
    return merged @ w_o


def setup_inputs(seed: int = 0) -> dict:
    key = jax.random.key(seed)
    ks = jax.random.split(key, 20)
    f32 = jnp.float32

    def nrm(k, shape, fan_in):
        return jax.random.normal(k, shape, f32) * (fan_in ** -0.5)

    return {
        "x": jax.random.normal(ks[0], (BATCH, SEQ, D_MODEL), f32),
        "mem": jax.random.normal(ks[1], (BATCH, MEM_LEN, D_MODEL), f32),
        "rel_bias": 0.5 * jax.random.normal(ks[2], (N_BUCKETS, SELF_HEADS), f32),
        "norm_gain": 1.0 + 0.05 * jax.random.normal(ks[3], (DEPTH, N_SANDWICH, D_MODEL), f32),
        "mem_norm_gain": 1.0 + 0.05 * jax.random.normal(ks[4], (DEPTH, D_MODEL), f32),
        "w_ffn1_in": nrm(ks[5], (DEPTH, D_MODEL, 2 * FFN_DIM), D_MODEL),
        "w_ffn1_out": nrm(ks[6], (DEPTH, FFN_DIM, D_MODEL), FFN_DIM),
        "w_in": nrm(ks[7], (DEPTH, D_MODEL, P_IN), D_MODEL),
        "sinks": 0.5 * jax.random.normal(ks[8], (DEPTH, SWA_HEADS), f32),
        "w_mem_kv": nrm(ks[9], (DEPTH, D_MODEL, 2 * C_W), D_MODEL),
        "w_gate": nrm(ks[10], (DEPTH, D_MODEL, N_BRANCH * D_MODEL), D_MODEL),
        "b_gate": 0.01 * jax.random.normal(ks[11], (DEPTH, N_BRANCH, D_MODEL), f32),
        "w_br_a": nrm(ks[12], (DEPTH, A_Q, D_MODEL), A_Q),
        "w_br_b": nrm(ks[13], (DEPTH, B_W, D_MODEL), B_W),
        "w_br_c": nrm(ks[14], (DEPTH, C_W, D_MODEL), C_W),
        "w_o": nrm(ks[15], (DEPTH, D_MODEL, D_MODEL), D_MODEL),
        "w_ffn2_in": nrm(ks[16], (DEPTH, D_MODEL, 2 * FFN_DIM), D_MODEL),
        "w_ffn2_out": nrm(ks[17], (DEPTH, FFN_DIM, D_MODEL), FFN_DIM),
    }


def reference(x, mem, rel_bias, norm_gain, mem_norm_gain, w_ffn1_in, w_ffn1_out,
              w_in, sinks, w_mem_kv, w_gate, b_gate, w_br_a, w_br_b, w_br_c, w_o,
              w_ffn2_in, w_ffn2_out):
    table = rel_bias.astype(jnp.float32)
    dist = band_distance()
    bias_a = table[t5_bucket(dist)][..., :SWA_HEADS].transpose(2, 0, 1)
    bias_a = bias_a.reshape(SWA_KV_HEADS, SWA_GROUP, BLOCK, 2 * BLOCK)
    bias_b = []
    for gi in range(DIL_GROUPS):
        h0 = SWA_HEADS + gi * DIL_HEADS_PER_GROUP
        bg = table[t5_bucket(dist * DIL_RATES[gi])][..., h0:h0 + DIL_HEADS_PER_GROUP]
        bias_b.append(bg.transpose(2, 0, 1)[:, None])

    for l in range(DEPTH):
        g = norm_gain[l]
        x = x + 0.5 * rms_norm(swiglu(rms_norm(x, g[0]), w_ffn1_in[l], w_ffn1_out[l]), g[1])
        y = mixing_sublayer(rms_norm(x, g[2]), mem, bias_a, bias_b, mem_norm_gain[l],
                            w_in[l], sinks[l], w_mem_kv[l], w_gate[l], b_gate[l],
                            w_br_a[l], w_br_b[l], w_br_c[l], w_o[l])
        x = x + rms_norm(y, g[3])
        x = x + 0.5 * rms_norm(swiglu(rms_norm(x, g[4]), w_ffn2_in[l], w_ffn2_out[l]), g[5])
    return x
```

```python
import math
import numpy as np
import ml_dtypes
import concourse.bass as bass
import concourse.mybir as mybir
from concourse.bass_utils import run_bass_kernel_spmd
from contextlib import ExitStack

F32 = mybir.dt.float32
BF16 = mybir.dt.bfloat16
AF = mybir.ActivationFunctionType
ALU = mybir.AluOpType

D = 1024
NCH = 8
TT = 512
FF = 2816
JH = 22
NL = 2
EPS = 1e-6
N_CORES = 8
DBG_SKIP = set()

COMPUTE = ("pe", "act", "dve", "pool")
STREAM_OF = {"pe": "pe", "act": "act", "dve": "dve", "pool": "pool",
             "sp": "sp", "actq": "act", "poolq": "pool"}


class Res:
    __slots__ = ("name", "w", "r", "dsem", "dcnt")

    def __init__(self, name):
        self.name = name
        self.w = []
        self.r = []
        self.dsem = None
        self.dcnt = 0


class Prog:
    def __init__(self, nc, es, n_dma_sems=92):
        self.nc = nc
        self.ops = {s: [] for s in ("pe", "act", "dve", "pool", "sp")}
        self.sem = {e: es.enter_context(nc.semaphore("sem_" + e)) for e in COMPUTE}
        self.free_dma = [es.enter_context(nc.semaphore(f"dsem{i}")) for i in range(n_dma_sems)]
        self.cnt = {e: 0 for e in COMPUTE}
        self.seen = {s: {} for s in self.ops}
        self.dma_res = []

    def res(self, name):
        return Res(name)

    def _dsem(self, r):
        if r.dsem is None:
            r.dsem = self.free_dma.pop()
            self.dma_res.append(r)
        return r.dsem

    def _waits(self, stream, evs, own=None):
        need = {}
        for ev in evs:
            k, v = ev
            if need.get(id(k), (None, -1))[1] < v:
                need[id(k)] = (k, v)
        seen = self.seen[stream]
        for kid, (k, v) in need.items():
            if own is not None and k is own and stream == "pe":
                continue
            if seen.get(kid, -1) >= v:
                continue
            seen[kid] = v
            self.ops[stream].append(("wait", k, v))

    def _deps(self, reads, writes, accumulate):
        evs = []
        for r in reads:
            evs.extend(r.w)
        for w in writes:
            if not accumulate:
                evs.extend(w.w)
            evs.extend(w.r)
        return evs

    def _record(self, ev, reads, writes, accumulate):
        for r in reads:
            r.r.append(ev)
        for w in writes:
            if accumulate:
                w.w.append(ev)
            else:
                w.w = [ev]
            w.r = []

    def op(self, eng, fn, reads=(), writes=()):
        stream = STREAM_OF[eng]
        sem = self.sem[eng]
        self._waits(stream, self._deps(reads, writes, False), own=sem)
        self.cnt[eng] += 1
        ev = (sem, self.cnt[eng])
        self.ops[stream].append(("inst", fn, sem, 1))
        self._record(ev, reads, writes, False)

    def mm(self, fns, reads=(), writes=()):
        sem = self.sem["pe"]
        self._waits("pe", self._deps(reads, writes, False), own=sem)
        for fn in fns[:-1]:
            self.ops["pe"].append(("inst", fn, None, 0))
        self.cnt["pe"] += 1
        ev = (sem, self.cnt["pe"])
        self.ops["pe"].append(("inst", fns[-1], sem, 1))
        self._record(ev, reads, writes, False)

    def dma(self, queue, fn, sb, reads=(), writes=(), accumulate=False, inc=16):
        stream = STREAM_OF[queue]
        self._waits(stream, self._deps(reads, writes, accumulate))
        k = self._dsem(sb)
        sb.dcnt += inc
        ev = (k, sb.dcnt)
        self.ops[stream].append(("inst", fn, k, inc))
        self._record(ev, reads, writes, accumulate)
        return ev

    def all_events(self, stream):
        evs = [(self.sem[e], self.cnt[e]) for e in COMPUTE if self.cnt[e] > 0]
        evs += [(r.dsem, r.dcnt) for r in self.dma_res
                if r.dcnt > 0 and (stream == "pool" or not r.name.startswith("cc"))]
        return evs

    def barrier(self, streams=("pe", "act", "dve", "pool", "sp")):
        for s in streams:
            self._waits(s, self.all_events(s))

    def emit(self, block):
        ops = self.ops

        def run(eng_obj, lst):
            for o in lst:
                if o[0] == "wait":
                    eng_obj.wait_ge(o[1], o[2])
                else:
                    ins = o[1](eng_obj)
                    if o[2] is not None:
                        ins.then_inc(o[2], o[3])

        @block.tensor
        def _(e):
            run(e, ops["pe"])

        @block.scalar
        def _(e):
            run(e, ops["act"])

        @block.vector
        def _(e):
            run(e, ops["dve"])

        @block.gpsimd
        def _(e):
            run(e, ops["pool"])

        @block.sync
        def _(e):
            run(e, ops["sp"])


class Arena:
    def __init__(self, ap_f32):
        self.ap = ap_f32
        self.n = ap_f32.shape[1]
        self.off = 0

    def reset(self):
        self.off = 0

    def alloc(self, free_shape, dtype):
        n = int(np.prod(free_shape))
        if dtype == BF16:
            assert n % 2 == 0
            n32 = n // 2
        else:
            n32 = n
        n32a = (n32 + 7) // 8 * 8
        assert self.off + n32a <= self.n, f"arena overflow {self.off}+{n32a}>{self.n}"
        v = self.ap[:, self.off:self.off + n32]
        self.off += n32a
        if dtype == BF16:
            v = v.bitcast(BF16)
        if len(free_shape) == 2:
            v = v.rearrange("p (a b) -> p a b", a=free_shape[0])
        elif len(free_shape) == 3:
            v = v.rearrange("p (a b c) -> p a b c", a=free_shape[0], b=free_shape[1])
        return v


QA_HEAD_ORDER = [0, 3, 1, 4, 2, 5]
SELF_CHUNKS = [
    (0, 0, 0, 1, 127, (0, 3), True),
    (1, 0, 0, 1, 127, (1, 4), True),
    (2, 0, 0, 1, 127, (2, 5), True),
    (3, 1, 1, 1, 128, (6, 7), False),
    (4, 2, 2, 4, 128, (8, 9), False),
    (5, 3, 3, 16, 128, (10, 11), False),
]
K_RATES = [1, 1, 4, 16]
Q_RATES = [1, 1, 1, 1, 4, 16, 1, 1]


def _t5_bucket(dist):
    dist = np.asarray(dist)
    me = 16
    dd = np.maximum(dist, 1).astype(np.float32)
    large = me + (np.log(dd / np.float32(me)) / np.float32(math.log(2048 / me))
                  * np.float32(32 - me)).astype(np.int32)
    large = np.minimum(large, 31)
    return np.where(dist < me, dist, large)


def _win_perm():
    qa = np.concatenate([np.arange(64 * h, 64 * h + 64) for h in QA_HEAD_ORDER])
    ka = np.arange(384, 512)
    va = np.arange(512, 640)
    qb = np.arange(640, 1024)
    kb = np.arange(1024, 1408)
    vb = np.arange(1408, 1792)
    qc = np.arange(1792, 2048)
    return np.concatenate([qa, qb, qc, ka, kb, va, vb])


def _static_bias_tables(rel_bias):
    kk = np.arange(128)[:, None]
    qq = np.arange(128)[None, :]
    bias = np.zeros((6, 128, 2, 2, 128), np.float32)
    mask = np.zeros((6, 128, 2, 2, 128), np.float32)
    for sc, (_, _, _, r, md, cols, _) in enumerate(SELF_CHUNKS):
        d_prev = qq + 128 - kk
        d_cur = qq - kk
        v_prev = (d_prev <= md)
        v_cur = (d_cur >= 0)
        for hi, col in enumerate(cols):
            tab = rel_bias[:, col]
            bias[sc, :, hi, 0, :] = np.where(v_prev, tab[_t5_bucket(d_prev * r)], 0.0)
            bias[sc, :, hi, 1, :] = np.where(v_cur, tab[_t5_bucket(np.maximum(d_cur, 0) * r)], 0.0)
            mask[sc, :, hi, 0, :] = v_prev
            mask[sc, :, hi, 1, :] = v_cur
    return bias.reshape(6, 128, 512), mask.reshape(6, 128, 512)


def _chunk_cols(v):
    v = np.asarray(v, np.float32)
    lead = v.shape[:-1]
    return np.ascontiguousarray(v.reshape(-1, NCH, 128).transpose(2, 0, 1).reshape(128, -1)), lead


def prep_shared(inp):
    sh = {}
    perm = _win_perm()
    sh["w_ffn1_in"] = np.ascontiguousarray(inp["w_ffn1_in"], np.float32)
    sh["w_ffn1_out"] = np.ascontiguousarray(inp["w_ffn1_out"], np.float32)
    sh["w_ffn2_in"] = np.ascontiguousarray(inp["w_ffn2_in"], np.float32)
    sh["w_ffn2_out"] = np.ascontiguousarray(inp["w_ffn2_out"], np.float32)
    sh["w_in"] = np.ascontiguousarray(np.asarray(inp["w_in"], np.float32)[:, :, perm])
    sh["w_gate"] = np.ascontiguousarray(inp["w_gate"], np.float32)
    rows_a = np.concatenate([np.arange(64 * h, 64 * h + 64) for h in QA_HEAD_ORDER])
    wbr = np.concatenate([np.asarray(inp["w_br_a"], np.float32)[:, rows_a, :],
                          np.asarray(inp["w_br_b"], np.float32),
                          np.asarray(inp["w_br_c"], np.float32)], axis=1)
    sh["w_br"] = np.ascontiguousarray(wbr)
    sh["w_o"] = np.ascontiguousarray(inp["w_o"], np.float32)
    sh["w_mem_kv"] = np.ascontiguousarray(inp["w_mem_kv"], np.float32)
    g, _ = _chunk_cols(inp["norm_gain"])
    sh["gains"] = g
    mg, _ = _chunk_cols(inp["mem_norm_gain"])
    sh["mgain"] = mg
    bg, _ = _chunk_cols(inp["b_gate"])
    sh["bgate"] = bg
    sinks = np.asarray(inp["sinks"], np.float32)
    sk = np.zeros((128, NL * 3), np.float32)
    for l in range(NL):
        for ci in range(3):
            sk[0:64, l * 3 + ci] = sinks[l, QA_HEAD_ORDER[2 * ci]]
            sk[64:128, l * 3 + ci] = sinks[l, QA_HEAD_ORDER[2 * ci + 1]]
    sh["sinks"] = sk
    bias, mask = _static_bias_tables(np.asarray(inp["rel_bias"], np.float32))
    sh["biasT"] = np.ascontiguousarray(bias.transpose(1, 0, 2))
    sh["maskT"] = np.ascontiguousarray(mask.transpose(1, 0, 2))
    sh["ident"] = np.eye(128, dtype=np.float32)
    return sh


def build(TOK, mode="fused", n_cores=N_CORES):
    NT = TOK // TT
    fused = mode == "fused"
    nc = bass.Bass("TRN2", target_bir_lowering=False)

    def din(name, shape, dt=F32):
        return nc.dram_tensor(name, list(shape), dt, kind="ExternalInput").ap()

    def dout(name, shape, dt=F32):
        return nc.dram_tensor(name, list(shape), dt, kind="ExternalOutput").ap()

    def dint(name, shape, dt=F32):
        return nc.dram_tensor(name, list(shape), dt).ap()

    def dscr(name, shape, dt, is_in, is_out):
        if fused:
            return dint(name, shape, dt), None
        i = din(name + "_i", shape, dt) if is_in else None
        o = dout(name + "_o", shape, dt) if is_out else None
        return i, o

    first = mode in ("fused", "A")
    last = mode in ("fused", "C")
    layers_s1 = {"fused": [0, 1], "A": [0], "B": [1], "C": []}[mode]
    layers_s2 = {"fused": [0, 1], "A": [], "B": [0], "C": [1]}[mode]

    x_in = din("x", [TOK, D]) if first else None
    mem_in = din("mem", [256, D])
    hv_in = din("halo_valid", [128, 1])
    w1i = [din("w_ffn1_in", [NL, D, 2 * FF]), din("w_ffn2_in", [NL, D, 2 * FF])]
    w1o = [din("w_ffn1_out", [NL, FF, D]), din("w_ffn2_out", [NL, FF, D])]
    w_in = din("w_in", [NL, D, 2048])
    w_gate = din("w_gate", [NL, D, 3 * D])
    w_br = din("w_br", [NL, D, D])
    w_o = din("w_o", [NL, D, D])
    w_mkv = din("w_mem_kv", [NL, D, 512])
    gains_in = din("gains", [128, NL * 6 * NCH])
    mgain_in = din("mgain", [128, NL * NCH])
    bgate_in = din("bgate", [128, NL * 3 * NCH])
    sinks_in = din("sinks", [128, NL * 3])
    biasT_in = din("biasT", [128, 6, 512])
    maskT_in = din("maskT", [128, 6, 512])
    ident_in = din("ident", [128, 128])
    out_ap = dout("out", [TOK, D]) if last else None

    HROWS = 2 * 2816 * 128 // 1024
    if fused:
        XS_r = XS_w = dint("XS", [NT, 128, NCH * TT], F32)
        QT_r = QT_w = dint("QT", [128, 8, TOK], BF16)
        KT_r = KT_w = dint("KT", [128, 4, TOK], BF16)
        VT_r = VT_w = dint("VT", [4, TOK, 128], BF16)
        HS = dint("HS", [HROWS, 1024], BF16)
        HR = dint("HR", [2 * HROWS, 1024], BF16)
    else:
        XS_r = din("XS_i", [NT, 128, NCH * TT], F32) if mode in ("B", "C") else None
        XS_w = dout("XS_o", [NT, 128, NCH * TT], F32) if mode in ("A", "B") else None
        QT_r = din("QT_i", [128, 8, TOK], BF16) if mode in ("B", "C") else None
        KT_r = din("KT_i", [128, 4, TOK], BF16) if mode in ("B", "C") else None
        VT_r = din("VT_i", [4, TOK, 128], BF16) if mode in ("B", "C") else None
        QT_w = dout("QT_o", [128, 8, TOK], BF16) if mode in ("A", "B") else None
        KT_w = dout("KT_o", [128, 4, TOK], BF16) if mode in ("A", "B") else None
        VT_w = dout("VT_o", [4, TOK, 128], BF16) if mode in ("A", "B") else None
        HS = None
        HR = din("HR", [2 * HROWS, 1024], BF16) if mode in ("B", "C") else None
    OT = dint("OT", [128, 8, TOK], BF16) if fused or not layers_s2 else dout("OT_dbg", [128, 8, TOK], BF16)
    W1 = [[dint(f"W1_{l}_{f}", [11, 128, 4096], BF16) for f in range(2)] for l in range(NL)]
    W2 = [[dint(f"W2_{l}_{f}", [8, 128, JH * 128], BF16) for f in range(2)] for l in range(NL)]
    WIN = [dint(f"WIN_{l}", [4, 128, 4096], BF16) for l in range(NL)]
    WG = [dint(f"WG_{l}", [6, 128, 4096], BF16) for l in range(NL)]
    WBR = [dint(f"WBR_{l}", [2, 128, 4096], BF16) for l in range(NL)]
    WO = [dint(f"WO_{l}", [2, 128, 4096], BF16) for l in range(NL)]
    WM = [dint(f"WM_{l}", [128, 4096], BF16) for l in range(NL)]

    es = ExitStack()
    with es:
        P = Prog(nc, es)

        def sb(name, shape, dt):
            return es.enter_context(nc.sbuf_tensor("sb_" + name, list(shape), dt))

        ident = sb("ident", [128, 128], F32)
        ones = sb("ones", [128, 128], BF16)
        gains = sb("gains", [128, NL * 6 * NCH], F32)
        mgain = sb("mgain", [128, NL * NCH], F32)
        bgate = sb("bgate", [128, NL * 3 * NCH], F32)
        sinke = sb("sinke", [128, NL * 3], F32)
        hval = sb("hval", [128, 1], F32)
        ccdummy = sb("ccdummy", [128, 8], F32)
        Emat = sb("Emat", [128, 6, 512], BF16)
        Efirst = sb("Efirst", [128, 6, 512], BF16)
        KC = sb("KC", [128, NL, 2, 2, 256], BF16)
        VC = sb("VC", [128, NL, 2, 256], BF16)
        psum = [es.enter_context(nc.psum_tensor(f"ps{i}", [128, 512], F32)) for i in range(8)]
        r_psum = [P.res(f"ps{i}") for i in range(8)]
        ps_i = [0]
        ARENA_W = 43000
        arena_t = sb("arena", [128, ARENA_W], F32)
        arena = Arena(arena_t[:, :])
        r_const = P.res("const")

        def next_ps():
            i = ps_i[0] % 7
            ps_i[0] += 1
            return psum[i], r_psum[i]

        def stat_ps():
            return psum[7], r_psum[7]

        def load_const(dst, src, nm):
            r = P.res(nm)
            P.dma("sp", lambda e, d=dst, s=src: e.dma_start(out=d, in_=s), r, writes=[r])
            return r

        r_ident = load_const(ident[:, :], ident_in[:, :], "ident")
        r_gains = load_const(gains[:, :], gains_in[:, :], "gains")
        r_mgain = load_const(mgain[:, :], mgain_in[:, :], "mgain")
        r_bgate = load_const(bgate[:, :], bgate_in[:, :], "bgate")
        r_sink = load_const(sinke[:, :], sinks_in[:, :], "sinks")
        r_hval = load_const(hval[:, :], hv_in[:, :], "hval")
        r_ones = P.res("ones")
        P.op("pool", lambda e: e.memset(ones[:, :], 1.0), writes=[r_ones])
        P.op("pool", lambda e: e.memset(KC[:, :, :, :, :].rearrange("p a b c d -> p (a b c d)"), 0.0), writes=[r_const])
        P.op("act", lambda e: e.activation(out=sinke[:, :], in_=sinke[:, :], func=AF.Exp),
             reads=[r_sink], writes=[r_sink])
        r_E = P.res("E")
        if layers_s2:
            arena.reset()
            bt = arena.alloc([6, 512], F32)
            mt = arena.alloc([6, 512], F32)
            r_bt, r_mt = P.res("bt"), P.res("mt")
            P.dma("sp", lambda e: e.dma_start(out=bt, in_=biasT_in[:, :, :]), r_bt, writes=[r_bt])
            P.dma("sp", lambda e: e.dma_start(out=mt, in_=maskT_in[:, :, :]), r_mt, writes=[r_mt])
            P.op("act", lambda e: e.activation(out=bt, in_=bt, func=AF.Exp), reads=[r_bt], writes=[r_bt])
            P.op("dve", lambda e: e.tensor_tensor(out=Emat[:, :, :], in0=bt, in1=mt, op=ALU.mult),
                 reads=[r_bt, r_mt], writes=[r_E])
            P.op("dve", lambda e: e.tensor_copy(out=Efirst[:, :, :], in_=Emat[:, :, :]),
                 reads=[r_E], writes=[r_E])
            for hi in range(2):
                P.op("dve", lambda e, hi=hi: e.tensor_scalar(
                    out=Efirst[:, :, hi * 256:hi * 256 + 128], in0=Emat[:, :, hi * 256:hi * 256 + 128],
                    scalar1=hval[:, 0:1], scalar2=None, op0=ALU.mult),
                    reads=[r_E, r_hval], writes=[r_E])

        P.barrier()

        r_W1 = [[P.res(f"W1{l}{f}") for f in range(2)] for l in range(NL)]
        r_W2 = [[P.res(f"W2{l}{f}") for f in range(2)] for l in range(NL)]
        r_WIN = [P.res(f"WIN{l}") for l in range(NL)]
        r_WG = [P.res(f"WG{l}") for l in range(NL)]
        r_WBR = [P.res(f"WBR{l}") for l in range(NL)]
        r_WO = [P.res(f"WO{l}") for l in range(NL)]
        r_WM = [P.res(f"WM{l}") for l in range(NL)]

        cast_tasks = []

        def cast(dst, src, r):
            cast_tasks.append((dst, src, r))

        def emit_casts(n=None):
            k = len(cast_tasks) if n is None else min(n, len(cast_tasks))
            for _ in range(k):
                dst, src, r = cast_tasks.pop(0)
                P.dma("poolq", lambda e, d=dst, s=src: e.dma_start(out=d, in_=s), r, writes=[r], accumulate=True)

        def cast_w1(l, f):
            src = w1i[f][l].rearrange("(c p) n -> p c n", p=128)
            for blk in range(11):
                dst = W1[l][f][blk].rearrange("p (c ab n) -> p c ab n", c=8, ab=2)
                for ab in range(2):
                    c0 = ab * FF + blk * 256
                    cast(dst[:, :, ab, :], src[:, :, c0:c0 + 256], r_W1[l][f])

        def cast_w2(l, f):
            src = w1o[f][l].rearrange("(j p) n -> p j n", p=128)
            for m in range(8):
                dst = W2[l][f][m].rearrange("p (j n) -> p j n", j=JH)
                cast(dst, src[:, :, m * 128:(m + 1) * 128], r_W2[l][f])

        def cast_cols(dst_blocks, src2d, nblk, width, r):
            src = src2d.rearrange("(c p) n -> p c n", p=128)
            for b in range(nblk):
                dst = dst_blocks[b].rearrange("p (c n) -> p c n", c=8)
                cast(dst, src[:, :, b * width:(b + 1) * width], r)

        need_s1 = set(layers_s1)
        need_s3 = set(layers_s2)
        for l in range(NL):
            cast(WM[l].rearrange("p (c n) -> p c n", c=8),
                 w_mkv[l].rearrange("(c p) n -> p c n", p=128), r_WM[l])
        for l in range(NL):
            if l in need_s1:
                cast_w1(l, 0)
                cast_w2(l, 0)
                cast_cols(WIN[l], w_in[l], 4, 512, r_WIN[l])
                if l == 0 and fused:
                    n_first = len(cast_tasks)
            if l in need_s3:
                cast_cols(WG[l], w_gate[l], 6, 512, r_WG[l])
                cast_cols(WBR[l], w_br[l], 2, 512, r_WBR[l])
                cast_cols(WO[l], w_o[l], 2, 512, r_WO[l])
                cast_w1(l, 1)
                cast_w2(l, 1)
        if fused:
            emit_casts(n_first)
        else:
            emit_casts()

        class WS:
            pass

        def alloc_token_stage():
            arena.reset()
            w = WS()
            w.X = [arena.alloc([NCH, TT], F32) for _ in range(2)]
            w.rX = [P.res("X0"), P.res("X1")]
            w.Y = arena.alloc([NCH, TT], F32)
            w.rY = P.res("Y")
            w.H = arena.alloc([NCH, TT], BF16)
            w.rH = P.res("H")
            w.G = arena.alloc([JH, TT], BF16)
            w.rG = [P.res(f"G{j}") for j in range(JH)]
            w.slots = [arena.alloc([4096], BF16) for _ in range(5)]
            w.rS = [P.res(f"slot{i}") for i in range(5)]
            w.si = 0
            w.rs = [arena.alloc([TT], F32) for _ in range(2)]
            w.r_rs = [P.res("rs0"), P.res("rs1")]
            w.rsi = 0
            w.tmp = [arena.alloc([TT], F32) for _ in range(3)]
            w.r_tmp = [P.res(f"tmp{i}") for i in range(3)]
            w.tmi = 0
            w.sil = [arena.alloc([TT], BF16) for _ in range(3)]
            w.r_sil = [P.res(f"sil{i}") for i in range(3)]
            w.sli = 0
            w.gt = [arena.alloc([TT], BF16) for _ in range(3)]
            w.r_gt = [P.res(f"gt{i}") for i in range(3)]
            w.M = arena.alloc([NCH, TT], BF16)
            w.rM = P.res("M")
            w.O = arena.alloc([NCH, TT], BF16)
            w.rO = P.res("Osb")
            w.QK = arena.alloc([12, TT], BF16)
            w.rQK = [P.res(f"QK{i}") for i in range(12)]
            w.V = arena.alloc([4, 512], BF16)
            w.rV = P.res("Vst")
            return w

        def slot_load(w, src_ap, r_src):
            i = w.si % len(w.slots)
            w.si += 1
            s, r = w.slots[i], w.rS[i]
            n = src_ap.shape[1]
            P.dma("sp", lambda e, s=s, a=src_ap: e.dma_start(out=s[:, 0:n], in_=a), r, reads=[r_src], writes=[r])
            return s, r

        def stat_chunk(w, c, src, r_src):
            ps, rps = stat_ps()
            P.op("act", lambda e: e.activation(out=w.H[:, c, :], in_=src, func=AF.Square),
                 reads=[r_src], writes=[w.rH])
            P.mm([lambda e: e.matmul(ps[:, :], lhsT=ones[:, :], rhs=w.H[:, c, :],
                                     start=(c == 0), stop=(c == NCH - 1))],
                 reads=[w.rH, r_ones], writes=[rps])

        def rstd_from_stats(w, alpha=1.0):
            ps, rps = stat_ps()
            i = w.rsi % 2
            w.rsi += 1
            rs, r_rs = w.rs[i], w.r_rs[i]
            a2 = float(alpha) ** 2
            P.op("act", lambda e: e.activation(out=rs, in_=ps[:, :], func=AF.Sqrt, bias=EPS / a2,
                                               scale=1.0 / (D * a2)), reads=[rps], writes=[r_rs])
            P.op("dve", lambda e: e.reciprocal(out=rs, in_=rs), reads=[r_rs], writes=[r_rs])
            return rs, r_rs

        def h_from(w, X, rX, gcol0, rs, r_rs, gtile=None, r_g=None):
            gtile = gains if gtile is None else gtile
            r_g = r_gains if r_g is None else r_g
            for c in range(NCH):
                P.op("dve", lambda e, c=c: e.scalar_tensor_tensor(
                    out=w.H[:, c, :], in0=X[:, c, :], scalar=gtile[:, gcol0 + c:gcol0 + c + 1], in1=rs,
                    op0=ALU.mult, op1=ALU.mult), reads=[rX, r_rs, r_g], writes=[w.rH])

        def prenorm(w, X, rX, gcol0, gtile=None, r_g=None):
            for c in range(NCH):
                stat_chunk(w, c, X[:, c, :], rX)
            rs, r_rs = rstd_from_stats(w)
            h_from(w, X, rX, gcol0, rs, r_rs, gtile, r_g)

        def postnorm_add(w, X, rX, gcol0, alpha, then_pre=None):
            rs, r_rs = rstd_from_stats(w, alpha)
            for c in range(NCH):
                i = w.tmi % 3
                w.tmi += 1
                t, rt = w.tmp[i], w.r_tmp[i]
                P.op("dve", lambda e, c=c, t=t: e.scalar_tensor_tensor(
                    out=t, in0=w.Y[:, c, :], scalar=gains[:, gcol0 + c:gcol0 + c + 1], in1=rs,
                    op0=ALU.mult, op1=ALU.mult), reads=[w.rY, r_rs, r_gains], writes=[rt])
                P.op("dve", lambda e, c=c, t=t: e.tensor_tensor(
                    out=X[:, c, :], in0=X[:, c, :], in1=t, op=ALU.add), reads=[rt, rX], writes=[rX])
                if then_pre is not None:
                    stat_chunk(w, c, X[:, c, :], rX)
            if then_pre is not None:
                rs2, r_rs2 = rstd_from_stats(w)
                h_from(w, X, rX, then_pre, rs2, r_rs2)

        def ffn(w, l, f):
            for blk in range(11):
                s, rs_ = slot_load(w, W1[l][f][blk], r_W1[l][f])
                sv = s.rearrange("p (c ab n) -> p c ab n", c=8, ab=2)
                for jj in range(2):
                    j = blk * 2 + jj
                    pa, rpa = next_ps()
                    P.mm([lambda e, c=c, pa=pa, jj=jj, sv=sv: e.matmul(
                        pa[:, :], lhsT=sv[:, c, 0, jj * 128:(jj + 1) * 128], rhs=w.H[:, c, :],
                        start=(c == 0), stop=(c == NCH - 1)) for c in range(NCH)],
                        reads=[rs_, w.rH], writes=[rpa])
                    pb, rpb = next_ps()
                    P.mm([lambda e, c=c, pb=pb, jj=jj, sv=sv: e.matmul(
                        pb[:, :], lhsT=sv[:, c, 1, jj * 128:(jj + 1) * 128], rhs=w.H[:, c, :],
                        start=(c == 0), stop=(c == NCH - 1)) for c in range(NCH)],
                        reads=[rs_, w.rH], writes=[rpb])
                    i = w.sli % 3
                    w.sli += 1
                    sl, rsl = w.sil[i], w.r_sil[i]
                    P.op("act", lambda e, sl=sl, pa=pa: e.activation(out=sl, in_=pa[:, :], func=AF.Silu),
                         reads=[rpa], writes=[rsl])
                    P.op("dve", lambda e, sl=sl, pb=pb, j=j: e.tensor_tensor(
                        out=w.G[:, j, :], in0=sl, in1=pb[:, :], op=ALU.mult),
                        reads=[rsl, rpb], writes=[w.rG[j]])
            for m in range(8):
                s, rs_ = slot_load(w, W2[l][f][m][:, :], r_W2[l][f])
                sv = s[:, 0:JH * 128].rearrange("p (j n) -> p j n", j=JH)
                py, rpy = next_ps()
                P.mm([lambda e, j=j, py=py, sv=sv: e.matmul(
                    py[:, :], lhsT=sv[:, j, :], rhs=w.G[:, j, :], start=(j == 0), stop=(j == JH - 1))
                    for j in range(JH)], reads=[rs_] + w.rG, writes=[rpy])
                P.op("act", lambda e, m=m, py=py: e.copy(out=w.Y[:, m, :], in_=py[:, :]),
                     reads=[rpy], writes=[w.rY])
                stat_chunk(w, m, py[:, :], rpy)

        def in_proj(w, l, t):
            ev = 0
            for blk in range(3):
                s, rs_ = slot_load(w, WIN[l][blk], r_WIN[l])
                sv = s.rearrange("p (c n) -> p c n", c=8)
                for cc in range(4):
                    qi = blk * 4 + cc
                    rate = Q_RATES[qi] if qi < 8 else K_RATES[qi - 8]
                    ps, rps = next_ps()
                    P.mm([lambda e, c=c, ps=ps, cc=cc, sv=sv: e.matmul(
                        ps[:, :], lhsT=sv[:, c, cc * 128:(cc + 1) * 128], rhs=w.H[:, c, :],
                        start=(c == 0), stop=(c == NCH - 1)) for c in range(NCH)],
                        reads=[rs_, w.rH], writes=[rps])
                    if rate == 1:
                        dst, src = w.QK[:, qi, :], ps[:, :]
                    else:
                        dst = w.QK[:, qi, :].rearrange("p (j i) -> p i j", j=rate)
                        src = ps[:, :].rearrange("p (i j) -> p i j", j=rate)
                    if ev % 2 == 0:
                        P.op("act", lambda e, dst=dst, src=src: e.copy(out=dst, in_=src),
                             reads=[rps], writes=[w.rQK[qi]])
                    else:
                        P.op("dve", lambda e, dst=dst, src=src: e.tensor_copy(out=dst, in_=src),
                             reads=[rps], writes=[w.rQK[qi]])
                    ev += 1
            def st(dst, src, rr, dres):
                P.dma("poolq", lambda e, d=dst, s=src: e.dma_start(out=d, in_=s), rr[0],
                      reads=rr, writes=[dres], accumulate=True)

            tsl = slice(t * TT, (t + 1) * TT)
            st(QT_w[:, 0:4, tsl], w.QK[:, 0:4, :], w.rQK[0:4], r_QT)
            for qi in (4, 5):
                r = Q_RATES[qi]
                st(QT_w[:, qi, :].rearrange("p (j n) -> p j n", j=r)[:, :, t * (TT // r):(t + 1) * (TT // r)],
                   w.QK[:, qi, :].rearrange("p (j i) -> p j i", j=r), [w.rQK[qi]], r_QT)
            st(QT_w[:, 6:8, tsl], w.QK[:, 6:8, :], w.rQK[6:8], r_QT)
            st(KT_w[:, 0:2, tsl], w.QK[:, 8:10, :], w.rQK[8:10], r_KT)
            for ki in (2, 3):
                r = K_RATES[ki]
                st(KT_w[:, ki, :].rearrange("p (j n) -> p j n", j=r)[:, :, t * (TT // r):(t + 1) * (TT // r)],
                   w.QK[:, 8 + ki, :].rearrange("p (j i) -> p j i", j=r), [w.rQK[8 + ki]], r_KT)
            s, rs_ = slot_load(w, WIN[l][3], r_WIN[l])
            sv = s.rearrange("p (c n) -> p c n", c=8)
            for tb in range(4):
                ps, rps = next_ps()
                P.mm([lambda e, c=c, ps=ps, tb=tb, sv=sv: e.matmul(
                    ps[:, :], lhsT=w.H[:, c, tb * 128:(tb + 1) * 128], rhs=sv[:, c, :],
                    start=(c == 0), stop=(c == NCH - 1)) for c in range(NCH)],
                    reads=[rs_, w.rH], writes=[rps])
                P.op("dve", lambda e, tb=tb, ps=ps: e.tensor_copy(out=w.V[:, tb, :], in_=ps[:, :]),
                     reads=[rps], writes=[w.rV])
            for g in range(4):
                r = K_RATES[g]
                src = w.V[:, :, g * 128:(g + 1) * 128]
                if r == 1:
                    dst = VT_w[g, t * TT:(t + 1) * TT, :].rearrange("(tb p) d -> p tb d", p=128)
                    st(dst, src, [w.rV], r_VT)
                else:
                    npr = TOK // r
                    for j in range(r):
                        dst = VT_w[g, j * npr + t * (TT // r):j * npr + (t + 1) * (TT // r), :] \
                            .rearrange("(tb i) d -> i tb d", tb=4)
                        st(dst, w.V[j:128:r, :, g * 128:(g + 1) * 128], [w.rV], r_VT)

        r_QT, r_KT, r_VT, r_OT = P.res("QT"), P.res("KT"), P.res("VT"), P.res("OT")
        r_XS = [P.res(f"XS{t}") for t in range(NT)]
        r_HS, r_HR = P.res("HS"), P.res("HR")
        r_out = P.res("out")

        def load_x_tokmajor(w, t, X, rX):
            xin = w.Y
            xv = xin.rearrange("p c t -> p (c t)").rearrange("p (tb f) -> p tb f", tb=4)
            P.dma("sp", lambda e: e.dma_start(
                out=xv, in_=x_in[t * TT:(t + 1) * TT, :].rearrange("(tb p) f -> p tb f", p=128)),
                w.rY, writes=[w.rY])
            for c in range(NCH):
                ps, rps = next_ps()
                P.mm([lambda e, tb=tb, ps=ps, c=c: e.transpose(
                    ps[:, tb * 128:(tb + 1) * 128], xv[:, tb, c * 128:(c + 1) * 128], ident[:, :])
                    for tb in range(4)], reads=[w.rY, r_ident], writes=[rps])
                if c % 2 == 0:
                    P.op("act", lambda e, c=c, ps=ps: e.copy(out=X[:, c, :], in_=ps[:, :]), reads=[rps], writes=[rX])
                else:
                    P.op("dve", lambda e, c=c, ps=ps: e.tensor_copy(out=X[:, c, :], in_=ps[:, :]),
                         reads=[rps], writes=[rX])

        def store_out_tokmajor(w, t, X, rX):
            xo = w.Y
            xv = xo.rearrange("p c t -> p (c t)").rearrange("p (tb f) -> p tb f", tb=4)
            for tb in range(4):
                for hf in range(2):
                    ps, rps = next_ps()
                    P.mm([lambda e, cc=cc, ps=ps, tb=tb, hf=hf: e.transpose(
                        ps[:, cc * 128:(cc + 1) * 128], X[:, hf * 4 + cc, tb * 128:(tb + 1) * 128], ident[:, :])
                        for cc in range(4)], reads=[rX, r_ident], writes=[rps])
                    if hf == 0:
                        P.op("act", lambda e, ps=ps, tb=tb: e.copy(out=xv[:, tb, 0:512], in_=ps[:, :]),
                             reads=[rps], writes=[w.rY])
                    else:
                        P.op("dve", lambda e, ps=ps, tb=tb: e.tensor_copy(out=xv[:, tb, 512:1024], in_=ps[:, :]),
                             reads=[rps], writes=[w.rY])
            P.dma("poolq", lambda e: e.dma_start(
                out=out_ap[t * TT:(t + 1) * TT, :].rearrange("(tb p) f -> p tb f", p=128), in_=xv),
                w.rY, reads=[w.rY], writes=[r_out], accumulate=True)

        def s1_body(w, l, t, X, rX, pre_done=False):
            gb = (l * 6) * NCH
            if not pre_done:
                prenorm(w, X, rX, gb + 0 * NCH)
            ffn(w, l, 0)
            postnorm_add(w, X, rX, gb + 1 * NCH, 0.5, then_pre=gb + 2 * NCH)
            P.dma("poolq", lambda e: e.dma_start(out=XS_w[t], in_=X.rearrange("p c t -> p (c t)")),
                  rX, reads=[rX], writes=[r_XS[t]])
            in_proj(w, l, t)

        def mixing(w, l, t, X, rX):
            s_o = None
            for c in range(NCH):
                hf, cc = divmod(c, 4)
                if cc == 0:
                    s_brh = slot_load(w, WBR[l][hf], r_WBR[l])
                    s_g = [slot_load(w, WG[l][br * 2 + hf], r_WG[l]) for br in range(3)]
                brv = s_brh[0].rearrange("p (k n) -> p k n", k=8)
                kr = [(0, 3), (3, 6), (6, 8)]
                mts = []
                for br in range(3):
                    gv = s_g[br][0].rearrange("p (k n) -> p k n", k=8)
                    pg, rpg = next_ps()
                    P.mm([lambda e, k=k, pg=pg, gv=gv, cc=cc: e.matmul(
                        pg[:, :], lhsT=gv[:, k, cc * 128:(cc + 1) * 128], rhs=w.H[:, k, :],
                        start=(k == 0), stop=(k == NCH - 1)) for k in range(NCH)],
                        reads=[s_g[br][1], w.rH], writes=[rpg])
                    pb, rpb = next_ps()
                    k0, k1 = kr[br]
                    P.mm([lambda e, k=k, pb=pb, brv=brv, cc=cc, k0=k0, k1=k1: e.matmul(
                        pb[:, :], lhsT=brv[:, k, cc * 128:(cc + 1) * 128], rhs=w.O[:, k, :],
                        start=(k == k0), stop=(k == k1 - 1)) for k in range(k0, k1)],
                        reads=[s_brh[1], w.rO], writes=[rpb])
                    gt, rgt = w.gt[br], w.r_gt[br]
                    bcol = (l * 3 + br) * NCH + c
                    P.op("act", lambda e, gt=gt, pg=pg, bcol=bcol: e.activation(
                        out=gt, in_=pg[:, :], func=AF.Sigmoid, bias=bgate[:, bcol:bcol + 1], scale=1.0),
                        reads=[rpg, r_bgate], writes=[rgt])
                    i = w.tmi % 3
                    w.tmi += 1
                    mt_, rmt = w.tmp[i], w.r_tmp[i]
                    P.op("dve", lambda e, mt_=mt_, gt=gt, pb=pb: e.tensor_tensor(
                        out=mt_, in0=gt, in1=pb[:, :], op=ALU.mult), reads=[rgt, rpb], writes=[rmt])
                    mts.append((mt_, rmt))
                P.op("dve", lambda e, a=mts[0][0], b=mts[1][0]: e.tensor_tensor(out=a, in0=a, in1=b, op=ALU.add),
                     reads=[mts[1][1]], writes=[mts[0][1]])
                P.op("dve", lambda e, a=mts[0][0], b=mts[2][0], c=c: e.tensor_tensor(
                    out=w.M[:, c, :], in0=a, in1=b, op=ALU.add),
                    reads=[mts[0][1], mts[2][1]], writes=[w.rM])
            for m in range(8):
                hf, mm_ = divmod(m, 4)
                if mm_ == 0:
                    s_o = slot_load(w, WO[l][hf], r_WO[l])
                ov = s_o[0].rearrange("p (k n) -> p k n", k=8)
                py, rpy = next_ps()
                P.mm([lambda e, k=k, py=py, ov=ov, mm_=mm_: e.matmul(
                    py[:, :], lhsT=ov[:, k, mm_ * 128:(mm_ + 1) * 128], rhs=w.M[:, k, :],
                    start=(k == 0), stop=(k == NCH - 1)) for k in range(NCH)],
                    reads=[s_o[1], w.rM], writes=[rpy])
                P.op("act", lambda e, m=m, py=py: e.copy(out=w.Y[:, m, :], in_=py[:, :]),
                     reads=[rpy], writes=[w.rY])
                stat_chunk(w, m, py[:, :], rpy)

        def s3_load_x(w, t):
            X, rX = w.X[t % 2], w.rX[t % 2]
            P.dma("sp", lambda e: e.dma_start(out=X.rearrange("p c t -> p (c t)"), in_=XS_r[t]),
                  rX, reads=[r_XS[t]], writes=[rX])

        def s3_body(w, l, t, X, rX):
            gb = (l * 6) * NCH
            if t == 0:
                s3_load_x(w, 0)
            if t + 1 < NT:
                s3_load_x(w, t + 1)
            P.dma("sp", lambda e: e.dma_start(out=w.O, in_=OT[:, :, t * TT:(t + 1) * TT]),
                  w.rO, reads=[r_OT], writes=[w.rO])
            prenorm(w, X, rX, gb + 2 * NCH)
            mixing(w, l, t, X, rX)
            postnorm_add(w, X, rX, gb + 3 * NCH, 1.0, then_pre=gb + 4 * NCH)
            ffn(w, l, 1)
            nxt = ((l + 1) * 6) * NCH if (l + 1 < NL and (l + 1) in layers_s1) else None
            postnorm_add(w, X, rX, gb + 5 * NCH, 0.5, then_pre=nxt)

        def mem_kv(w):
            X, rX = w.X[0], w.rX[0]
            mv = w.Y.rearrange("p c t -> p (c t)")[:, 0:2 * D].rearrange("p (tb f) -> p tb f", tb=2)
            P.dma("sp", lambda e: e.dma_start(out=mv, in_=mem_in[:, :].rearrange("(tb p) f -> p tb f", p=128)),
                  w.rY, writes=[w.rY])
            P.op("pool", lambda e: e.memset(X.rearrange("p c t -> p (c t)"), 1.0), writes=[rX])
            for c in range(NCH):
                ps, rps = next_ps()
                P.mm([lambda e, tb=tb, ps=ps, c=c: e.transpose(
                    ps[:, tb * 128:(tb + 1) * 128], mv[:, tb, c * 128:(c + 1) * 128], ident[:, :])
                    for tb in range(2)], reads=[w.rY, r_ident], writes=[rps])
                P.op("dve", lambda e, c=c, ps=ps: e.tensor_copy(out=X[:, c, 0:256], in_=ps[:, 0:256]),
                     reads=[rps], writes=[rX])
            for l in sorted(set(layers_s2)):
                prenorm(w, X, rX, l * NCH, gtile=mgain, r_g=r_mgain)
                s, rs_ = slot_load(w, WM[l][:, :], r_WM[l])
                sv = s.rearrange("p (c n) -> p c n", c=8)
                for mc in range(2):
                    ps, rps = next_ps()
                    P.mm([lambda e, c=c, ps=ps, mc=mc, sv=sv: e.matmul(
                        ps[:, 0:256], lhsT=sv[:, c, mc * 128:(mc + 1) * 128], rhs=w.H[:, c, 0:256],
                        start=(c == 0), stop=(c == NCH - 1)) for c in range(NCH)],
                        reads=[rs_, w.rH], writes=[rps])
                    P.op("dve", lambda e, ps=ps, mc=mc, l=l: e.tensor_copy(out=KC[0:64, l, mc, 0, :], in_=ps[0:64, 0:256]),
                         reads=[rps], writes=[r_const])
                    P.op("dve", lambda e, ps=ps, mc=mc, l=l: e.tensor_copy(out=KC[64:128, l, mc, 1, :], in_=ps[64:128, 0:256]),
                         reads=[rps], writes=[r_const])
                for kt in range(2):
                    ps, rps = next_ps()
                    P.mm([lambda e, c=c, ps=ps, kt=kt, sv=sv: e.matmul(
                        ps[:, 0:256], lhsT=w.H[:, c, kt * 128:(kt + 1) * 128], rhs=sv[:, c, 256:512],
                        start=(c == 0), stop=(c == NCH - 1)) for c in range(NCH)],
                        reads=[rs_, w.rH], writes=[rps])
                    P.op("dve", lambda e, ps=ps, kt=kt, l=l: e.tensor_copy(out=VC[:, l, kt, :], in_=ps[:, 0:256]),
                         reads=[rps], writes=[r_const])

        def s2_stage(l):
            arena.reset()
            NQT = TOK // 128
            Qsb = [arena.alloc([TOK], BF16) for _ in range(2)]
            rQ = [P.res("Qsb0"), P.res("Qsb1")]
            KMAX = 16 * 128 + TOK
            KA = [arena.alloc([KMAX], BF16) for _ in range(2)]
            KB = [arena.alloc([KMAX], BF16) for _ in range(2)]
            rK = [P.res("Ksb0"), P.res("Ksb1")]
            for i in range(2):
                P.op("pool", lambda e, i=i: e.memset(KA[i][64:128, :], 0.0), writes=[rK[i]])
                P.op("pool", lambda e, i=i: e.memset(KB[i][0:64, :], 0.0), writes=[rK[i]])
            VBLK = 16 + TOK // 128
            Vsb = [arena.alloc([VBLK * 128], BF16) for _ in range(2)]
            rV = [P.res("Vsb0"), P.res("Vsb1")]
            Oacc = [arena.alloc([TOK], BF16) for _ in range(3)]
            rOa = [P.res(f"Oacc{g}") for g in range(3)]
            Dacc = [arena.alloc([TOK], F32)]
            rDa = [P.res("Dacc0")]
            Ost = [arena.alloc([TOK], BF16) for _ in range(2)]
            rOst = [P.res("Ost0"), P.res("Ost1")]
            Pex = [arena.alloc([512], BF16) for _ in range(3)]
            rPex = [P.res(f"Pex{i}") for i in range(3)]
            PT = [arena.alloc([512], BF16) for _ in range(3)]
            rPT = [P.res(f"PT{i}") for i in range(3)]
            Dr = [arena.alloc([512], F32) for _ in range(2)]
            rDr = [P.res("Dr0"), P.res("Dr1")]
            cnt = {"q": 0, "kv": 0, "p": 0, "d": 0, "o": 0}

            def load_q(qchunk):
                i = cnt["q"] % 2
                cnt["q"] += 1
                P.dma("sp", lambda e, i=i: e.dma_start(out=Qsb[i], in_=QT_r[:, qchunk, :]), rQ[i],
                      reads=[r_QT], writes=[rQ[i]])
                return Qsb[i], rQ[i]

            def load_kv(kchunk, vgroup, r):
                i = cnt["kv"] % 2
                cnt["kv"] += 1
                npos = TOK // r
                nb = npos // 128
                kvA = KA[i][:, 0:r * (128 + npos)].rearrange("p (j n) -> p j n", j=r)
                kvB = KB[i][:, 0:r * (128 + npos)].rearrange("p (j n) -> p j n", j=r)
                kv = (kvA, kvB)
                vv = Vsb[i][:, 0:r * (nb + 1) * 128].rearrange("p (j b d) -> p j b d", j=r, b=nb + 1)
                koff = sum(K_RATES[:kchunk]) * 128
                hk = HR[0:HROWS // 2, :].rearrange("a b -> (a b)").rearrange("(p n) -> p n", p=128)
                hv = HR[HROWS // 2:HROWS, :].rearrange("a b -> (a b)").rearrange("(n d) -> n d", d=128)
                first_dma = True
                for hi, kz in enumerate(kv):
                    ps_ = slice(hi * 64, (hi + 1) * 64)
                    P.dma("sp", lambda e, kz=kz, ps_=ps_: e.dma_start(
                        out=kz[ps_, :, 0:128], in_=hk[ps_, koff:koff + r * 128].rearrange("p (j n) -> p j n", j=r)),
                        rK[i], reads=[r_HR], writes=[rK[i]], accumulate=not first_dma)
                    first_dma = False
                    P.dma("sp", lambda e, kz=kz, ps_=ps_: e.dma_start(
                        out=kz[ps_, :, 128:], in_=KT_r[ps_, kchunk, :].rearrange("p (j n) -> p j n", j=r)),
                        rK[i], reads=[r_KT], writes=[rK[i]], accumulate=True)
                P.dma("sp", lambda e: e.dma_start(
                    out=vv[:, :, 0, :], in_=hv[koff:koff + r * 128, :].rearrange("(j p) d -> p j d", p=128)),
                    rV[i], reads=[r_HR], writes=[rV[i]])
                for j in range(r):
                    P.dma("sp", lambda e, j=j: e.dma_start(
                        out=vv[:, j, 1:, :],
                        in_=VT_r[vgroup, j * npos:(j + 1) * npos, :].rearrange("(b p) d -> p b d", p=128)),
                        rV[i], reads=[r_VT], writes=[rV[i]], accumulate=True)
                return kv, rK[i], vv, rV[i]

            def run_chunk(nqt, Q, rQ_, key_fn, val_fn, E_fn, rKV, sink_for_group):
                def s1(qt):
                    qcols = slice(qt * 128, (qt + 1) * 128)
                    pss, rpss = next_ps()
                    P.mm([lambda e, hi=hi, pc=pc, pss=pss, qt=qt, qcols=qcols: e.matmul(
                        pss[:, (hi * 2 + pc) * 128:(hi * 2 + pc + 1) * 128],
                        lhsT=key_fn(qt, hi, pc), rhs=Q[:, qcols], start=True, stop=True)
                        for hi in range(2) for pc in range(2)], reads=[rQ_] + rKV, writes=[rpss])
                    i = cnt["p"] % 3
                    cnt["p"] += 1
                    E = E_fn(qt)
                    if E is None:
                        P.op("act", lambda e, i=i, pss=pss: e.activation(
                            out=PT[i], in_=pss[:, :], func=AF.Exp, scale=0.125), reads=[rpss], writes=[rPT[i]])
                    else:
                        P.op("act", lambda e, i=i, pss=pss: e.activation(
                            out=Pex[i], in_=pss[:, :], func=AF.Exp, scale=0.125), reads=[rpss], writes=[rPex[i]])
                        P.op("pool" if qt % 3 == 2 else "dve",
                             lambda e, i=i, E=E: e.tensor_tensor(out=PT[i], in0=Pex[i], in1=E, op=ALU.mult),
                             reads=[rPex[i], r_E], writes=[rPT[i]])
                    return i

                def s2(qt, i, qi, pso, rpso, psd, rpsd):
                    fns = []
                    for hi in range(2):
                        for pc in range(2):
                            fns.append(lambda e, hi=hi, pc=pc, i=i, qt=qt, qi=qi: e.matmul(
                                pso[hi * 64:(hi + 1) * 64, qi * 128:(qi + 1) * 128],
                                lhsT=val_fn(qt, hi, pc), rhs=PT[i][:, (hi * 2 + pc) * 128:(hi * 2 + pc + 1) * 128],
                                start=(pc == 0), stop=(pc == 1)))
                    P.mm(fns, reads=[rPT[i]] + rKV, writes=[rpso])
                    fns = []
                    for hi in range(2):
                        for pc in range(2):
                            fns.append(lambda e, hi=hi, pc=pc, i=i, qi=qi: e.matmul(
                                psd[hi * 64:(hi + 1) * 64, qi * 128:(qi + 1) * 128],
                                lhsT=ones[:, hi * 64:(hi + 1) * 64], rhs=PT[i][:, (hi * 2 + pc) * 128:(hi * 2 + pc + 1) * 128],
                                start=(pc == 0), stop=(pc == 1)))
                    P.mm(fns, reads=[rPT[i], r_ones], writes=[rpsd])

                pend = s1(0)
                banks = None
                for qt in range(nqt):
                    nxt = s1(qt + 1) if qt + 1 < nqt else None
                    if qt % 4 == 0:
                        banks = next_ps() + next_ps()
                    s2(qt, pend, qt % 4, *banks)
                    if qt % 4 == 3:
                        sink_for_group(qt - 3)(*banks)
                    pend = nxt

            def finish_direct(sink_col, Ot, rOt, c0):
                def f(pso, rpso, psd, rpsd):
                    i = cnt["d"] % 2
                    cnt["d"] += 1
                    if sink_col is not None:
                        P.op("dve", lambda e, i=i: e.tensor_scalar(
                            out=Dr[i], in0=psd[:, :], scalar1=sinke[:, sink_col:sink_col + 1], scalar2=None,
                            op0=ALU.add), reads=[rpsd, r_sink], writes=[rDr[i]])
                        P.op("dve", lambda e, i=i: e.reciprocal(out=Dr[i], in_=Dr[i]), reads=[rDr[i]], writes=[rDr[i]])
                    else:
                        P.op("dve", lambda e, i=i: e.reciprocal(out=Dr[i], in_=psd[:, :]), reads=[rpsd], writes=[rDr[i]])
                    P.op("dve", lambda e, i=i, c0=c0: e.tensor_tensor(
                        out=Ot[:, c0:c0 + 512], in0=pso[:, :], in1=Dr[i], op=ALU.mult),
                        reads=[rpso, rDr[i]], writes=[rOt])
                return f

            kv_cache = {}
            for sc, (qch, kch, vg, r, md, cols, has_sink) in enumerate(SELF_CHUNKS):
                if ("s2_sw" in DBG_SKIP and has_sink) or ("s2_dil" in DBG_SKIP and not has_sink):
                    continue
                Q, rQ_ = load_q(qch)
                if (kch, vg) not in kv_cache:
                    kv_cache.clear()
                    kv_cache[(kch, vg)] = load_kv(kch, vg, r)
                kv, rKc, vv, rVc = kv_cache[(kch, vg)]
                npos = TOK // r
                nb = npos // 128

                def key_fn(qt, hi, pc, kv=kv, nb=nb):
                    j, b = divmod(qt, nb)
                    return kv[hi][:, j, b * 128 + pc * 128:b * 128 + pc * 128 + 128]

                def val_fn(qt, hi, pc, vv=vv, nb=nb):
                    j, b = divmod(qt, nb)
                    return vv[:, j, b + pc, hi * 64:(hi + 1) * 64]

                def E_fn(qt, sc=sc, nb=nb):
                    return (Efirst if qt % nb == 0 else Emat)[:, sc, :]

                if has_sink:
                    oi = cnt["o"] % 2
                    cnt["o"] += 1
                    run_chunk(NQT, Q, rQ_, key_fn, val_fn, E_fn, [rKc, rVc],
                              lambda qt0, sc=sc, oi=oi: finish_direct(l * 3 + sc, Ost[oi], rOst[oi], qt0 * 128))
                    P.dma("poolq", lambda e, oi=oi, qch=qch: e.dma_start(out=OT[:, qch, :], in_=Ost[oi]), rOst[oi],
                          reads=[rOst[oi]], writes=[r_OT], accumulate=True)
                else:
                    g = sc - 3
                    ov3 = Oacc[g].rearrange("p (n r) -> p r n", r=r)
                    dv3 = Dacc[0].rearrange("p (n r) -> p r n", r=r)

                    def sink_acc(pso, rpso, psd, rpsd, qt0=None):
                        pass

                    def dil_sink(qt0, nb=nb, ov3=ov3, dv3=dv3, g=g):
                        if nb >= 4:
                            j, b0 = divmod(qt0, nb)
                            od = ov3[:, j, b0 * 128:b0 * 128 + 512]
                            dd = dv3[:, j, b0 * 128:b0 * 128 + 512]
                            shp = None
                        else:
                            a = 4 // nb
                            j0 = qt0 // nb
                            od = ov3[:, j0:j0 + a, :]
                            dd = dv3[:, j0:j0 + a, :]
                            shp = a

                        def f(pso, rpso, psd, rpsd, od=od, dd=dd, shp=shp, g=g):
                            so = pso[:, :] if shp is None else pso[:, :].rearrange("p (a n) -> p a n", a=shp)
                            sd = psd[:, :] if shp is None else psd[:, :].rearrange("p (a n) -> p a n", a=shp)
                            P.op("act", lambda e: e.copy(out=od, in_=so), reads=[rpso], writes=[rOa[g]])
                            if g == 0:
                                P.op("dve", lambda e: e.tensor_copy(out=dd, in_=sd), reads=[rpsd], writes=[rDa[0]])
                            else:
                                P.op("dve", lambda e: e.tensor_tensor(out=dd, in0=dd, in1=sd, op=ALU.add),
                                     reads=[rpsd], writes=[rDa[0]])
                        return f

                    run_chunk(NQT, Q, rQ_, key_fn, val_fn, E_fn, [rKc, rVc], dil_sink)
            if "s2_dil" in DBG_SKIP:
                P.op("pool", lambda e: e.memset(Dacc[0], 1.0), writes=[rDa[0]])
                for g in range(3):
                    P.op("pool", lambda e, g=g: e.memset(Oacc[g], 1.0), writes=[rOa[g]])
            if "s2_comb" not in DBG_SKIP:
                P.op("dve", lambda e: e.reciprocal(out=Dacc[0], in_=Dacc[0]), reads=[rDa[0]], writes=[rDa[0]])
            for g in range(3 if "s2_comb" not in DBG_SKIP else 0):
                oi = cnt["o"] % 2
                cnt["o"] += 1
                P.op("dve", lambda e, g=g, oi=oi: e.tensor_tensor(out=Ost[oi], in0=Oacc[g], in1=Dacc[0], op=ALU.mult),
                     reads=[rOa[g], rDa[0]], writes=[rOst[oi]])
                P.dma("poolq", lambda e, oi=oi, g=g: e.dma_start(out=OT[:, 3 + g, :], in_=Ost[oi]), rOst[oi],
                      reads=[rOst[oi]], writes=[r_OT], accumulate=True)
            for mc in range(2 if "s2_mem" not in DBG_SKIP else 0):
                Q, rQ_ = load_q(6 + mc)

                def key_fn(qt, hi, pc, mc=mc):
                    return KC[:, l, mc, hi, pc * 128:(pc + 1) * 128]

                def val_fn(qt, hi, pc, mc=mc):
                    return VC[:, l, pc, mc * 128 + hi * 64:mc * 128 + (hi + 1) * 64]

                oi = cnt["o"] % 2
                cnt["o"] += 1
                run_chunk(NQT, Q, rQ_, key_fn, val_fn, lambda qt: None, [r_const],
                          lambda qt0, oi=oi: finish_direct(None, Ost[oi], rOst[oi], qt0 * 128))
                P.dma("poolq", lambda e, oi=oi, mc=mc: e.dma_start(out=OT[:, 6 + mc, :], in_=Ost[oi]), rOst[oi],
                      reads=[rOst[oi]], writes=[r_OT], accumulate=True)

        def halo_send(dst_buf):
            hk = dst_buf[0:HROWS // 2, :].rearrange("a b -> (a b)").rearrange("(p n) -> p n", p=128)
            hv = dst_buf[HROWS // 2:HROWS, :].rearrange("a b -> (a b)").rearrange("(n d) -> n d", d=128)
            for kch in range(4):
                r = K_RATES[kch]
                npos = TOK // r
                koff = sum(K_RATES[:kch]) * 128
                P.dma("sp", lambda e, kch=kch, r=r, npos=npos, koff=koff: e.dma_start(
                    out=hk[:, koff:koff + r * 128].rearrange("p (j n) -> p j n", j=r),
                    in_=KT_w[:, kch, :].rearrange("p (j n) -> p j n", j=r)[:, :, npos - 128:npos]),
                    r_HS, reads=[r_KT], writes=[r_HS], accumulate=True)
                P.dma("sp", lambda e, kch=kch, r=r, npos=npos, koff=koff: e.dma_start(
                    out=hv[koff:koff + r * 128, :].rearrange("(j p) d -> j p d", p=128),
                    in_=VT_w[kch, :, :].rearrange("(j n) d -> j n d", j=r)[:, npos - 128:npos, :]),
                    r_HS, reads=[r_VT], writes=[r_HS], accumulate=True)

        if not first:
            pass
        wts = alloc_token_stage()
        if layers_s2:
            mem_kv(wts)
        if first:
            per_tile = -(-len(cast_tasks) // max(NT - 1, 1))
            for t in range(NT):
                X, rX = wts.X[t % 2], wts.rX[t % 2]
                load_x_tokmajor(wts, t, X, rX)
                s1_body(wts, 0, t, X, rX)
                emit_casts(per_tile)
            emit_casts()
        for l in range(NL):
            if l not in layers_s2:
                continue
            if fused:
                halo_send(HS)
                r_cc = P.res("cc")
                P.dma("poolq", lambda e: e.collective_compute(
                    "AllGather", ALU.bypass, replica_groups=[[2 * i, 2 * i + 1] for i in range(n_cores // 2)],
                    ins=[HS[:, :].opt()], outs=[HR[:, :].opt()]), r_cc, reads=[r_HS], writes=[r_cc, r_HR], inc=1)
                P.op("pool", lambda e: e.memset(ccdummy[:, :], 0.0), reads=[r_cc], writes=[r_HR])
            P.barrier()
            if "s2" not in DBG_SKIP:
                s2_stage(l)
            P.barrier()
            wts = alloc_token_stage()
            for t in range(NT if "s3" not in DBG_SKIP else 0):
                X, rX = wts.X[t % 2], wts.rX[t % 2]
                s3_body(wts, l, t, X, rX)
                if l + 1 < NL and (l + 1) in layers_s1:
                    s1_body(wts, l + 1, t, X, rX, pre_done=True)
                elif l == NL - 1:
                    store_out_tokmajor(wts, t, X, rX)
        P.barrier(streams=("sp",))
        with nc.Block() as block:
            P.emit(block)
    return nc


_CACHE = {}


def _get_nc(TOK, mode, n_cores=N_CORES):
    key = (TOK, mode, n_cores)
    if key not in _CACHE:
        _CACHE[key] = build(TOK, mode, n_cores)
    return _CACHE[key]


def kernel(**inputs):
    x = np.asarray(inputs["x"], np.float32)
    mem = np.asarray(inputs["mem"], np.float32)
    B, S, _ = x.shape
    TOK = B * S // N_CORES
    halves = S // TOK
    sh = prep_shared(inputs)
    in_maps = []
    for c in range(N_CORES):
        b, hf = divmod(c, halves)
        m = dict(sh)
        m["x"] = np.ascontiguousarray(x[b, hf * TOK:(hf + 1) * TOK, :])
        m["mem"] = np.ascontiguousarray(mem[b])
        m["halo_valid"] = np.full((128, 1), 1.0 if hf > 0 else 0.0, np.float32)
        in_maps.append(m)
    nc = _get_nc(TOK, "fused")
    res = run_bass_kernel_spmd(nc, in_maps, core_ids=list(range(N_CORES)))
    out = np.empty((B, S, D), np.float32)
    for c in range(N_CORES):
        b, hf = divmod(c, halves)
        out[b, hf * TOK:(hf + 1) * TOK, :] = res.results[c]["out"]
    return out


def _halo_from(KT, VT, TOK):
    HROWS = 2 * 2816 * 128 // 1024
    hk = np.zeros((128, 2816), KT.dtype)
    hv = np.zeros((2816, 128), VT.dtype)
    for kch in range(4):
        r = K_RATES[kch]
        npos = TOK // r
        koff = sum(K_RATES[:kch]) * 128
        kk = KT[:, kch, :].reshape(128, r, npos)[:, :, npos - 128:]
        hk[:, koff:koff + r * 128] = kk.reshape(128, r * 128)
        vv = VT[kch].reshape(r, npos, 128)[:, npos - 128:, :]
        hv[koff:koff + r * 128, :] = vv.reshape(r * 128, 128)
    HR = np.zeros((2 * HROWS, 1024), KT.dtype)
    HR[0:HROWS // 2] = hk.reshape(HROWS // 2, 1024)
    HR[HROWS // 2:HROWS] = hv.reshape(HROWS // 2, 1024)
    return HR


def kernel_unfused(n_cores=N_CORES, debug=None, **inputs):
    x = np.asarray(inputs["x"], np.float32)
    mem = np.asarray(inputs["mem"], np.float32)
    B, S, _ = x.shape
    TOK = B * S // n_cores
    halves = S // TOK
    sh = prep_shared(inputs)
    base = []
    for c in range(n_cores):
        b, hf = divmod(c, halves)
        m = dict(sh)
        m["mem"] = np.ascontiguousarray(mem[b])
        m["halo_valid"] = np.full((128, 1), 1.0 if hf > 0 else 0.0, np.float32)
        base.append(m)
    cores = list(range(n_cores))

    def halos(res):
        hr = []
        for c in range(n_cores):
            b, hf = divmod(c, halves)
            src = c - 1 if hf > 0 else c
            hr.append(_halo_from(np.asarray(res[src]["KT_o"]), np.asarray(res[src]["VT_o"]), TOK))
        return hr

    ins = []
    for c in range(n_cores):
        b, hf = divmod(c, halves)
        m = dict(base[c])
        m["x"] = np.ascontiguousarray(x[b, hf * TOK:(hf + 1) * TOK, :])
        ins.append(m)
    ra = run_bass_kernel_spmd(_get_nc(TOK, "A", n_cores), ins, core_ids=cores).results
    if debug is not None:
        debug["A"] = ra
    hr = halos(ra)
    ins = []
    for c in range(n_cores):
        m = dict(base[c])
        m.update({"XS_i": ra[c]["XS_o"], "QT_i": ra[c]["QT_o"], "KT_i": ra[c]["KT_o"], "VT_i": ra[c]["VT_o"],
                  "HR": hr[c]})
        ins.append(m)
    rb = run_bass_kernel_spmd(_get_nc(TOK, "B", n_cores), ins, core_ids=cores).results
    if debug is not None:
        debug["B"] = rb
    hr = halos(rb)
    ins = []
    for c in range(n_cores):
        m = dict(base[c])
        m.update({"XS_i": rb[c]["XS_o"], "QT_i": rb[c]["QT_o"], "KT_i": rb[c]["KT_o"], "VT_i": rb[c]["VT_o"],
                  "HR": hr[c]})
        ins.append(m)
    rc = run_bass_kernel_spmd(_get_nc(TOK, "C", n_cores), ins, core_ids=cores).results
    if debug is not None:
        debug["C"] = rc
    out = np.empty((B, S, D), np.float32)
    for c in range(n_cores):
        b, hf = divmod(c, halves)
        out[b, hf * TOK:(hf + 1) * TOK, :] = rc[c]["out"]
    return out
```

```python
import math
import numpy as np
import ml_dtypes
import concourse.bass as bass
import concourse.mybir as mybir
from concourse.bass_utils import run_bass_kernel_spmd
from contextlib import ExitStack

F32 = mybir.dt.float32
BF16 = mybir.dt.bfloat16
AF = mybir.ActivationFunctionType
ALU = mybir.AluOpType

D = 1024
NCH = 8
TT = 512
FF = 2816
JH = 22
NL = 2
EPS = 1e-6
N_CORES = 8
DBG_SKIP = set()

COMPUTE = ("pe", "act", "dve", "pool")
STREAM_OF = {"pe": "pe", "act": "act", "dve": "dve", "pool": "pool",
             "sp": "sp", "actq": "act", "poolq": "pool"}


class Res:
    __slots__ = ("name", "w", "r", "dsem", "dcnt")

    def __init__(self, name):
        self.name = name
        self.w = []
        self.r = []
        self.dsem = None
        self.dcnt = 0


class Prog:
    def __init__(self, nc, es, n_dma_sems=92):
        self.nc = nc
        self.ops = {s: [] for s in ("pe", "act", "dve", "pool", "sp")}
        self.sem = {e: es.enter_context(nc.semaphore("sem_" + e)) for e in COMPUTE}
        self.free_dma = [es.enter_context(nc.semaphore(f"dsem{i}")) for i in range(n_dma_sems)]
        self.cnt = {e: 0 for e in COMPUTE}
        self.seen = {s: {} for s in self.ops}
        self.dma_res = []

    def res(self, name):
        return Res(name)

    def _dsem(self, r):
        if r.dsem is None:
            r.dsem = self.free_dma.pop()
            self.dma_res.append(r)
        return r.dsem

    def _waits(self, stream, evs, own=None):
        need = {}
        for ev in evs:
            k, v = ev
            if need.get(id(k), (None, -1))[1] < v:
                need[id(k)] = (k, v)
        seen = self.seen[stream]
        for kid, (k, v) in need.items():
            if own is not None and k is own and stream == "pe":
                continue
            if seen.get(kid, -1) >= v:
                continue
            seen[kid] = v
            self.ops[stream].append(("wait", k, v))

    def _deps(self, reads, writes, accumulate):
        evs = []
        for r in reads:
            evs.extend(r.w)
        for w in writes:
            if not accumulate:
                evs.extend(w.w)
            evs.extend(w.r)
        return evs

    def _record(self, ev, reads, writes, accumulate):
        for r in reads:
            r.r.append(ev)
        for w in writes:
            if accumulate:
                w.w.append(ev)
            else:
                w.w = [ev]
            w.r = []

    def op(self, eng, fn, reads=(), writes=()):
        stream = STREAM_OF[eng]
        sem = self.sem[eng]
        self._waits(stream, self._deps(reads, writes, False), own=sem)
        self.cnt[eng] += 1
        ev = (sem, self.cnt[eng])
        self.ops[stream].append(("inst", fn, sem, 1))
        self._record(ev, reads, writes, False)

    def mm(self, fns, reads=(), writes=()):
        sem = self.sem["pe"]
        self._waits("pe", self._deps(reads, writes, False), own=sem)
        for fn in fns[:-1]:
            self.ops["pe"].append(("inst", fn, None, 0))
        self.cnt["pe"] += 1
        ev = (sem, self.cnt["pe"])
        self.ops["pe"].append(("inst", fns[-1], sem, 1))
        self._record(ev, reads, writes, False)

    def dma(self, queue, fn, sb, reads=(), writes=(), accumulate=False, inc=16):
        stream = STREAM_OF[queue]
        self._waits(stream, self._deps(reads, writes, accumulate))
        k = self._dsem(sb)
        sb.dcnt += inc
        ev = (k, sb.dcnt)
        self.ops[stream].append(("inst", fn, k, inc))
        self._record(ev, reads, writes, accumulate)
        return ev

    def all_events(self, stream):
        evs = [(self.sem[e], self.cnt[e]) for e in COMPUTE if self.cnt[e] > 0]
        evs += [(r.dsem, r.dcnt) for r in self.dma_res
                if r.dcnt > 0 and (stream == "pool" or not r.name.startswith("cc"))]
        return evs

    def barrier(self, streams=("pe", "act", "dve", "pool", "sp")):
        for s in streams:
            self._waits(s, self.all_events(s))

    def emit(self, block):
        ops = self.ops

        def run(eng_obj, lst):
            for o in lst:
                if o[0] == "wait":
                    eng_obj.wait_ge(o[1], o[2])
                else:
                    ins = o[1](eng_obj)
                    if o[2] is not None:
                        ins.then_inc(o[2], o[3])

        @block.tensor
        def _(e):
            run(e, ops["pe"])

        @block.scalar
        def _(e):
            run(e, ops["act"])

        @block.vector
        def _(e):
            run(e, ops["dve"])

        @block.gpsimd
        def _(e):
            run(e, ops["pool"])

        @block.sync
        def _(e):
            run(e, ops["sp"])


class Arena:
    def __init__(self, ap_f32):
        self.ap = ap_f32
        self.n = ap_f32.shape[1]
        self.off = 0

    def reset(self):
        self.off = 0

    def alloc(self, free_shape, dtype):
        n = int(np.prod(free_shape))
        if dtype == BF16:
            assert n % 2 == 0
            n32 = n // 2
        else:
            n32 = n
        n32a = (n32 + 7) // 8 * 8
        assert self.off + n32a <= self.n, f"arena overflow {self.off}+{n32a}>{self.n}"
        v = self.ap[:, self.off:self.off + n32]
        self.off += n32a
        if dtype == BF16:
            v = v.bitcast(BF16)
        if len(free_shape) == 2:
            v = v.rearrange("p (a b) -> p a b", a=free_shape[0])
        elif len(free_shape) == 3:
            v = v.rearrange("p (a b c) -> p a b c", a=free_shape[0], b=free_shape[1])
        return v


QA_HEAD_ORDER = [0, 3, 1, 4, 2, 5]
SELF_CHUNKS = [
    (0, 0, 0, 1, 127, (0, 3), True),
    (1, 0, 0, 1, 127, (1, 4), True),
    (2, 0, 0, 1, 127, (2, 5), True),
    (3, 1, 1, 1, 128, (6, 7), False),
    (4, 2, 2, 4, 128, (8, 9), False),
    (5, 3, 3, 16, 128, (10, 11), False),
]
K_RATES = [1, 1, 4, 16]
Q_RATES = [1, 1, 1, 1, 4, 16, 1, 1]


def _t5_bucket(dist):
    dist = np.asarray(dist)
    me = 16
    dd = np.maximum(dist, 1).astype(np.float32)
    large = me + (np.log(dd / np.float32(me)) / np.float32(math.log(2048 / me))
                  * np.float32(32 - me)).astype(np.int32)
    large = np.minimum(large, 31)
    return np.where(dist < me, dist, large)


def _win_perm():
    qa = np.concatenate([np.arange(64 * h, 64 * h + 64) for h in QA_HEAD_ORDER])
    ka = np.arange(384, 512)
    va = np.arange(512, 640)
    qb = np.arange(640, 1024)
    kb = np.arange(1024, 1408)
    vb = np.arange(1408, 1792)
    qc = np.arange(1792, 2048)
    return np.concatenate([qa, qb, qc, ka, kb, va, vb])


def _static_bias_tables(rel_bias):
    kk = np.arange(128)[:, None]
    qq = np.arange(128)[None, :]
    bias = np.zeros((6, 128, 2, 2, 128), np.float32)
    mask = np.zeros((6, 128, 2, 2, 128), np.float32)
    for sc, (_, _, _, r, md, cols, _) in enumerate(SELF_CHUNKS):
        d_prev = qq + 128 - kk
        d_cur = qq - kk
        v_prev = (d_prev <= md)
        v_cur = (d_cur >= 0)
        for hi, col in enumerate(cols):
            tab = rel_bias[:, col]
            bias[sc, :, hi, 0, :] = np.where(v_prev, tab[_t5_bucket(d_prev * r)], 0.0)
            bias[sc, :, hi, 1, :] = np.where(v_cur, tab[_t5_bucket(np.maximum(d_cur, 0) * r)], 0.0)
            mask[sc, :, hi, 0, :] = v_prev
            mask[sc, :, hi, 1, :] = v_cur
    return bias.reshape(6, 128, 512), mask.reshape(6, 128, 512)


def _chunk_cols(v):
    v = np.asarray(v, np.float32)
    lead = v.shape[:-1]
    return np.ascontiguousarray(v.reshape(-1, NCH, 128).transpose(2, 0, 1).reshape(128, -1)), lead


def prep_shared(inp):
    sh = {}
    perm = _win_perm()
    sh["w_ffn1_in"] = np.ascontiguousarray(inp["w_ffn1_in"], np.float32)
    sh["w_ffn1_out"] = np.ascontiguousarray(inp["w_ffn1_out"], np.float32)
    sh["w_ffn2_in"] = np.ascontiguousarray(inp["w_ffn2_in"], np.float32)
    sh["w_ffn2_out"] = np.ascontiguousarray(inp["w_ffn2_out"], np.float32)
    sh["w_in"] = np.ascontiguousarray(np.asarray(inp["w_in"], np.float32)[:, :, perm])
    sh["w_gate"] = np.ascontiguousarray(inp["w_gate"], np.float32)
    rows_a = np.concatenate([np.arange(64 * h, 64 * h + 64) for h in QA_HEAD_ORDER])
    wbr = np.concatenate([np.asarray(inp["w_br_a"], np.float32)[:, rows_a, :],
                          np.asarray(inp["w_br_b"], np.float32),
                          np.asarray(inp["w_br_c"], np.float32)], axis=1)
    sh["w_br"] = np.ascontiguousarray(wbr)
    sh["w_o"] = np.ascontiguousarray(inp["w_o"], np.float32)
    sh["w_mem_kv"] = np.ascontiguousarray(inp["w_mem_kv"], np.float32)
    g, _ = _chunk_cols(inp["norm_gain"])
    sh["gains"] = g
    mg, _ = _chunk_cols(inp["mem_norm_gain"])
    sh["mgain"] = mg
    bg, _ = _chunk_cols(inp["b_gate"])
    sh["bgate"] = bg
    sinks = np.asarray(inp["sinks"], np.float32)
    sk = np.zeros((128, NL * 3), np.float32)
    for l in range(NL):
        for ci in range(3):
            sk[0:64, l * 3 + ci] = sinks[l, QA_HEAD_ORDER[2 * ci]]
            sk[64:128, l * 3 + ci] = sinks[l, QA_HEAD_ORDER[2 * ci + 1]]
    sh["sinks"] = sk
    bias, mask = _static_bias_tables(np.asarray(inp["rel_bias"], np.float32))
    sh["biasT"] = np.ascontiguousarray(bias.transpose(1, 0, 2))
    sh["maskT"] = np.ascontiguousarray(mask.transpose(1, 0, 2))
    sh["ident"] = np.eye(128, dtype=np.float32)
    return sh


def build(TOK, mode="fused", n_cores=N_CORES):
    NT = TOK // TT
    fused = mode == "fused"
    nc = bass.Bass("TRN2", target_bir_lowering=False)

    def din(name, shape, dt=F32):
        return nc.dram_tensor(name, list(shape), dt, kind="ExternalInput").ap()

    def dout(name, shape, dt=F32):
        return nc.dram_tensor(name, list(shape), dt, kind="ExternalOutput").ap()

    def dint(name, shape, dt=F32):
        return nc.dram_tensor(name, list(shape), dt).ap()

    def dscr(name, shape, dt, is_in, is_out):
        if fused:
            return dint(name, shape, dt), None
        i = din(name + "_i", shape, dt) if is_in else None
        o = dout(name + "_o", shape, dt) if is_out else None
        return i, o

    first = mode in ("fused", "A")
    last = mode in ("fused", "C")
    layers_s1 = {"fused": [0, 1], "A": [0], "B": [1], "C": []}[mode]
    layers_s2 = {"fused": [0, 1], "A": [], "B": [0], "C": [1]}[mode]

    x_in = din("x", [TOK, D]) if first else None
    mem_in = din("mem", [256, D])
    hv_in = din("halo_valid", [128, 1])
    w1i = [din("w_ffn1_in", [NL, D, 2 * FF]), din("w_ffn2_in", [NL, D, 2 * FF])]
    w1o = [din("w_ffn1_out", [NL, FF, D]), din("w_ffn2_out", [NL, FF, D])]
    w_in = din("w_in", [NL, D, 2048])
    w_gate = din("w_gate", [NL, D, 3 * D])
    w_br = din("w_br", [NL, D, D])
    w_o = din("w_o", [NL, D, D])
    w_mkv = din("w_mem_kv", [NL, D, 512])
    gains_in = din("gains", [128, NL * 6 * NCH])
    mgain_in = din("mgain", [128, NL * NCH])
    bgate_in = din("bgate", [128, NL * 3 * NCH])
    sinks_in = din("sinks", [128, NL * 3])
    biasT_in = din("biasT", [128, 6, 512])
    maskT_in = din("maskT", [128, 6, 512])
    ident_in = din("ident", [128, 128])
    out_ap = dout("out", [TOK, D]) if last else None

    HROWS = 2 * 2816 * 128 // 1024
    if fused:
        XS_r = XS_w = dint("XS", [NT, 128, NCH * TT], F32)
        QT_r = QT_w = dint("QT", [128, 8, TOK], BF16)
        KT_r = KT_w = dint("KT", [128, 4, TOK], BF16)
        VT_r = VT_w = dint("VT", [4, TOK, 128], BF16)
        HS = dint("HS", [HROWS, 1024], BF16)
        HR = dint("HR", [2 * HROWS, 1024], BF16)
    else:
        XS_r = din("XS_i", [NT, 128, NCH * TT], F32) if mode in ("B", "C") else None
        XS_w = dout("XS_o", [NT, 128, NCH * TT], F32) if mode in ("A", "B") else None
        QT_r = din("QT_i", [128, 8, TOK], BF16) if mode in ("B", "C") else None
        KT_r = din("KT_i", [128, 4, TOK], BF16) if mode in ("B", "C") else None
        VT_r = din("VT_i", [4, TOK, 128], BF16) if mode in ("B", "C") else None
        QT_w = dout("QT_o", [128, 8, TOK], BF16) if mode in ("A", "B") else None
        KT_w = dout("KT_o", [128, 4, TOK], BF16) if mode in ("A", "B") else None
        VT_w = dout("VT_o", [4, TOK, 128], BF16) if mode in ("A", "B") else None
        HS = None
        HR = din("HR", [2 * HROWS, 1024], BF16) if mode in ("B", "C") else None
    OT = dint("OT", [128, 8, TOK], BF16) if fused or not layers_s2 else dout("OT_dbg", [128, 8, TOK], BF16)
    W1 = [[dint(f"W1_{l}_{f}", [11, 128, 4096], BF16) for f in range(2)] for l in range(NL)]
    W2 = [[dint(f"W2_{l}_{f}", [8, 128, JH * 128], BF16) for f in range(2)] for l in range(NL)]
    WIN = [dint(f"WIN_{l}", [4, 128, 4096], BF16) for l in range(NL)]
    WG = [dint(f"WG_{l}", [6, 128, 4096], BF16) for l in range(NL)]
    WBR = [dint(f"WBR_{l}", [2, 128, 4096], BF16) for l in range(NL)]
    WO = [dint(f"WO_{l}", [2, 128, 4096], BF16) for l in range(NL)]
    WM = [dint(f"WM_{l}", [128, 4096], BF16) for l in range(NL)]

    es = ExitStack()
    with es:
        P = Prog(nc, es)

        def sb(name, shape, dt):
            return es.enter_context(nc.sbuf_tensor("sb_" + name, list(shape), dt))

        ident = sb("ident", [128, 128], F32)
        ones = sb("ones", [128, 128], BF16)
        gains = sb("gains", [128, NL * 6 * NCH], F32)
        mgain = sb("mgain", [128, NL * NCH], F32)
        bgate = sb("bgate", [128, NL * 3 * NCH], F32)
        sinke = sb("sinke", [128, NL * 3], F32)
        hval = sb("hval", [128, 1], F32)
        ccdummy = sb("ccdummy", [128, 8], F32)
        epsb = sb("epsb", [128, 2], F32)
        Emat = sb("Emat", [128, 6, 512], BF16)
        Efirst = sb("Efirst", [128, 6, 512], BF16)
        KC = sb("KC", [128, NL, 2, 2, 256], BF16)
        VC = sb("VC", [128, NL, 2, 256], BF16)
        psum = [es.enter_context(nc.psum_tensor(f"ps{i}", [128, 512], F32)) for i in range(8)]
        r_psum = [P.res(f"ps{i}") for i in range(8)]
        ps_i = [0]
        ARENA_W = 43000
        arena_t = sb("arena", [128, ARENA_W], F32)
        arena = Arena(arena_t[:, :])
        r_const = P.res("const")

        def next_ps():
            i = ps_i[0] % 7
            ps_i[0] += 1
            return psum[i], r_psum[i]

        def stat_ps():
            return psum[7], r_psum[7]

        def load_const(dst, src, nm):
            r = P.res(nm)
            P.dma("sp", lambda e, d=dst, s=src: e.dma_start(out=d, in_=s), r, writes=[r])
            return r

        r_ident = load_const(ident[:, :], ident_in[:, :], "ident")
        r_gains = load_const(gains[:, :], gains_in[:, :], "gains")
        r_mgain = load_const(mgain[:, :], mgain_in[:, :], "mgain")
        r_bgate = load_const(bgate[:, :], bgate_in[:, :], "bgate")
        r_sink = load_const(sinke[:, :], sinks_in[:, :], "sinks")
        r_hval = load_const(hval[:, :], hv_in[:, :], "hval")
        r_ones = P.res("ones")
        P.op("pool", lambda e: e.memset(epsb[:, 0:1], EPS), writes=[r_ones])
        P.op("pool", lambda e: e.memset(epsb[:, 1:2], EPS / 0.25), writes=[r_ones])
        P.op("pool", lambda e: e.memset(ones[:, :], 1.0), writes=[r_ones])
        P.op("pool", lambda e: e.memset(KC[:, :, :, :, :].rearrange("p a b c d -> p (a b c d)"), 0.0), writes=[r_const])
        P.op("act", lambda e: e.activation(out=sinke[:, :], in_=sinke[:, :], func=AF.Exp),
             reads=[r_sink], writes=[r_sink])
        r_E = P.res("E")
        if layers_s2:
            arena.reset()
            bt = arena.alloc([6, 512], F32)
            mt = arena.alloc([6, 512], F32)
            r_bt, r_mt = P.res("bt"), P.res("mt")
            P.dma("sp", lambda e: e.dma_start(out=bt, in_=biasT_in[:, :, :]), r_bt, writes=[r_bt])
            P.dma("sp", lambda e: e.dma_start(out=mt, in_=maskT_in[:, :, :]), r_mt, writes=[r_mt])
            P.op("act", lambda e: e.activation(out=bt, in_=bt, func=AF.Exp), reads=[r_bt], writes=[r_bt])
            P.op("dve", lambda e: e.tensor_tensor(out=Emat[:, :, :], in0=bt, in1=mt, op=ALU.mult),
                 reads=[r_bt, r_mt], writes=[r_E])
            P.op("dve", lambda e: e.tensor_copy(out=Efirst[:, :, :], in_=Emat[:, :, :]),
                 reads=[r_E], writes=[r_E])
            for hi in range(2):
                P.op("dve", lambda e, hi=hi: e.tensor_scalar(
                    out=Efirst[:, :, hi * 256:hi * 256 + 128], in0=Emat[:, :, hi * 256:hi * 256 + 128],
                    scalar1=hval[:, 0:1], scalar2=None, op0=ALU.mult),
                    reads=[r_E, r_hval], writes=[r_E])

        P.barrier()

        r_W1 = [[P.res(f"W1{l}{f}") for f in range(2)] for l in range(NL)]
        r_W2 = [[P.res(f"W2{l}{f}") for f in range(2)] for l in range(NL)]
        r_WIN = [P.res(f"WIN{l}") for l in range(NL)]
        r_WG = [P.res(f"WG{l}") for l in range(NL)]
        r_WBR = [P.res(f"WBR{l}") for l in range(NL)]
        r_WO = [P.res(f"WO{l}") for l in range(NL)]
        r_WM = [P.res(f"WM{l}") for l in range(NL)]

        cast_tasks = []

        def cast(dst, src, r):
            cast_tasks.append((dst, src, r))

        def emit_casts(n=None):
            k = len(cast_tasks) if n is None else min(n, len(cast_tasks))
            for _ in range(k):
                dst, src, r = cast_tasks.pop(0)
                P.dma("poolq", lambda e, d=dst, s=src: e.dma_start(out=d, in_=s), r, writes=[r], accumulate=True)

        def cast_w1(l, f):
            src = w1i[f][l].rearrange("(c p) n -> p c n", p=128)
            for blk in range(11):
                dst = W1[l][f][blk].rearrange("p (c ab n) -> p c ab n", c=8, ab=2)
                for ab in range(2):
                    c0 = ab * FF + blk * 256
                    cast(dst[:, :, ab, :], src[:, :, c0:c0 + 256], r_W1[l][f])

        def cast_w2(l, f):
            src = w1o[f][l].rearrange("(j p) n -> p j n", p=128)
            for m in range(8):
                dst = W2[l][f][m].rearrange("p (j n) -> p j n", j=JH)
                cast(dst, src[:, :, m * 128:(m + 1) * 128], r_W2[l][f])

        def cast_cols(dst_blocks, src2d, nblk, width, r):
            src = src2d.rearrange("(c p) n -> p c n", p=128)
            for b in range(nblk):
                dst = dst_blocks[b].rearrange("p (c n) -> p c n", c=8)
                cast(dst, src[:, :, b * width:(b + 1) * width], r)

        need_s1 = set(layers_s1)
        need_s3 = set(layers_s2)
        for l in range(NL):
            cast(WM[l].rearrange("p (c n) -> p c n", c=8),
                 w_mkv[l].rearrange("(c p) n -> p c n", p=128), r_WM[l])
        for l in range(NL):
            if l in need_s1:
                cast_w1(l, 0)
                cast_w2(l, 0)
                cast_cols(WIN[l], w_in[l], 4, 512, r_WIN[l])
                if l == 0 and fused:
                    n_first = len(cast_tasks)
            if l in need_s3:
                cast_cols(WG[l], w_gate[l], 6, 512, r_WG[l])
                cast_cols(WBR[l], w_br[l], 2, 512, r_WBR[l])
                cast_cols(WO[l], w_o[l], 2, 512, r_WO[l])
                cast_w1(l, 1)
                cast_w2(l, 1)
        if fused:
            emit_casts(n_first)
        else:
            emit_casts()

        class WS:
            pass

        def alloc_token_stage():
            arena.reset()
            w = WS()
            w.X = [arena.alloc([NCH, TT], F32) for _ in range(2)]
            w.rX = [P.res("X0"), P.res("X1")]
            w.Y = arena.alloc([NCH, TT], F32)
            w.rY = P.res("Y")
            w.H = arena.alloc([NCH, TT], BF16)
            w.rH = P.res("H")
            w.G = arena.alloc([JH, TT], BF16)
            w.rG = [P.res(f"G{j}") for j in range(JH)]
            w.slots = [arena.alloc([4096], BF16) for _ in range(5)]
            w.rS = [P.res(f"slot{i}") for i in range(5)]
            w.si = 0
            w.rs = [arena.alloc([TT], F32) for _ in range(2)]
            w.r_rs = [P.res("rs0"), P.res("rs1")]
            w.rsi = 0
            w.tmp = [arena.alloc([TT], F32) for _ in range(3)]
            w.r_tmp = [P.res(f"tmp{i}") for i in range(3)]
            w.tmi = 0
            w.sil = [arena.alloc([TT], BF16) for _ in range(3)]
            w.r_sil = [P.res(f"sil{i}") for i in range(3)]
            w.sli = 0
            w.gt = [arena.alloc([TT], BF16) for _ in range(3)]
            w.r_gt = [P.res(f"gt{i}") for i in range(3)]
            w.M = arena.alloc([NCH, TT], BF16)
            w.rM = P.res("M")
            w.O = arena.alloc([NCH, TT], BF16)
            w.rO = P.res("Osb")
            w.QK = arena.alloc([12, TT], BF16)
            w.rQK = [P.res(f"QK{i}") for i in range(12)]
            w.V = arena.alloc([4, 512], BF16)
            w.rV = P.res("Vst")
            return w

        def slot_load(w, src_ap, r_src):
            i = w.si % len(w.slots)
            w.si += 1
            s, r = w.slots[i], w.rS[i]
            n = src_ap.shape[1]
            P.dma("sp", lambda e, s=s, a=src_ap: e.dma_start(out=s[:, 0:n], in_=a), r, reads=[r_src], writes=[r])
            return s, r

        def stat_chunk(w, c, src, r_src):
            ps, rps = stat_ps()
            P.op("act", lambda e: e.activation(out=w.H[:, c, :], in_=src, func=AF.Square),
                 reads=[r_src], writes=[w.rH])
            P.mm([lambda e: e.matmul(ps[:, :], lhsT=ones[:, :], rhs=w.H[:, c, :],
                                     start=(c == 0), stop=(c == NCH - 1))],
                 reads=[w.rH, r_ones], writes=[rps])

        def rstd_from_stats(w, alpha=1.0):
            ps, rps = stat_ps()
            i = w.rsi % 2
            w.rsi += 1
            rs, r_rs = w.rs[i], w.r_rs[i]
            a2 = float(alpha) ** 2
            P.op("act", lambda e: e.activation(out=rs, in_=ps[:, :], func=AF.Ln, bias=epsb[:, 0:1] if a2 == 1.0 else epsb[:, 1:2],
                                               scale=1.0 / (D * a2)), reads=[rps, r_ones], writes=[r_rs])
            P.op("act", lambda e: e.activation(out=rs, in_=rs, func=AF.Exp, scale=-0.5), reads=[r_rs], writes=[r_rs])
            return rs, r_rs

        def h_from(w, X, rX, gcol0, rs, r_rs, gtile=None, r_g=None):
            gtile = gains if gtile is None else gtile
            r_g = r_gains if r_g is None else r_g
            for c in range(NCH):
                P.op("dve", lambda e, c=c: e.scalar_tensor_tensor(
                    out=w.H[:, c, :], in0=X[:, c, :], scalar=gtile[:, gcol0 + c:gcol0 + c + 1], in1=rs,
                    op0=ALU.mult, op1=ALU.mult), reads=[rX, r_rs, r_g], writes=[w.rH])

        def prenorm(w, X, rX, gcol0, gtile=None, r_g=None):
            for c in range(NCH):
                stat_chunk(w, c, X[:, c, :], rX)
            rs, r_rs = rstd_from_stats(w)
            h_from(w, X, rX, gcol0, rs, r_rs, gtile, r_g)

        def postnorm_add(w, X, rX, gcol0, alpha, then_pre=None):
            rs, r_rs = rstd_from_stats(w, alpha)
            for c in range(NCH):
                i = w.tmi % 3
                w.tmi += 1
                t, rt = w.tmp[i], w.r_tmp[i]
                P.op("dve", lambda e, c=c, t=t: e.scalar_tensor_tensor(
                    out=t, in0=w.Y[:, c, :], scalar=gains[:, gcol0 + c:gcol0 + c + 1], in1=rs,
                    op0=ALU.mult, op1=ALU.mult), reads=[w.rY, r_rs, r_gains], writes=[rt])
                P.op("dve", lambda e, c=c, t=t: e.tensor_tensor(
                    out=X[:, c, :], in0=X[:, c, :], in1=t, op=ALU.add), reads=[rt, rX], writes=[rX])
                if then_pre is not None:
                    stat_chunk(w, c, X[:, c, :], rX)
            if then_pre is not None:
                rs2, r_rs2 = rstd_from_stats(w)
                h_from(w, X, rX, then_pre, rs2, r_rs2)

        def ffn(w, l, f):
            for blk in range(11):
                s, rs_ = slot_load(w, W1[l][f][blk], r_W1[l][f])
                sv = s.rearrange("p (c ab n) -> p c ab n", c=8, ab=2)
                for jj in range(2):
                    j = blk * 2 + jj
                    pa, rpa = next_ps()
                    P.mm([lambda e, c=c, pa=pa, jj=jj, sv=sv: e.matmul(
                        pa[:, :], lhsT=sv[:, c, 0, jj * 128:(jj + 1) * 128], rhs=w.H[:, c, :],
                        start=(c == 0), stop=(c == NCH - 1)) for c in range(NCH)],
                        reads=[rs_, w.rH], writes=[rpa])
                    pb, rpb = next_ps()
                    P.mm([lambda e, c=c, pb=pb, jj=jj, sv=sv: e.matmul(
                        pb[:, :], lhsT=sv[:, c, 1, jj * 128:(jj + 1) * 128], rhs=w.H[:, c, :],
                        start=(c == 0), stop=(c == NCH - 1)) for c in range(NCH)],
                        reads=[rs_, w.rH], writes=[rpb])
                    i = w.sli % 3
                    w.sli += 1
                    sl, rsl = w.sil[i], w.r_sil[i]
                    P.op("act", lambda e, sl=sl, pa=pa: e.activation(out=sl, in_=pa[:, :], func=AF.Silu),
                         reads=[rpa], writes=[rsl])
                    P.op("dve", lambda e, sl=sl, pb=pb, j=j: e.tensor_tensor(
                        out=w.G[:, j, :], in0=sl, in1=pb[:, :], op=ALU.mult),
                        reads=[rsl, rpb], writes=[w.rG[j]])
            for m in range(8):
                s, rs_ = slot_load(w, W2[l][f][m][:, :], r_W2[l][f])
                sv = s[:, 0:JH * 128].rearrange("p (j n) -> p j n", j=JH)
                py, rpy = next_ps()
                P.mm([lambda e, j=j, py=py, sv=sv: e.matmul(
                    py[:, :], lhsT=sv[:, j, :], rhs=w.G[:, j, :], start=(j == 0), stop=(j == JH - 1))
                    for j in range(JH)], reads=[rs_] + w.rG, writes=[rpy])
                P.op("act", lambda e, m=m, py=py: e.copy(out=w.Y[:, m, :], in_=py[:, :]),
                     reads=[rpy], writes=[w.rY])
                stat_chunk(w, m, py[:, :], rpy)

        def in_proj(w, l, t):
            ev = 0
            for blk in range(3):
                s, rs_ = slot_load(w, WIN[l][blk], r_WIN[l])
                sv = s.rearrange("p (c n) -> p c n", c=8)
                for cc in range(4):
                    qi = blk * 4 + cc
                    rate = Q_RATES[qi] if qi < 8 else K_RATES[qi - 8]
                    ps, rps = next_ps()
                    P.mm([lambda e, c=c, ps=ps, cc=cc, sv=sv: e.matmul(
                        ps[:, :], lhsT=sv[:, c, cc * 128:(cc + 1) * 128], rhs=w.H[:, c, :],
                        start=(c == 0), stop=(c == NCH - 1)) for c in range(NCH)],
                        reads=[rs_, w.rH], writes=[rps])
                    if rate == 1:
                        dst, src = w.QK[:, qi, :], ps[:, :]
                    else:
                        dst = w.QK[:, qi, :].rearrange("p (j i) -> p i j", j=rate)
                        src = ps[:, :].rearrange("p (i j) -> p i j", j=rate)
                    if ev % 2 == 0:
                        P.op("act", lambda e, dst=dst, src=src: e.copy(out=dst, in_=src),
                             reads=[rps], writes=[w.rQK[qi]])
                    else:
                        P.op("dve", lambda e, dst=dst, src=src: e.tensor_copy(out=dst, in_=src),
                             reads=[rps], writes=[w.rQK[qi]])
                    ev += 1
            def st(dst, src, rr, dres):
                P.dma("poolq", lambda e, d=dst, s=src: e.dma_start(out=d, in_=s), rr[0],
                      reads=rr, writes=[dres], accumulate=True)

            tsl = slice(t * TT, (t + 1) * TT)
            st(QT_w[:, 0:4, tsl], w.QK[:, 0:4, :], w.rQK[0:4], r_QT)
            for qi in (4, 5):
                r = Q_RATES[qi]
                st(QT_w[:, qi, :].rearrange("p (j n) -> p j n", j=r)[:, :, t * (TT // r):(t + 1) * (TT // r)],
                   w.QK[:, qi, :].rearrange("p (j i) -> p j i", j=r), [w.rQK[qi]], r_QT)
            st(QT_w[:, 6:8, tsl], w.QK[:, 6:8, :], w.rQK[6:8], r_QT)
            st(KT_w[:, 0:2, tsl], w.QK[:, 8:10, :], w.rQK[8:10], r_KT)
            for ki in (2, 3):
                r = K_RATES[ki]
                st(KT_w[:, ki, :].rearrange("p (j n) -> p j n", j=r)[:, :, t * (TT // r):(t + 1) * (TT // r)],
                   w.QK[:, 8 + ki, :].rearrange("p (j i) -> p j i", j=r), [w.rQK[8 + ki]], r_KT)
            s, rs_ = slot_load(w, WIN[l][3], r_WIN[l])
            sv = s.rearrange("p (c n) -> p c n", c=8)
            for tb in range(4):
                ps, rps = next_ps()
                P.mm([lambda e, c=c, ps=ps, tb=tb, sv=sv: e.matmul(
                    ps[:, :], lhsT=w.H[:, c, tb * 128:(tb + 1) * 128], rhs=sv[:, c, :],
                    start=(c == 0), stop=(c == NCH - 1)) for c in range(NCH)],
                    reads=[rs_, w.rH], writes=[rps])
                P.op("dve", lambda e, tb=tb, ps=ps: e.tensor_copy(out=w.V[:, tb, :], in_=ps[:, :]),
                     reads=[rps], writes=[w.rV])
            for g in range(4):
                r = K_RATES[g]
                src = w.V[:, :, g * 128:(g + 1) * 128]
                if r == 1:
                    dst = VT_w[g, t * TT:(t + 1) * TT, :].rearrange("(tb p) d -> p tb d", p=128)
                    st(dst, src, [w.rV], r_VT)
                else:
                    npr = TOK // r
                    for j in range(r):
                        dst = VT_w[g, j * npr + t * (TT // r):j * npr + (t + 1) * (TT // r), :] \
                            .rearrange("(tb i) d -> i tb d", tb=4)
                        st(dst, w.V[j:128:r, :, g * 128:(g + 1) * 128], [w.rV], r_VT)

        r_QT, r_KT, r_VT, r_OT = P.res("QT"), P.res("KT"), P.res("VT"), P.res("OT")
        r_XS = [P.res(f"XS{t}") for t in range(NT)]
        r_HS, r_HR = P.res("HS"), P.res("HR")
        r_out = P.res("out")

        def load_x_tokmajor(w, t, X, rX):
            xin = w.Y
            xv = xin.rearrange("p c t -> p (c t)").rearrange("p (tb f) -> p tb f", tb=4)
            P.dma("sp", lambda e: e.dma_start(
                out=xv, in_=x_in[t * TT:(t + 1) * TT, :].rearrange("(tb p) f -> p tb f", p=128)),
                w.rY, writes=[w.rY])
            for c in range(NCH):
                ps, rps = next_ps()
                P.mm([lambda e, tb=tb, ps=ps, c=c: e.transpose(
                    ps[:, tb * 128:(tb + 1) * 128], xv[:, tb, c * 128:(c + 1) * 128], ident[:, :])
                    for tb in range(4)], reads=[w.rY, r_ident], writes=[rps])
                if c % 2 == 0:
                    P.op("act", lambda e, c=c, ps=ps: e.copy(out=X[:, c, :], in_=ps[:, :]), reads=[rps], writes=[rX])
                else:
                    P.op("dve", lambda e, c=c, ps=ps: e.tensor_copy(out=X[:, c, :], in_=ps[:, :]),
                         reads=[rps], writes=[rX])

        def store_out_tokmajor(w, t, X, rX):
            xo = w.Y
            xv = xo.rearrange("p c t -> p (c t)").rearrange("p (tb f) -> p tb f", tb=4)
            for tb in range(4):
                for hf in range(2):
                    ps, rps = next_ps()
                    P.mm([lambda e, cc=cc, ps=ps, tb=tb, hf=hf: e.transpose(
                        ps[:, cc * 128:(cc + 1) * 128], X[:, hf * 4 + cc, tb * 128:(tb + 1) * 128], ident[:, :])
                        for cc in range(4)], reads=[rX, r_ident], writes=[rps])
                    if hf == 0:
                        P.op("act", lambda e, ps=ps, tb=tb: e.copy(out=xv[:, tb, 0:512], in_=ps[:, :]),
                             reads=[rps], writes=[w.rY])
                    else:
                        P.op("dve", lambda e, ps=ps, tb=tb: e.tensor_copy(out=xv[:, tb, 512:1024], in_=ps[:, :]),
                             reads=[rps], writes=[w.rY])
            P.dma("poolq", lambda e: e.dma_start(
                out=out_ap[t * TT:(t + 1) * TT, :].rearrange("(tb p) f -> p tb f", p=128), in_=xv),
                w.rY, reads=[w.rY], writes=[r_out], accumulate=True)

        def s1_body(w, l, t, X, rX, pre_done=False):
            gb = (l * 6) * NCH
            if not pre_done:
                prenorm(w, X, rX, gb + 0 * NCH)
            ffn(w, l, 0)
            postnorm_add(w, X, rX, gb + 1 * NCH, 0.5, then_pre=gb + 2 * NCH)
            P.dma("poolq", lambda e: e.dma_start(out=XS_w[t], in_=X.rearrange("p c t -> p (c t)")),
                  rX, reads=[rX], writes=[r_XS[t]])
            in_proj(w, l, t)

        def mixing(w, l, t, X, rX):
            s_o = None
            for c in range(NCH):
                hf, cc = divmod(c, 4)
                if cc == 0:
                    s_brh = slot_load(w, WBR[l][hf], r_WBR[l])
                    s_g = [slot_load(w, WG[l][br * 2 + hf], r_WG[l]) for br in range(3)]
                brv = s_brh[0].rearrange("p (k n) -> p k n", k=8)
                kr = [(0, 3), (3, 6), (6, 8)]
                mts = []
                for br in range(3):
                    gv = s_g[br][0].rearrange("p (k n) -> p k n", k=8)
                    pg, rpg = next_ps()
                    P.mm([lambda e, k=k, pg=pg, gv=gv, cc=cc: e.matmul(
                        pg[:, :], lhsT=gv[:, k, cc * 128:(cc + 1) * 128], rhs=w.H[:, k, :],
                        start=(k == 0), stop=(k == NCH - 1)) for k in range(NCH)],
                        reads=[s_g[br][1], w.rH], writes=[rpg])
                    pb, rpb = next_ps()
                    k0, k1 = kr[br]
                    P.mm([lambda e, k=k, pb=pb, brv=brv, cc=cc, k0=k0, k1=k1: e.matmul(
                        pb[:, :], lhsT=brv[:, k, cc * 128:(cc + 1) * 128], rhs=w.O[:, k, :],
                        start=(k == k0), stop=(k == k1 - 1)) for k in range(k0, k1)],
                        reads=[s_brh[1], w.rO], writes=[rpb])
                    gt, rgt = w.gt[br], w.r_gt[br]
                    bcol = (l * 3 + br) * NCH + c
                    P.op("act", lambda e, gt=gt, pg=pg, bcol=bcol: e.activation(
                        out=gt, in_=pg[:, :], func=AF.Sigmoid, bias=bgate[:, bcol:bcol + 1], scale=1.0),
                        reads=[rpg, r_bgate], writes=[rgt])
                    i = w.tmi % 3
                    w.tmi += 1
                    mt_, rmt = w.tmp[i], w.r_tmp[i]
                    P.op("dve", lambda e, mt_=mt_, gt=gt, pb=pb: e.tensor_tensor(
                        out=mt_, in0=gt, in1=pb[:, :], op=ALU.mult), reads=[rgt, rpb], writes=[rmt])
                    mts.append((mt_, rmt))
                P.op("dve", lambda e, a=mts[0][0], b=mts[1][0]: e.tensor_tensor(out=a, in0=a, in1=b, op=ALU.add),
                     reads=[mts[1][1]], writes=[mts[0][1]])
                P.op("dve", lambda e, a=mts[0][0], b=mts[2][0], c=c: e.tensor_tensor(
                    out=w.M[:, c, :], in0=a, in1=b, op=ALU.add),
                    reads=[mts[0][1], mts[2][1]], writes=[w.rM])
            for m in range(8):
                hf, mm_ = divmod(m, 4)
                if mm_ == 0:
                    s_o = slot_load(w, WO[l][hf], r_WO[l])
                ov = s_o[0].rearrange("p (k n) -> p k n", k=8)
                py, rpy = next_ps()
                P.mm([lambda e, k=k, py=py, ov=ov, mm_=mm_: e.matmul(
                    py[:, :], lhsT=ov[:, k, mm_ * 128:(mm_ + 1) * 128], rhs=w.M[:, k, :],
                    start=(k == 0), stop=(k == NCH - 1)) for k in range(NCH)],
                    reads=[s_o[1], w.rM], writes=[rpy])
                P.op("act", lambda e, m=m, py=py: e.copy(out=w.Y[:, m, :], in_=py[:, :]),
                     reads=[rpy], writes=[w.rY])
                stat_chunk(w, m, py[:, :], rpy)

        def s3_load_x(w, t):
            X, rX = w.X[t % 2], w.rX[t % 2]
            P.dma("sp", lambda e: e.dma_start(out=X.rearrange("p c t -> p (c t)"), in_=XS_r[t]),
                  rX, reads=[r_XS[t]], writes=[rX])

        def s3_body(w, l, t, X, rX):
            gb = (l * 6) * NCH
            if t == 0:
                s3_load_x(w, 0)
            if t + 1 < NT:
                s3_load_x(w, t + 1)
            P.dma("sp", lambda e: e.dma_start(out=w.O, in_=OT[:, :, t * TT:(t + 1) * TT]),
                  w.rO, reads=[r_OT], writes=[w.rO])
            prenorm(w, X, rX, gb + 2 * NCH)
            mixing(w, l, t, X, rX)
            postnorm_add(w, X, rX, gb + 3 * NCH, 1.0, then_pre=gb + 4 * NCH)
            ffn(w, l, 1)
            nxt = ((l + 1) * 6) * NCH if (l + 1 < NL and (l + 1) in layers_s1) else None
            postnorm_add(w, X, rX, gb + 5 * NCH, 0.5, then_pre=nxt)

        def mem_kv(w):
            X, rX = w.X[0], w.rX[0]
            mv = w.Y.rearrange("p c t -> p (c t)")[:, 0:2 * D].rearrange("p (tb f) -> p tb f", tb=2)
            P.dma("sp", lambda e: e.dma_start(out=mv, in_=mem_in[:, :].rearrange("(tb p) f -> p tb f", p=128)),
                  w.rY, writes=[w.rY])
            P.op("pool", lambda e: e.memset(X.rearrange("p c t -> p (c t)"), 1.0), writes=[rX])
            for c in range(NCH):
                ps, rps = next_ps()
                P.mm([lambda e, tb=tb, ps=ps, c=c: e.transpose(
                    ps[:, tb * 128:(tb + 1) * 128], mv[:, tb, c * 128:(c + 1) * 128], ident[:, :])
                    for tb in range(2)], reads=[w.rY, r_ident], writes=[rps])
                P.op("dve", lambda e, c=c, ps=ps: e.tensor_copy(out=X[:, c, 0:256], in_=ps[:, 0:256]),
                     reads=[rps], writes=[rX])
            for l in sorted(set(layers_s2)):
                prenorm(w, X, rX, l * NCH, gtile=mgain, r_g=r_mgain)
                s, rs_ = slot_load(w, WM[l][:, :], r_WM[l])
                sv = s.rearrange("p (c n) -> p c n", c=8)
                for mc in range(2):
                    ps, rps = next_ps()
                    P.mm([lambda e, c=c, ps=ps, mc=mc, sv=sv: e.matmul(
                        ps[:, 0:256], lhsT=sv[:, c, mc * 128:(mc + 1) * 128], rhs=w.H[:, c, 0:256],
                        start=(c == 0), stop=(c == NCH - 1)) for c in range(NCH)],
                        reads=[rs_, w.rH], writes=[rps])
                    P.op("dve", lambda e, ps=ps, mc=mc, l=l: e.tensor_copy(out=KC[0:64, l, mc, 0, :], in_=ps[0:64, 0:256]),
                         reads=[rps], writes=[r_const])
                    P.op("dve", lambda e, ps=ps, mc=mc, l=l: e.tensor_copy(out=KC[64:128, l, mc, 1, :], in_=ps[64:128, 0:256]),
                         reads=[rps], writes=[r_const])
                for kt in range(2):
                    ps, rps = next_ps()
                    P.mm([lambda e, c=c, ps=ps, kt=kt, sv=sv: e.matmul(
                        ps[:, 0:256], lhsT=w.H[:, c, kt * 128:(kt + 1) * 128], rhs=sv[:, c, 256:512],
                        start=(c == 0), stop=(c == NCH - 1)) for c in range(NCH)],
                        reads=[rs_, w.rH], writes=[rps])
                    P.op("dve", lambda e, ps=ps, kt=kt, l=l: e.tensor_copy(out=VC[:, l, kt, :], in_=ps[:, 0:256]),
                         reads=[rps], writes=[r_const])

        def s2_stage(l):
            arena.reset()
            NQT = TOK // 128
            Qsb = [arena.alloc([TOK], BF16) for _ in range(2)]
            rQ = [P.res("Qsb0"), P.res("Qsb1")]
            KMAX = 16 * 128 + TOK
            KA = [arena.alloc([KMAX], BF16) for _ in range(2)]
            KB = [arena.alloc([KMAX], BF16) for _ in range(2)]
            rK = [P.res("Ksb0"), P.res("Ksb1")]
            for i in range(2):
                P.op("pool", lambda e, i=i: e.memset(KA[i][64:128, :], 0.0), writes=[rK[i]])
                P.op("pool", lambda e, i=i: e.memset(KB[i][0:64, :], 0.0), writes=[rK[i]])
            VBLK = 16 + TOK // 128
            Vsb = [arena.alloc([VBLK * 128], BF16) for _ in range(2)]
            rV = [P.res("Vsb0"), P.res("Vsb1")]
            Oacc = [arena.alloc([TOK], BF16) for _ in range(3)]
            rOa = [P.res(f"Oacc{g}") for g in range(3)]
            Dacc = [arena.alloc([TOK], F32)]
            rDa = [P.res("Dacc0")]
            Ost = [arena.alloc([TOK], BF16) for _ in range(2)]
            rOst = [P.res("Ost0"), P.res("Ost1")]
            Pex = [arena.alloc([512], BF16) for _ in range(4)]
            rPex = [P.res(f"Pex{i}") for i in range(4)]
            PT = [arena.alloc([512], BF16) for _ in range(4)]
            rPT = [P.res(f"PT{i}") for i in range(4)]
            Dr = [arena.alloc([512], F32) for _ in range(2)]
            rDr = [P.res("Dr0"), P.res("Dr1")]
            cnt = {"q": 0, "kv": 0, "p": 0, "d": 0, "o": 0}

            def load_q(qchunk):
                i = cnt["q"] % 2
                cnt["q"] += 1
                P.dma("sp", lambda e, i=i: e.dma_start(out=Qsb[i], in_=QT_r[:, qchunk, :]), rQ[i],
                      reads=[r_QT], writes=[rQ[i]])
                return Qsb[i], rQ[i]

            def load_kv(kchunk, vgroup, r):
                i = cnt["kv"] % 2
                cnt["kv"] += 1
                npos = TOK // r
                nb = npos // 128
                kvA = KA[i][:, 0:r * (128 + npos)].rearrange("p (j n) -> p j n", j=r)
                kvB = KB[i][:, 0:r * (128 + npos)].rearrange("p (j n) -> p j n", j=r)
                kv = (kvA, kvB)
                vv = Vsb[i][:, 0:r * (nb + 1) * 128].rearrange("p (j b d) -> p j b d", j=r, b=nb + 1)
                koff = sum(K_RATES[:kchunk]) * 128
                hk = HR[0:HROWS // 2, :].rearrange("a b -> (a b)").rearrange("(p n) -> p n", p=128)
                hv = HR[HROWS // 2:HROWS, :].rearrange("a b -> (a b)").rearrange("(n d) -> n d", d=128)
                first_dma = True
                for hi, kz in enumerate(kv):
                    ps_ = slice(hi * 64, (hi + 1) * 64)
                    P.dma("sp", lambda e, kz=kz, ps_=ps_: e.dma_start(
                        out=kz[ps_, :, 0:128], in_=hk[ps_, koff:koff + r * 128].rearrange("p (j n) -> p j n", j=r)),
                        rK[i], reads=[r_HR], writes=[rK[i]], accumulate=not first_dma)
                    first_dma = False
                    P.dma("sp", lambda e, kz=kz, ps_=ps_: e.dma_start(
                        out=kz[ps_, :, 128:], in_=KT_r[ps_, kchunk, :].rearrange("p (j n) -> p j n", j=r)),
                        rK[i], reads=[r_KT], writes=[rK[i]], accumulate=True)
                P.dma("sp", lambda e: e.dma_start(
                    out=vv[:, :, 0, :], in_=hv[koff:koff + r * 128, :].rearrange("(j p) d -> p j d", p=128)),
                    rV[i], reads=[r_HR], writes=[rV[i]])
                for j in range(r):
                    P.dma("sp", lambda e, j=j: e.dma_start(
                        out=vv[:, j, 1:, :],
                        in_=VT_r[vgroup, j * npos:(j + 1) * npos, :].rearrange("(b p) d -> p b d", p=128)),
                        rV[i], reads=[r_VT], writes=[rV[i]], accumulate=True)
                return kv, rK[i], vv, rV[i]

            def run_chunk(nqt, Q, rQ_, key_fn, val_fn, E_fn, rKV, sink_for_group):
                def s1(qt):
                    qcols = slice(qt * 128, (qt + 1) * 128)
                    pss, rpss = next_ps()
                    P.mm([lambda e, hi=hi, pc=pc, pss=pss, qt=qt, qcols=qcols: e.matmul(
                        pss[:, (hi * 2 + pc) * 128:(hi * 2 + pc + 1) * 128],
                        lhsT=key_fn(qt, hi, pc), rhs=Q[:, qcols], start=True, stop=True)
                        for hi in range(2) for pc in range(2)], reads=[rQ_] + rKV, writes=[rpss])
                    i = cnt["p"] % 4
                    cnt["p"] += 1
                    E = E_fn(qt)
                    if E is None:
                        P.op("act", lambda e, i=i, pss=pss: e.activation(
                            out=PT[i], in_=pss[:, :], func=AF.Exp, scale=0.125), reads=[rpss], writes=[rPT[i]])
                    else:
                        P.op("act", lambda e, i=i, pss=pss: e.activation(
                            out=Pex[i], in_=pss[:, :], func=AF.Exp, scale=0.125), reads=[rpss], writes=[rPex[i]])
                        P.op("pool" if qt % 3 == 2 else "dve",
                             lambda e, i=i, E=E: e.tensor_tensor(out=PT[i], in0=Pex[i], in1=E, op=ALU.mult),
                             reads=[rPex[i], r_E], writes=[rPT[i]])
                    return i

                def s2(qt, i, qi, pso, rpso, psd, rpsd):
                    fns = []
                    for hi in range(2):
                        for pc in range(2):
                            fns.append(lambda e, hi=hi, pc=pc, i=i, qt=qt, qi=qi: e.matmul(
                                pso[hi * 64:(hi + 1) * 64, qi * 128:(qi + 1) * 128],
                                lhsT=val_fn(qt, hi, pc), rhs=PT[i][:, (hi * 2 + pc) * 128:(hi * 2 + pc + 1) * 128],
                                start=(pc == 0), stop=(pc == 1)))
                    P.mm(fns, reads=[rPT[i]] + rKV, writes=[rpso])
                    fns = []
                    for hi in range(2):
                        for pc in range(2):
                            fns.append(lambda e, hi=hi, pc=pc, i=i, qi=qi: e.matmul(
                                psd[hi * 64:(hi + 1) * 64, qi * 128:(qi + 1) * 128],
                                lhsT=ones[:, hi * 64:(hi + 1) * 64], rhs=PT[i][:, (hi * 2 + pc) * 128:(hi * 2 + pc + 1) * 128],
                                start=(pc == 0), stop=(pc == 1)))
                    P.mm(fns, reads=[rPT[i], r_ones], writes=[rpsd])

                LA = 2
                pend = [s1(q) for q in range(min(LA, nqt))]
                banks = None
                for qt in range(nqt):
                    if qt + LA < nqt:
                        pend.append(s1(qt + LA))
                    if qt % 4 == 0:
                        banks = next_ps() + next_ps()
                    s2(qt, pend.pop(0), qt % 4, *banks)
                    if qt % 4 == 3:
                        sink_for_group(qt - 3)(*banks)

            def finish_direct(sink_col, Ot, rOt, c0):
                def f(pso, rpso, psd, rpsd):
                    i = cnt["d"] % 2
                    cnt["d"] += 1
                    if sink_col is not None:
                        P.op("act", lambda e, i=i: e.activation(
                            out=Dr[i], in_=psd[:, :], func=AF.Ln, bias=sinke[:, sink_col:sink_col + 1], scale=1.0),
                            reads=[rpsd, r_sink], writes=[rDr[i]])
                    else:
                        P.op("act", lambda e, i=i: e.activation(out=Dr[i], in_=psd[:, :], func=AF.Ln),
                             reads=[rpsd], writes=[rDr[i]])
                    P.op("act", lambda e, i=i: e.activation(out=Dr[i], in_=Dr[i], func=AF.Exp, scale=-1.0),
                         reads=[rDr[i]], writes=[rDr[i]])
                    P.op("dve", lambda e, i=i, c0=c0: e.tensor_tensor(
                        out=Ot[:, c0:c0 + 512], in0=pso[:, :], in1=Dr[i], op=ALU.mult),
                        reads=[rpso, rDr[i]], writes=[rOt])
                return f

            kv_cache = {}
            for sc, (qch, kch, vg, r, md, cols, has_sink) in enumerate(SELF_CHUNKS):
                if ("s2_sw" in DBG_SKIP and has_sink) or ("s2_dil" in DBG_SKIP and not has_sink):
                    continue
                Q, rQ_ = load_q(qch)
                if (kch, vg) not in kv_cache:
                    kv_cache.clear()
                    kv_cache[(kch, vg)] = load_kv(kch, vg, r)
                kv, rKc, vv, rVc = kv_cache[(kch, vg)]
                npos = TOK // r
                nb = npos // 128

                def key_fn(qt, hi, pc, kv=kv, nb=nb):
                    j, b = divmod(qt, nb)
                    return kv[hi][:, j, b * 128 + pc * 128:b * 128 + pc * 128 + 128]

                def val_fn(qt, hi, pc, vv=vv, nb=nb):
                    j, b = divmod(qt, nb)
                    return vv[:, j, b + pc, hi * 64:(hi + 1) * 64]

                def E_fn(qt, sc=sc, nb=nb):
                    return (Efirst if qt % nb == 0 else Emat)[:, sc, :]

                if has_sink:
                    oi = cnt["o"] % 2
                    cnt["o"] += 1
                    run_chunk(NQT, Q, rQ_, key_fn, val_fn, E_fn, [rKc, rVc],
                              lambda qt0, sc=sc, oi=oi: finish_direct(l * 3 + sc, Ost[oi], rOst[oi], qt0 * 128))
                    P.dma("poolq", lambda e, oi=oi, qch=qch: e.dma_start(out=OT[:, qch, :], in_=Ost[oi]), rOst[oi],
                          reads=[rOst[oi]], writes=[r_OT], accumulate=True)
                else:
                    g = sc - 3
                    ov3 = Oacc[g].rearrange("p (n r) -> p r n", r=r)
                    dv3 = Dacc[0].rearrange("p (n r) -> p r n", r=r)

                    def sink_acc(pso, rpso, psd, rpsd, qt0=None):
                        pass

                    def dil_sink(qt0, nb=nb, ov3=ov3, dv3=dv3, g=g):
                        if nb >= 4:
                            j, b0 = divmod(qt0, nb)
                            od = ov3[:, j, b0 * 128:b0 * 128 + 512]
                            dd = dv3[:, j, b0 * 128:b0 * 128 + 512]
                            shp = None
                        else:
                            a = 4 // nb
                            j0 = qt0 // nb
                            od = ov3[:, j0:j0 + a, :]
                            dd = dv3[:, j0:j0 + a, :]
                            shp = a

                        def f(pso, rpso, psd, rpsd, od=od, dd=dd, shp=shp, g=g):
                            so = pso[:, :] if shp is None else pso[:, :].rearrange("p (a n) -> p a n", a=shp)
                            sd = psd[:, :] if shp is None else psd[:, :].rearrange("p (a n) -> p a n", a=shp)
                            P.op("act", lambda e: e.copy(out=od, in_=so), reads=[rpso], writes=[rOa[g]])
                            if g == 0:
                                P.op("dve", lambda e: e.tensor_copy(out=dd, in_=sd), reads=[rpsd], writes=[rDa[0]])
                            else:
                                P.op("dve", lambda e: e.tensor_tensor(out=dd, in0=dd, in1=sd, op=ALU.add),
                                     reads=[rpsd], writes=[rDa[0]])
                        return f

                    run_chunk(NQT, Q, rQ_, key_fn, val_fn, E_fn, [rKc, rVc], dil_sink)
            if "s2_dil" in DBG_SKIP:
                P.op("pool", lambda e: e.memset(Dacc[0], 1.0), writes=[rDa[0]])
                for g in range(3):
                    P.op("pool", lambda e, g=g: e.memset(Oacc[g], 1.0), writes=[rOa[g]])
            if "s2_comb" not in DBG_SKIP:
                P.op("dve", lambda e: e.reciprocal(out=Dacc[0], in_=Dacc[0]), reads=[rDa[0]], writes=[rDa[0]])
            for g in range(3 if "s2_comb" not in DBG_SKIP else 0):
                oi = cnt["o"] % 2
                cnt["o"] += 1
                P.op("dve", lambda e, g=g, oi=oi: e.tensor_tensor(out=Ost[oi], in0=Oacc[g], in1=Dacc[0], op=ALU.mult),
                     reads=[rOa[g], rDa[0]], writes=[rOst[oi]])
                P.dma("poolq", lambda e, oi=oi, g=g: e.dma_start(out=OT[:, 3 + g, :], in_=Ost[oi]), rOst[oi],
                      reads=[rOst[oi]], writes=[r_OT], accumulate=True)
            for mc in range(2 if "s2_mem" not in DBG_SKIP else 0):
                Q, rQ_ = load_q(6 + mc)

                def key_fn(qt, hi, pc, mc=mc):
                    return KC[:, l, mc, hi, pc * 128:(pc + 1) * 128]

                def val_fn(qt, hi, pc, mc=mc):
                    return VC[:, l, pc, mc * 128 + hi * 64:mc * 128 + (hi + 1) * 64]

                oi = cnt["o"] % 2
                cnt["o"] += 1
                run_chunk(NQT, Q, rQ_, key_fn, val_fn, lambda qt: None, [r_const],
                          lambda qt0, oi=oi: finish_direct(None, Ost[oi], rOst[oi], qt0 * 128))
                P.dma("poolq", lambda e, oi=oi, mc=mc: e.dma_start(out=OT[:, 6 + mc, :], in_=Ost[oi]), rOst[oi],
                      reads=[rOst[oi]], writes=[r_OT], accumulate=True)

        def halo_send(dst_buf):
            hk = dst_buf[0:HROWS // 2, :].rearrange("a b -> (a b)").rearrange("(p n) -> p n", p=128)
            hv = dst_buf[HROWS // 2:HROWS, :].rearrange("a b -> (a b)").rearrange("(n d) -> n d", d=128)
            for kch in range(4):
                r = K_RATES[kch]
                npos = TOK // r
                koff = sum(K_RATES[:kch]) * 128
                P.dma("sp", lambda e, kch=kch, r=r, npos=npos, koff=koff: e.dma_start(
                    out=hk[:, koff:koff + r * 128].rearrange("p (j n) -> p j n", j=r),
                    in_=KT_w[:, kch, :].rearrange("p (j n) -> p j n", j=r)[:, :, npos - 128:npos]),
                    r_HS, reads=[r_KT], writes=[r_HS], accumulate=True)
                P.dma("sp", lambda e, kch=kch, r=r, npos=npos, koff=koff: e.dma_start(
                    out=hv[koff:koff + r * 128, :].rearrange("(j p) d -> j p d", p=128),
                    in_=VT_w[kch, :, :].rearrange("(j n) d -> j n d", j=r)[:, npos - 128:npos, :]),
                    r_HS, reads=[r_VT], writes=[r_HS], accumulate=True)

        if not first:
            pass
        wts = alloc_token_stage()
        if layers_s2:
            mem_kv(wts)
        if first:
            per_tile = -(-len(cast_tasks) // max(NT - 1, 1))
            for t in range(NT):
                X, rX = wts.X[t % 2], wts.rX[t % 2]
                load_x_tokmajor(wts, t, X, rX)
                s1_body(wts, 0, t, X, rX)
                emit_casts(per_tile)
            emit_casts()
        for l in range(NL):
            if l not in layers_s2:
                continue
            if fused:
                halo_send(HS)
                r_cc = P.res("cc")
                P.dma("poolq", lambda e: e.collective_compute(
                    "AllGather", ALU.bypass, replica_groups=[[2 * i, 2 * i + 1] for i in range(n_cores // 2)],
                    ins=[HS[:, :].opt()], outs=[HR[:, :].opt()]), r_cc, reads=[r_HS], writes=[r_cc, r_HR], inc=1)
                P.op("pool", lambda e: e.memset(ccdummy[:, :], 0.0), reads=[r_cc], writes=[r_HR])
            P.barrier()
            if "s2" not in DBG_SKIP:
                s2_stage(l)
            P.barrier()
            wts = alloc_token_stage()
            for t in range(NT if "s3" not in DBG_SKIP else 0):
                X, rX = wts.X[t % 2], wts.rX[t % 2]
                s3_body(wts, l, t, X, rX)
                if l + 1 < NL and (l + 1) in layers_s1:
                    s1_body(wts, l + 1, t, X, rX, pre_done=True)
                elif l == NL - 1:
                    store_out_tokmajor(wts, t, X, rX)
        P.barrier(streams=("sp",))
        with nc.Block() as block:
            P.emit(block)
    return nc


_CACHE = {}


def _get_nc(TOK, mode, n_cores=N_CORES):
    key = (TOK, mode, n_cores)
    if key not in _CACHE:
        _CACHE[key] = build(TOK, mode, n_cores)
    return _CACHE[key]


def kernel(**inputs):
    x = np.asarray(inputs["x"], np.float32)
    mem = np.asarray(inputs["mem"], np.float32)
    B, S, _ = x.shape
    TOK = B * S // N_CORES
    halves = S // TOK
    sh = prep_shared(inputs)
    in_maps = []
    for c in range(N_CORES):
        b, hf = divmod(c, halves)
        m = dict(sh)
        m["x"] = np.ascontiguousarray(x[b, hf * TOK:(hf + 1) * TOK, :])
        m["mem"] = np.ascontiguousarray(mem[b])
        m["halo_valid"] = np.full((128, 1), 1.0 if hf > 0 else 0.0, np.float32)
        in_maps.append(m)
    nc = _get_nc(TOK, "fused")
    res = run_bass_kernel_spmd(nc, in_maps, core_ids=list(range(N_CORES)))
    out = np.empty((B, S, D), np.float32)
    for c in range(N_CORES):
        b, hf = divmod(c, halves)
        out[b, hf * TOK:(hf + 1) * TOK, :] = res.results[c]["out"]
    return out


def _halo_from(KT, VT, TOK):
    HROWS = 2 * 2816 * 128 // 1024
    hk = np.zeros((128, 2816), KT.dtype)
    hv = np.zeros((2816, 128), VT.dtype)
    for kch in range(4):
        r = K_RATES[kch]
        npos = TOK // r
        koff = sum(K_RATES[:kch]) * 128
        kk = KT[:, kch, :].reshape(128, r, npos)[:, :, npos - 128:]
        hk[:, koff:koff + r * 128] = kk.reshape(128, r * 128)
        vv = VT[kch].reshape(r, npos, 128)[:, npos - 128:, :]
        hv[koff:koff + r * 128, :] = vv.reshape(r * 128, 128)
    HR = np.zeros((2 * HROWS, 1024), KT.dtype)
    HR[0:HROWS // 2] = hk.reshape(HROWS // 2, 1024)
    HR[HROWS // 2:HROWS] = hv.reshape(HROWS // 2, 1024)
    return HR


def kernel_unfused(n_cores=N_CORES, debug=None, **inputs):
    x = np.asarray(inputs["x"], np.float32)
    mem = np.asarray(inputs["mem"], np.float32)
    B, S, _ = x.shape
    TOK = B * S // n_cores
    halves = S // TOK
    sh = prep_shared(inputs)
    base = []
    for c in range(n_cores):
        b, hf = divmod(c, halves)
        m = dict(sh)
        m["mem"] = np.ascontiguousarray(mem[b])
        m["halo_valid"] = np.full((128, 1), 1.0 if hf > 0 else 0.0, np.float32)
        base.append(m)
    cores = list(range(n_cores))

    def halos(res):
        hr = []
        for c in range(n_cores):
            b, hf = divmod(c, halves)
            src = c - 1 if hf > 0 else c
            hr.append(_halo_from(np.asarray(res[src]["KT_o"]), np.asarray(res[src]["VT_o"]), TOK))
        return hr

    ins = []
    for c in range(n_cores):
        b, hf = divmod(c, halves)
        m = dict(base[c])
        m["x"] = np.ascontiguousarray(x[b, hf * TOK:(hf + 1) * TOK, :])
        ins.append(m)
    ra = run_bass_kernel_spmd(_get_nc(TOK, "A", n_cores), ins, core_ids=cores).results
    if debug is not None:
        debug["A"] = ra
    hr = halos(ra)
    ins = []
    for c in range(n_cores):
        m = dict(base[c])
        m.update({"XS_i": ra[c]["XS_o"], "QT_i": ra[c]["QT_o"], "KT_i": ra[c]["KT_o"], "VT_i": ra[c]["VT_o"],
                  "HR": hr[c]})
        ins.append(m)
    rb = run_bass_kernel_spmd(_get_nc(TOK, "B", n_cores), ins, core_ids=cores).results
    if debug is not None:
        debug["B"] = rb
    hr = halos(rb)
    ins = []
    for c in range(n_cores):
        m = dict(base[c])
        m.update({"XS_i": rb[c]["XS_o"], "QT_i": rb[c]["QT_o"], "KT_i": rb[c]["KT_o"], "VT_i": rb[c]["VT_o"],
                  "HR": hr[c]})
        ins.append(m)
    rc = run_bass_kernel_spmd(_get_nc(TOK, "C", n_cores), ins, core_ids=cores).results
    if debug is not None:
        debug["C"] = rc
    out = np.empty((B, S, D), np.float32)
    for c in range(n_cores):
        b, hf = divmod(c, halves)
        out[b, hf * TOK:(hf + 1) * TOK, :] = rc[c]["out"]
    return out
```

```python
import math
import numpy as np
import ml_dtypes
import concourse.bass as bass
import concourse.mybir as mybir
from concourse.bass_utils import run_bass_kernel_spmd
from contextlib import ExitStack

F32 = mybir.dt.float32
BF16 = mybir.dt.bfloat16
AF = mybir.ActivationFunctionType
ALU = mybir.AluOpType

D = 1024
NCH = 8
TT = 512
FF = 2816
JH = 22
NL = 2
EPS = 1e-6
N_CORES = 8
DBG_SKIP = set()

COMPUTE = ("pe", "act", "dve", "pool")
STREAM_OF = {"pe": "pe", "act": "act", "dve": "dve", "pool": "pool",
             "sp": "sp", "actq": "act", "poolq": "pool"}


class Res:
    __slots__ = ("name", "w", "r", "dsem", "dcnt")

    def __init__(self, name):
        self.name = name
        self.w = []
        self.r = []
        self.dsem = None
        self.dcnt = 0


class Prog:
    def __init__(self, nc, es, n_dma_sems=92):
        self.nc = nc
        self.ops = {s: [] for s in ("pe", "act", "dve", "pool", "sp")}
        self.sem = {e: es.enter_context(nc.semaphore("sem_" + e)) for e in COMPUTE}
        self.free_dma = [es.enter_context(nc.semaphore(f"dsem{i}")) for i in range(n_dma_sems)]
        self.cnt = {e: 0 for e in COMPUTE}
        self.seen = {s: {} for s in self.ops}
        self.dma_res = []

    def res(self, name):
        return Res(name)

    def _dsem(self, r):
        if r.dsem is None:
            r.dsem = self.free_dma.pop()
            self.dma_res.append(r)
        return r.dsem

    def _waits(self, stream, evs, own=None):
        need = {}
        for ev in evs:
            k, v = ev
            if need.get(id(k), (None, -1))[1] < v:
                need[id(k)] = (k, v)
        seen = self.seen[stream]
        for kid, (k, v) in need.items():
            if own is not None and k is own and stream == "pe":
                continue
            if seen.get(kid, -1) >= v:
                continue
            seen[kid] = v
            self.ops[stream].append(("wait", k, v))

    def _deps(self, reads, writes, accumulate):
        evs = []
        for r in reads:
            evs.extend(r.w)
        for w in writes:
            if not accumulate:
                evs.extend(w.w)
            evs.extend(w.r)
        return evs

    def _record(self, ev, reads, writes, accumulate):
        for r in reads:
            r.r.append(ev)
        for w in writes:
            if accumulate:
                w.w.append(ev)
            else:
                w.w = [ev]
            w.r = []

    def op(self, eng, fn, reads=(), writes=()):
        stream = STREAM_OF[eng]
        sem = self.sem[eng]
        self._waits(stream, self._deps(reads, writes, False), own=sem)
        self.cnt[eng] += 1
        ev = (sem, self.cnt[eng])
        self.ops[stream].append(("inst", fn, sem, 1))
        self._record(ev, reads, writes, False)

    def mm(self, fns, reads=(), writes=()):
        sem = self.sem["pe"]
        self._waits("pe", self._deps(reads, writes, False), own=sem)
        for fn in fns[:-1]:
            self.ops["pe"].append(("inst", fn, None, 0))
        self.cnt["pe"] += 1
        ev = (sem, self.cnt["pe"])
        self.ops["pe"].append(("inst", fns[-1], sem, 1))
        self._record(ev, reads, writes, False)

    def dma(self, queue, fn, sb, reads=(), writes=(), accumulate=False, inc=16):
        stream = STREAM_OF[queue]
        self._waits(stream, self._deps(reads, writes, accumulate))
        k = self._dsem(sb)
        sb.dcnt += inc
        ev = (k, sb.dcnt)
        self.ops[stream].append(("inst", fn, k, inc))
        self._record(ev, reads, writes, accumulate)
        return ev

    def all_events(self, stream):
        evs = [(self.sem[e], self.cnt[e]) for e in COMPUTE if self.cnt[e] > 0]
        evs += [(r.dsem, r.dcnt) for r in self.dma_res
                if r.dcnt > 0 and (stream == "pool" or not r.name.startswith("cc"))]
        return evs

    def barrier(self, streams=("pe", "act", "dve", "pool", "sp")):
        for s in streams:
            self._waits(s, self.all_events(s))

    def emit(self, block):
        ops = self.ops

        def run(eng_obj, lst):
            for o in lst:
                if o[0] == "wait":
                    eng_obj.wait_ge(o[1], o[2])
                else:
                    ins = o[1](eng_obj)
                    if o[2] is not None:
                        ins.then_inc(o[2], o[3])

        @block.tensor
        def _(e):
            run(e, ops["pe"])

        @block.scalar
        def _(e):
            run(e, ops["act"])

        @block.vector
        def _(e):
            run(e, ops["dve"])

        @block.gpsimd
        def _(e):
            run(e, ops["pool"])

        @block.sync
        def _(e):
            run(e, ops["sp"])


class Arena:
    def __init__(self, ap_f32):
        self.ap = ap_f32
        self.n = ap_f32.shape[1]
        self.off = 0

    def reset(self):
        self.off = 0

    def alloc(self, free_shape, dtype):
        n = int(np.prod(free_shape))
        if dtype == BF16:
            assert n % 2 == 0
            n32 = n // 2
        else:
            n32 = n
        n32a = (n32 + 7) // 8 * 8
        assert self.off + n32a <= self.n, f"arena overflow {self.off}+{n32a}>{self.n}"
        v = self.ap[:, self.off:self.off + n32]
        self.off += n32a
        if dtype == BF16:
            v = v.bitcast(BF16)
        if len(free_shape) == 2:
            v = v.rearrange("p (a b) -> p a b", a=free_shape[0])
        elif len(free_shape) == 3:
            v = v.rearrange("p (a b c) -> p a b c", a=free_shape[0], b=free_shape[1])
        return v


QA_HEAD_ORDER = [0, 3, 1, 4, 2, 5]
SELF_CHUNKS = [
    (0, 0, 0, 1, 127, (0, 3), True),
    (1, 0, 0, 1, 127, (1, 4), True),
    (2, 0, 0, 1, 127, (2, 5), True),
    (3, 1, 1, 1, 128, (6, 7), False),
    (4, 2, 2, 4, 128, (8, 9), False),
    (5, 3, 3, 16, 128, (10, 11), False),
]
K_RATES = [1, 1, 4, 16]
Q_RATES = [1, 1, 1, 1, 4, 16, 1, 1]


def _t5_bucket(dist):
    dist = np.asarray(dist)
    me = 16
    dd = np.maximum(dist, 1).astype(np.float32)
    large = me + (np.log(dd / np.float32(me)) / np.float32(math.log(2048 / me))
                  * np.float32(32 - me)).astype(np.int32)
    large = np.minimum(large, 31)
    return np.where(dist < me, dist, large)


def _win_perm():
    qa = np.concatenate([np.arange(64 * h, 64 * h + 64) for h in QA_HEAD_ORDER])
    ka = np.arange(384, 512)
    va = np.arange(512, 640)
    qb = np.arange(640, 1024)
    kb = np.arange(1024, 1408)
    vb = np.arange(1408, 1792)
    qc = np.arange(1792, 2048)
    return np.concatenate([qa, qb, qc, ka, kb, va, vb])


def _static_bias_tables(rel_bias):
    kk = np.arange(128)[:, None]
    qq = np.arange(128)[None, :]
    bias = np.zeros((6, 128, 2, 2, 128), np.float32)
    mask = np.zeros((6, 128, 2, 2, 128), np.float32)
    for sc, (_, _, _, r, md, cols, _) in enumerate(SELF_CHUNKS):
        d_prev = qq + 128 - kk
        d_cur = qq - kk
        v_prev = (d_prev <= md)
        v_cur = (d_cur >= 0)
        for hi, col in enumerate(cols):
            tab = rel_bias[:, col]
            bias[sc, :, hi, 0, :] = np.where(v_prev, tab[_t5_bucket(d_prev * r)], 0.0)
            bias[sc, :, hi, 1, :] = np.where(v_cur, tab[_t5_bucket(np.maximum(d_cur, 0) * r)], 0.0)
            mask[sc, :, hi, 0, :] = v_prev
            mask[sc, :, hi, 1, :] = v_cur
    return bias.reshape(6, 128, 512), mask.reshape(6, 128, 512)


def _chunk_cols(v):
    v = np.asarray(v, np.float32)
    lead = v.shape[:-1]
    return np.ascontiguousarray(v.reshape(-1, NCH, 128).transpose(2, 0, 1).reshape(128, -1)), lead


def prep_shared(inp):
    sh = {}
    perm = _win_perm()
    sh["w_ffn1_in"] = np.ascontiguousarray(inp["w_ffn1_in"], np.float32)
    sh["w_ffn1_out"] = np.ascontiguousarray(inp["w_ffn1_out"], np.float32)
    sh["w_ffn2_in"] = np.ascontiguousarray(inp["w_ffn2_in"], np.float32)
    sh["w_ffn2_out"] = np.ascontiguousarray(inp["w_ffn2_out"], np.float32)
    sh["w_in"] = np.ascontiguousarray(np.asarray(inp["w_in"], np.float32)[:, :, perm])
    sh["w_gate"] = np.ascontiguousarray(inp["w_gate"], np.float32)
    rows_a = np.concatenate([np.arange(64 * h, 64 * h + 64) for h in QA_HEAD_ORDER])
    wbr = np.concatenate([np.asarray(inp["w_br_a"], np.float32)[:, rows_a, :],
                          np.asarray(inp["w_br_b"], np.float32),
                          np.asarray(inp["w_br_c"], np.float32)], axis=1)
    sh["w_br"] = np.ascontiguousarray(wbr)
    sh["w_o"] = np.ascontiguousarray(inp["w_o"], np.float32)
    sh["w_mem_kv"] = np.ascontiguousarray(inp["w_mem_kv"], np.float32)
    g, _ = _chunk_cols(inp["norm_gain"])
    sh["gains"] = g
    mg, _ = _chunk_cols(inp["mem_norm_gain"])
    sh["mgain"] = mg
    bg, _ = _chunk_cols(inp["b_gate"])
    sh["bgate"] = bg
    sinks = np.asarray(inp["sinks"], np.float32)
    sk = np.zeros((128, NL * 3), np.float32)
    for l in range(NL):
        for ci in range(3):
            sk[0:64, l * 3 + ci] = sinks[l, QA_HEAD_ORDER[2 * ci]]
            sk[64:128, l * 3 + ci] = sinks[l, QA_HEAD_ORDER[2 * ci + 1]]
    sh["sinks"] = sk
    bias, mask = _static_bias_tables(np.asarray(inp["rel_bias"], np.float32))
    sh["biasT"] = np.ascontiguousarray(bias.transpose(1, 0, 2))
    sh["maskT"] = np.ascontiguousarray(mask.transpose(1, 0, 2))
    sh["ident"] = np.eye(128, dtype=np.float32)
    return sh


def build(TOK, mode="fused", n_cores=N_CORES):
    NT = TOK // TT
    fused = mode == "fused"
    nc = bass.Bass("TRN2", target_bir_lowering=False)

    def din(name, shape, dt=F32):
        return nc.dram_tensor(name, list(shape), dt, kind="ExternalInput").ap()

    def dout(name, shape, dt=F32):
        return nc.dram_tensor(name, list(shape), dt, kind="ExternalOutput").ap()

    def dint(name, shape, dt=F32):
        return nc.dram_tensor(name, list(shape), dt).ap()

    def dscr(name, shape, dt, is_in, is_out):
        if fused:
            return dint(name, shape, dt), None
        i = din(name + "_i", shape, dt) if is_in else None
        o = dout(name + "_o", shape, dt) if is_out else None
        return i, o

    first = mode in ("fused", "A")
    last = mode in ("fused", "C")
    layers_s1 = {"fused": [0, 1], "A": [0], "B": [1], "C": []}[mode]
    layers_s2 = {"fused": [0, 1], "A": [], "B": [0], "C": [1]}[mode]

    x_in = din("x", [TOK, D]) if first else None
    mem_in = din("mem", [256, D])
    hv_in = din("halo_valid", [128, 1])
    w1i = [din("w_ffn1_in", [NL, D, 2 * FF]), din("w_ffn2_in", [NL, D, 2 * FF])]
    w1o = [din("w_ffn1_out", [NL, FF, D]), din("w_ffn2_out", [NL, FF, D])]
    w_in = din("w_in", [NL, D, 2048])
    w_gate = din("w_gate", [NL, D, 3 * D])
    w_br = din("w_br", [NL, D, D])
    w_o = din("w_o", [NL, D, D])
    w_mkv = din("w_mem_kv", [NL, D, 512])
    gains_in = din("gains", [128, NL * 6 * NCH])
    mgain_in = din("mgain", [128, NL * NCH])
    bgate_in = din("bgate", [128, NL * 3 * NCH])
    sinks_in = din("sinks", [128, NL * 3])
    biasT_in = din("biasT", [128, 6, 512])
    maskT_in = din("maskT", [128, 6, 512])
    ident_in = din("ident", [128, 128])
    out_ap = dout("out", [TOK, D]) if last else None

    HROWS = 2 * 2816 * 128 // 1024
    if fused:
        XS_r = XS_w = dint("XS", [NT, 128, NCH * TT], F32)
        QT_r = QT_w = dint("QT", [128, 8, TOK], BF16)
        KT_r = KT_w = dint("KT", [128, 4, TOK], BF16)
        VT_r = VT_w = dint("VT", [4, TOK, 128], BF16)
        HS = dint("HS", [HROWS, 1024], BF16)
        HR = dint("HR", [2 * HROWS, 1024], BF16)
    else:
        XS_r = din("XS_i", [NT, 128, NCH * TT], F32) if mode in ("B", "C") else None
        XS_w = dout("XS_o", [NT, 128, NCH * TT], F32) if mode in ("A", "B") else None
        QT_r = din("QT_i", [128, 8, TOK], BF16) if mode in ("B", "C") else None
        KT_r = din("KT_i", [128, 4, TOK], BF16) if mode in ("B", "C") else None
        VT_r = din("VT_i", [4, TOK, 128], BF16) if mode in ("B", "C") else None
        QT_w = dout("QT_o", [128, 8, TOK], BF16) if mode in ("A", "B") else None
        KT_w = dout("KT_o", [128, 4, TOK], BF16) if mode in ("A", "B") else None
        VT_w = dout("VT_o", [4, TOK, 128], BF16) if mode in ("A", "B") else None
        HS = None
        HR = din("HR", [2 * HROWS, 1024], BF16) if mode in ("B", "C") else None
    OT = dint("OT", [128, 8, TOK], BF16) if fused or not layers_s2 else dout("OT_dbg", [128, 8, TOK], BF16)
    W1 = [[dint(f"W1_{l}_{f}", [11, 128, 4096], BF16) for f in range(2)] for l in range(NL)]
    W2 = [[dint(f"W2_{l}_{f}", [8, 128, JH * 128], BF16) for f in range(2)] for l in range(NL)]
    WIN = [dint(f"WIN_{l}", [4, 128, 4096], BF16) for l in range(NL)]
    WG = [dint(f"WG_{l}", [6, 128, 4096], BF16) for l in range(NL)]
    WBR = [dint(f"WBR_{l}", [2, 128, 4096], BF16) for l in range(NL)]
    WO = [dint(f"WO_{l}", [2, 128, 4096], BF16) for l in range(NL)]
    WM = [dint(f"WM_{l}", [128, 4096], BF16) for l in range(NL)]

    es = ExitStack()
    with es:
        P = Prog(nc, es)

        def sb(name, shape, dt):
            return es.enter_context(nc.sbuf_tensor("sb_" + name, list(shape), dt))

        ident = sb("ident", [128, 128], F32)
        ones = sb("ones", [128, 128], BF16)
        gains = sb("gains", [128, NL * 6 * NCH], F32)
        mgain = sb("mgain", [128, NL * NCH], F32)
        bgate = sb("bgate", [128, NL * 3 * NCH], F32)
        sinke = sb("sinke", [128, NL * 3], F32)
        hval = sb("hval", [128, 1], F32)
        ccdummy = sb("ccdummy", [128, 8], F32)
        epsb = sb("epsb", [128, 2], F32)
        Emat = sb("Emat", [128, 6, 512], BF16)
        Efirst = sb("Efirst", [128, 6, 512], BF16)
        KC = sb("KC", [128, NL, 2, 2, 256], BF16)
        VC = sb("VC", [128, NL, 2, 256], BF16)
        psum = [es.enter_context(nc.psum_tensor(f"ps{i}", [128, 512], F32)) for i in range(8)]
        r_psum = [P.res(f"ps{i}") for i in range(8)]
        ps_i = [0]
        ARENA_W = 45200
        arena_t = sb("arena", [128, ARENA_W], F32)
        arena = Arena(arena_t[:, :])
        r_const = P.res("const")

        def next_ps():
            i = ps_i[0] % 7
            ps_i[0] += 1
            return psum[i], r_psum[i]

        def stat_ps():
            return psum[7], r_psum[7]

        def load_const(dst, src, nm):
            r = P.res(nm)
            P.dma("sp", lambda e, d=dst, s=src: e.dma_start(out=d, in_=s), r, writes=[r])
            return r

        r_ident = load_const(ident[:, :], ident_in[:, :], "ident")
        r_gains = load_const(gains[:, :], gains_in[:, :], "gains")
        r_mgain = load_const(mgain[:, :], mgain_in[:, :], "mgain")
        r_bgate = load_const(bgate[:, :], bgate_in[:, :], "bgate")
        r_sink = load_const(sinke[:, :], sinks_in[:, :], "sinks")
        r_hval = load_const(hval[:, :], hv_in[:, :], "hval")
        r_ones = P.res("ones")
        P.op("pool", lambda e: e.memset(epsb[:, 0:1], EPS), writes=[r_ones])
        P.op("pool", lambda e: e.memset(epsb[:, 1:2], EPS / 0.25), writes=[r_ones])
        P.op("pool", lambda e: e.memset(ones[:, :], 1.0), writes=[r_ones])
        P.op("pool", lambda e: e.memset(KC[:, :, :, :, :].rearrange("p a b c d -> p (a b c d)"), 0.0), writes=[r_const])
        P.op("act", lambda e: e.activation(out=sinke[:, :], in_=sinke[:, :], func=AF.Exp),
             reads=[r_sink], writes=[r_sink])
        r_E = P.res("E")
        if layers_s2:
            arena.reset()
            bt = arena.alloc([6, 512], F32)
            mt = arena.alloc([6, 512], F32)
            r_bt, r_mt = P.res("bt"), P.res("mt")
            P.dma("sp", lambda e: e.dma_start(out=bt, in_=biasT_in[:, :, :]), r_bt, writes=[r_bt])
            P.dma("sp", lambda e: e.dma_start(out=mt, in_=maskT_in[:, :, :]), r_mt, writes=[r_mt])
            P.op("act", lambda e: e.activation(out=bt, in_=bt, func=AF.Exp), reads=[r_bt], writes=[r_bt])
            P.op("dve", lambda e: e.tensor_tensor(out=Emat[:, :, :], in0=bt, in1=mt, op=ALU.mult),
                 reads=[r_bt, r_mt], writes=[r_E])
            P.op("dve", lambda e: e.tensor_copy(out=Efirst[:, :, :], in_=Emat[:, :, :]),
                 reads=[r_E], writes=[r_E])
            for hi in range(2):
                P.op("dve", lambda e, hi=hi: e.tensor_scalar(
                    out=Efirst[:, :, hi * 256:hi * 256 + 128], in0=Emat[:, :, hi * 256:hi * 256 + 128],
                    scalar1=hval[:, 0:1], scalar2=None, op0=ALU.mult),
                    reads=[r_E, r_hval], writes=[r_E])

        P.barrier()

        r_W1 = [[P.res(f"W1{l}{f}") for f in range(2)] for l in range(NL)]
        r_W2 = [[P.res(f"W2{l}{f}") for f in range(2)] for l in range(NL)]
        r_WIN = [P.res(f"WIN{l}") for l in range(NL)]
        r_WG = [P.res(f"WG{l}") for l in range(NL)]
        r_WBR = [P.res(f"WBR{l}") for l in range(NL)]
        r_WO = [P.res(f"WO{l}") for l in range(NL)]
        r_WM = [P.res(f"WM{l}") for l in range(NL)]

        cast_tasks = []

        def cast(dst, src, r):
            cast_tasks.append((dst, src, r))

        def emit_casts(n=None):
            k = len(cast_tasks) if n is None else min(n, len(cast_tasks))
            for _ in range(k):
                dst, src, r = cast_tasks.pop(0)
                P.dma("poolq", lambda e, d=dst, s=src: e.dma_start(out=d, in_=s), r, writes=[r], accumulate=True)

        def cast_w1(l, f):
            src = w1i[f][l].rearrange("(c p) n -> p c n", p=128)
            for blk in range(11):
                dst = W1[l][f][blk].rearrange("p (c ab n) -> p c ab n", c=8, ab=2)
                for ab in range(2):
                    c0 = ab * FF + blk * 256
                    cast(dst[:, :, ab, :], src[:, :, c0:c0 + 256], r_W1[l][f])

        def cast_w2(l, f):
            src = w1o[f][l].rearrange("(j p) n -> p j n", p=128)
            for m in range(8):
                dst = W2[l][f][m].rearrange("p (j n) -> p j n", j=JH)
                cast(dst, src[:, :, m * 128:(m + 1) * 128], r_W2[l][f])

        def cast_cols(dst_blocks, src2d, nblk, width, r):
            src = src2d.rearrange("(c p) n -> p c n", p=128)
            for b in range(nblk):
                dst = dst_blocks[b].rearrange("p (c n) -> p c n", c=8)
                cast(dst, src[:, :, b * width:(b + 1) * width], r)

        need_s1 = set(layers_s1)
        need_s3 = set(layers_s2)
        for l in range(NL):
            cast(WM[l].rearrange("p (c n) -> p c n", c=8),
                 w_mkv[l].rearrange("(c p) n -> p c n", p=128), r_WM[l])
        for l in range(NL):
            if l in need_s1:
                cast_w1(l, 0)
                cast_w2(l, 0)
                cast_cols(WIN[l], w_in[l], 4, 512, r_WIN[l])
                if l == 0 and fused:
                    n_first = len(cast_tasks)
            if l in need_s3:
                cast_cols(WG[l], w_gate[l], 6, 512, r_WG[l])
                cast_cols(WBR[l], w_br[l], 2, 512, r_WBR[l])
                cast_cols(WO[l], w_o[l], 2, 512, r_WO[l])
                cast_w1(l, 1)
                cast_w2(l, 1)
        if fused:
            emit_casts(n_first)
        else:
            emit_casts()

        class WS:
            pass

        def alloc_token_stage():
            arena.reset()
            w = WS()
            w.X = [arena.alloc([NCH, TT], F32) for _ in range(2)]
            w.rX = [P.res("X0"), P.res("X1")]
            w.Y = arena.alloc([NCH, TT], F32)
            w.rY = P.res("Y")
            w.H = arena.alloc([NCH, TT], BF16)
            w.rH = P.res("H")
            w.XG = arena.alloc([NCH, TT], BF16)
            w.rXG = P.res("XG")
            w.G = arena.alloc([JH, TT], BF16)
            w.rG = [P.res(f"G{j}") for j in range(JH)]
            w.slots = [arena.alloc([4096], BF16) for _ in range(5)]
            w.rS = [P.res(f"slot{i}") for i in range(5)]
            w.si = 0
            w.rs = [arena.alloc([TT], F32) for _ in range(2)]
            w.r_rs = [P.res("rs0"), P.res("rs1")]
            w.rsi = 0
            w.tmp = [arena.alloc([TT], F32) for _ in range(3)]
            w.r_tmp = [P.res(f"tmp{i}") for i in range(3)]
            w.tmi = 0
            w.sil = [arena.alloc([TT], BF16) for _ in range(3)]
            w.r_sil = [P.res(f"sil{i}") for i in range(3)]
            w.sli = 0
            w.gt = [arena.alloc([TT], BF16) for _ in range(3)]
            w.r_gt = [P.res(f"gt{i}") for i in range(3)]
            w.M = arena.alloc([NCH, TT], BF16)
            w.rM = P.res("M")
            w.O = arena.alloc([NCH, TT], BF16)
            w.rO = P.res("Osb")
            w.QK = arena.alloc([12, TT], BF16)
            w.rQK = [P.res(f"QK{i}") for i in range(12)]
            w.V = arena.alloc([4, 512], BF16)
            w.rV = P.res("Vst")
            return w

        def slot_load(w, src_ap, r_src):
            i = w.si % len(w.slots)
            w.si += 1
            s, r = w.slots[i], w.rS[i]
            n = src_ap.shape[1]
            P.dma("sp", lambda e, s=s, a=src_ap: e.dma_start(out=s[:, 0:n], in_=a), r, reads=[r_src], writes=[r])
            return s, r

        def stat_chunk(w, c, src, r_src):
            ps, rps = stat_ps()
            P.op("act", lambda e: e.activation(out=w.H[:, c, :], in_=src, func=AF.Square),
                 reads=[r_src], writes=[w.rH])
            P.mm([lambda e: e.matmul(ps[:, :], lhsT=ones[:, :], rhs=w.H[:, c, :],
                                     start=(c == 0), stop=(c == NCH - 1))],
                 reads=[w.rH, r_ones], writes=[rps])

        def rstd_from_stats(w, alpha=1.0):
            ps, rps = stat_ps()
            i = w.rsi % 2
            w.rsi += 1
            rs, r_rs = w.rs[i], w.r_rs[i]
            a2 = float(alpha) ** 2
            P.op("act", lambda e: e.activation(out=rs, in_=ps[:, :], func=AF.Ln, bias=epsb[:, 0:1] if a2 == 1.0 else epsb[:, 1:2],
                                               scale=1.0 / (D * a2)), reads=[rps, r_ones], writes=[r_rs])
            P.op("act", lambda e: e.activation(out=rs, in_=rs, func=AF.Exp, scale=-0.5), reads=[r_rs], writes=[r_rs])
            return rs, r_rs

        def h_from(w, X, rX, gcol0, rs, r_rs, gtile=None, r_g=None):
            gtile = gains if gtile is None else gtile
            r_g = r_gains if r_g is None else r_g
            for c in range(NCH):
                P.op("dve", lambda e, c=c: e.scalar_tensor_tensor(
                    out=w.H[:, c, :], in0=X[:, c, :], scalar=gtile[:, gcol0 + c:gcol0 + c + 1], in1=rs,
                    op0=ALU.mult, op1=ALU.mult), reads=[rX, r_rs, r_g], writes=[w.rH])

        def prenorm(w, X, rX, gcol0, gtile=None, r_g=None):
            for c in range(NCH):
                stat_chunk(w, c, X[:, c, :], rX)
            rs, r_rs = rstd_from_stats(w)
            h_from(w, X, rX, gcol0, rs, r_rs, gtile, r_g)

        POOL_A = (1, 3, 5)
        POOL_B = (2, 5)

        def postnorm_add(w, X, rX, gcol0, alpha, then_pre=None):
            rs, r_rs = rstd_from_stats(w, alpha)
            for c in range(NCH):
                i = w.tmi % 3
                w.tmi += 1
                t, rt = w.tmp[i], w.r_tmp[i]
                P.op("pool" if c in POOL_A else "dve", lambda e, c=c, t=t: e.tensor_tensor(
                    out=t, in0=w.Y[:, c, :], in1=rs, op=ALU.mult), reads=[w.rY, r_rs], writes=[rt])
                P.op("dve", lambda e, c=c, t=t: e.tensor_tensor(
                    out=X[:, c, :], in0=X[:, c, :], in1=t, op=ALU.add), reads=[rt, rX], writes=[rX])
                if then_pre is not None:
                    stat_chunk(w, c, X[:, c, :], rX)
                    P.op("act", lambda e, c=c: e.mul(out=w.XG[:, c, :], in_=X[:, c, :],
                                                     mul=gains[:, then_pre + c:then_pre + c + 1]),
                         reads=[rX, r_gains], writes=[w.rXG])
            if then_pre is not None:
                rs2, r_rs2 = rstd_from_stats(w)
                for c in range(NCH):
                    P.op("pool" if c in POOL_B else "dve", lambda e, c=c: e.tensor_tensor(
                        out=w.H[:, c, :], in0=w.XG[:, c, :], in1=rs2, op=ALU.mult),
                        reads=[w.rXG, r_rs2], writes=[w.rH])

        def ffn(w, l, f):
            ycol = (l * 6 + (1 if f == 0 else 5)) * NCH
            for blk in range(11):
                s, rs_ = slot_load(w, W1[l][f][blk], r_W1[l][f])
                sv = s.rearrange("p (c ab n) -> p c ab n", c=8, ab=2)
                for jj in range(2):
                    j = blk * 2 + jj
                    pa, rpa = next_ps()
                    P.mm([lambda e, c=c, pa=pa, jj=jj, sv=sv: e.matmul(
                        pa[:, :], lhsT=sv[:, c, 0, jj * 128:(jj + 1) * 128], rhs=w.H[:, c, :],
                        start=(c == 0), stop=(c == NCH - 1)) for c in range(NCH)],
                        reads=[rs_, w.rH], writes=[rpa])
                    pb, rpb = next_ps()
                    P.mm([lambda e, c=c, pb=pb, jj=jj, sv=sv: e.matmul(
                        pb[:, :], lhsT=sv[:, c, 1, jj * 128:(jj + 1) * 128], rhs=w.H[:, c, :],
                        start=(c == 0), stop=(c == NCH - 1)) for c in range(NCH)],
                        reads=[rs_, w.rH], writes=[rpb])
                    i = w.sli % 3
                    w.sli += 1
                    sl, rsl = w.sil[i], w.r_sil[i]
                    P.op("act", lambda e, sl=sl, pa=pa: e.activation(out=sl, in_=pa[:, :], func=AF.Silu),
                         reads=[rpa], writes=[rsl])
                    P.op("dve", lambda e, sl=sl, pb=pb, j=j: e.tensor_tensor(
                        out=w.G[:, j, :], in0=sl, in1=pb[:, :], op=ALU.mult),
                        reads=[rsl, rpb], writes=[w.rG[j]])
            for m in range(8):
                s, rs_ = slot_load(w, W2[l][f][m][:, :], r_W2[l][f])
                sv = s[:, 0:JH * 128].rearrange("p (j n) -> p j n", j=JH)
                py, rpy = next_ps()
                P.mm([lambda e, j=j, py=py, sv=sv: e.matmul(
                    py[:, :], lhsT=sv[:, j, :], rhs=w.G[:, j, :], start=(j == 0), stop=(j == JH - 1))
                    for j in range(JH)], reads=[rs_] + w.rG, writes=[rpy])
                P.op("act", lambda e, m=m, py=py: e.mul(out=w.Y[:, m, :], in_=py[:, :],
                                                          mul=gains[:, ycol + m:ycol + m + 1]),
                     reads=[rpy, r_gains], writes=[w.rY])
                stat_chunk(w, m, py[:, :], rpy)

        def in_proj(w, l, t):
            ev = 0
            for blk in range(3):
                s, rs_ = slot_load(w, WIN[l][blk], r_WIN[l])
                sv = s.rearrange("p (c n) -> p c n", c=8)
                for cc in range(4):
                    qi = blk * 4 + cc
                    rate = Q_RATES[qi] if qi < 8 else K_RATES[qi - 8]
                    ps, rps = next_ps()
                    P.mm([lambda e, c=c, ps=ps, cc=cc, sv=sv: e.matmul(
                        ps[:, :], lhsT=sv[:, c, cc * 128:(cc + 1) * 128], rhs=w.H[:, c, :],
                        start=(c == 0), stop=(c == NCH - 1)) for c in range(NCH)],
                        reads=[rs_, w.rH], writes=[rps])
                    if rate == 1:
                        dst, src = w.QK[:, qi, :], ps[:, :]
                    else:
                        dst = w.QK[:, qi, :].rearrange("p (j i) -> p i j", j=rate)
                        src = ps[:, :].rearrange("p (i j) -> p i j", j=rate)
                    if ev % 2 == 0:
                        P.op("act", lambda e, dst=dst, src=src: e.copy(out=dst, in_=src),
                             reads=[rps], writes=[w.rQK[qi]])
                    else:
                        P.op("dve", lambda e, dst=dst, src=src: e.tensor_copy(out=dst, in_=src),
                             reads=[rps], writes=[w.rQK[qi]])
                    ev += 1
            def st(dst, src, rr, dres):
                P.dma("poolq", lambda e, d=dst, s=src: e.dma_start(out=d, in_=s), rr[0],
                      reads=rr, writes=[dres], accumulate=True)

            tsl = slice(t * TT, (t + 1) * TT)
            st(QT_w[:, 0:4, tsl], w.QK[:, 0:4, :], w.rQK[0:4], r_QT)
            for qi in (4, 5):
                r = Q_RATES[qi]
                st(QT_w[:, qi, :].rearrange("p (j n) -> p j n", j=r)[:, :, t * (TT // r):(t + 1) * (TT // r)],
                   w.QK[:, qi, :].rearrange("p (j i) -> p j i", j=r), [w.rQK[qi]], r_QT)
            st(QT_w[:, 6:8, tsl], w.QK[:, 6:8, :], w.rQK[6:8], r_QT)
            st(KT_w[:, 0:2, tsl], w.QK[:, 8:10, :], w.rQK[8:10], r_KT)
            for ki in (2, 3):
                r = K_RATES[ki]
                st(KT_w[:, ki, :].rearrange("p (j n) -> p j n", j=r)[:, :, t * (TT // r):(t + 1) * (TT // r)],
                   w.QK[:, 8 + ki, :].rearrange("p (j i) -> p j i", j=r), [w.rQK[8 + ki]], r_KT)
            s, rs_ = slot_load(w, WIN[l][3], r_WIN[l])
            sv = s.rearrange("p (c n) -> p c n", c=8)
            for tb in range(4):
                ps, rps = next_ps()
                P.mm([lambda e, c=c, ps=ps, tb=tb, sv=sv: e.matmul(
                    ps[:, :], lhsT=w.H[:, c, tb * 128:(tb + 1) * 128], rhs=sv[:, c, :],
                    start=(c == 0), stop=(c == NCH - 1)) for c in range(NCH)],
                    reads=[rs_, w.rH], writes=[rps])
                P.op("dve", lambda e, tb=tb, ps=ps: e.tensor_copy(out=w.V[:, tb, :], in_=ps[:, :]),
                     reads=[rps], writes=[w.rV])
            for g in range(4):
                r = K_RATES[g]
                src = w.V[:, :, g * 128:(g + 1) * 128]
                if r == 1:
                    dst = VT_w[g, t * TT:(t + 1) * TT, :].rearrange("(tb p) d -> p tb d", p=128)
                    st(dst, src, [w.rV], r_VT)
                else:
                    npr = TOK // r
                    for j in range(r):
                        dst = VT_w[g, j * npr + t * (TT // r):j * npr + (t + 1) * (TT // r), :] \
                            .rearrange("(tb i) d -> i tb d", tb=4)
                        st(dst, w.V[j:128:r, :, g * 128:(g + 1) * 128], [w.rV], r_VT)

        r_QT, r_KT, r_VT, r_OT = P.res("QT"), P.res("KT"), P.res("VT"), P.res("OT")
        r_XS = [P.res(f"XS{t}") for t in range(NT)]
        r_HS, r_HR = P.res("HS"), P.res("HR")
        r_out = P.res("out")

        def load_x_tokmajor(w, t, X, rX):
            xin = w.Y
            xv = xin.rearrange("p c t -> p (c t)").rearrange("p (tb f) -> p tb f", tb=4)
            P.dma("sp", lambda e: e.dma_start(
                out=xv, in_=x_in[t * TT:(t + 1) * TT, :].rearrange("(tb p) f -> p tb f", p=128)),
                w.rY, writes=[w.rY])
            for c in range(NCH):
                ps, rps = next_ps()
                P.mm([lambda e, tb=tb, ps=ps, c=c: e.transpose(
                    ps[:, tb * 128:(tb + 1) * 128], xv[:, tb, c * 128:(c + 1) * 128], ident[:, :])
                    for tb in range(4)], reads=[w.rY, r_ident], writes=[rps])
                if c % 2 == 0:
                    P.op("act", lambda e, c=c, ps=ps: e.copy(out=X[:, c, :], in_=ps[:, :]), reads=[rps], writes=[rX])
                else:
                    P.op("dve", lambda e, c=c, ps=ps: e.tensor_copy(out=X[:, c, :], in_=ps[:, :]),
                         reads=[rps], writes=[rX])

        def store_out_tokmajor(w, t, X, rX):
            xo = w.Y
            xv = xo.rearrange("p c t -> p (c t)").rearrange("p (tb f) -> p tb f", tb=4)
            for tb in range(4):
                for hf in range(2):
                    ps, rps = next_ps()
                    P.mm([lambda e, cc=cc, ps=ps, tb=tb, hf=hf: e.transpose(
                        ps[:, cc * 128:(cc + 1) * 128], X[:, hf * 4 + cc, tb * 128:(tb + 1) * 128], ident[:, :])
                        for cc in range(4)], reads=[rX, r_ident], writes=[rps])
                    if hf == 0:
                        P.op("act", lambda e, ps=ps, tb=tb: e.copy(out=xv[:, tb, 0:512], in_=ps[:, :]),
                             reads=[rps], writes=[w.rY])
                    else:
                        P.op("dve", lambda e, ps=ps, tb=tb: e.tensor_copy(out=xv[:, tb, 512:1024], in_=ps[:, :]),
                             reads=[rps], writes=[w.rY])
            P.dma("poolq", lambda e: e.dma_start(
                out=out_ap[t * TT:(t + 1) * TT, :].rearrange("(tb p) f -> p tb f", p=128), in_=xv),
                w.rY, reads=[w.rY], writes=[r_out], accumulate=True)

        def s1_body(w, l, t, X, rX, pre_done=False):
            gb = (l * 6) * NCH
            if not pre_done:
                prenorm(w, X, rX, gb + 0 * NCH)
            ffn(w, l, 0)
            postnorm_add(w, X, rX, gb + 1 * NCH, 0.5, then_pre=gb + 2 * NCH)
            P.dma("poolq", lambda e: e.dma_start(out=XS_w[t], in_=X.rearrange("p c t -> p (c t)")),
                  rX, reads=[rX], writes=[r_XS[t]])
            in_proj(w, l, t)

        def mixing(w, l, t, X, rX):
            ycol = (l * 6 + 3) * NCH
            s_o = None
            for c in range(NCH):
                hf, cc = divmod(c, 4)
                if cc == 0:
                    s_brh = slot_load(w, WBR[l][hf], r_WBR[l])
                    s_g = [slot_load(w, WG[l][br * 2 + hf], r_WG[l]) for br in range(3)]
                brv = s_brh[0].rearrange("p (k n) -> p k n", k=8)
                kr = [(0, 3), (3, 6), (6, 8)]
                mts = []
                for br in range(3):
                    gv = s_g[br][0].rearrange("p (k n) -> p k n", k=8)
                    pg, rpg = next_ps()
                    P.mm([lambda e, k=k, pg=pg, gv=gv, cc=cc: e.matmul(
                        pg[:, :], lhsT=gv[:, k, cc * 128:(cc + 1) * 128], rhs=w.H[:, k, :],
                        start=(k == 0), stop=(k == NCH - 1)) for k in range(NCH)],
                        reads=[s_g[br][1], w.rH], writes=[rpg])
                    pb, rpb = next_ps()
                    k0, k1 = kr[br]
                    P.mm([lambda e, k=k, pb=pb, brv=brv, cc=cc, k0=k0, k1=k1: e.matmul(
                        pb[:, :], lhsT=brv[:, k, cc * 128:(cc + 1) * 128], rhs=w.O[:, k, :],
                        start=(k == k0), stop=(k == k1 - 1)) for k in range(k0, k1)],
                        reads=[s_brh[1], w.rO], writes=[rpb])
                    gt, rgt = w.gt[br], w.r_gt[br]
                    bcol = (l * 3 + br) * NCH + c
                    P.op("act", lambda e, gt=gt, pg=pg, bcol=bcol: e.activation(
                        out=gt, in_=pg[:, :], func=AF.Sigmoid, bias=bgate[:, bcol:bcol + 1], scale=1.0),
                        reads=[rpg, r_bgate], writes=[rgt])
                    i = w.tmi % 3
                    w.tmi += 1
                    mt_, rmt = w.tmp[i], w.r_tmp[i]
                    P.op("dve", lambda e, mt_=mt_, gt=gt, pb=pb: e.tensor_tensor(
                        out=mt_, in0=gt, in1=pb[:, :], op=ALU.mult), reads=[rgt, rpb], writes=[rmt])
                    mts.append((mt_, rmt))
                P.op("dve", lambda e, a=mts[0][0], b=mts[1][0]: e.tensor_tensor(out=a, in0=a, in1=b, op=ALU.add),
                     reads=[mts[1][1]], writes=[mts[0][1]])
                P.op("dve", lambda e, a=mts[0][0], b=mts[2][0], c=c: e.tensor_tensor(
                    out=w.M[:, c, :], in0=a, in1=b, op=ALU.add),
                    reads=[mts[0][1], mts[2][1]], writes=[w.rM])
            for m in range(8):
                hf, mm_ = divmod(m, 4)
                if mm_ == 0:
                    s_o = slot_load(w, WO[l][hf], r_WO[l])
                ov = s_o[0].rearrange("p (k n) -> p k n", k=8)
                py, rpy = next_ps()
                P.mm([lambda e, k=k, py=py, ov=ov, mm_=mm_: e.matmul(
                    py[:, :], lhsT=ov[:, k, mm_ * 128:(mm_ + 1) * 128], rhs=w.M[:, k, :],
                    start=(k == 0), stop=(k == NCH - 1)) for k in range(NCH)],
                    reads=[s_o[1], w.rM], writes=[rpy])
                P.op("act", lambda e, m=m, py=py: e.mul(out=w.Y[:, m, :], in_=py[:, :],
                                                          mul=gains[:, ycol + m:ycol + m + 1]),
                     reads=[rpy, r_gains], writes=[w.rY])
                stat_chunk(w, m, py[:, :], rpy)

        def s3_load_x(w, t):
            X, rX = w.X[t % 2], w.rX[t % 2]
            P.dma("sp", lambda e: e.dma_start(out=X.rearrange("p c t -> p (c t)"), in_=XS_r[t]),
                  rX, reads=[r_XS[t]], writes=[rX])

        def s3_body(w, l, t, X, rX):
            gb = (l * 6) * NCH
            if t == 0:
                s3_load_x(w, 0)
            if t + 1 < NT:
                s3_load_x(w, t + 1)
            P.dma("sp", lambda e: e.dma_start(out=w.O, in_=OT[:, :, t * TT:(t + 1) * TT]),
                  w.rO, reads=[r_OT], writes=[w.rO])
            prenorm(w, X, rX, gb + 2 * NCH)
            mixing(w, l, t, X, rX)
            postnorm_add(w, X, rX, gb + 3 * NCH, 1.0, then_pre=gb + 4 * NCH)
            ffn(w, l, 1)
            nxt = ((l + 1) * 6) * NCH if (l + 1 < NL and (l + 1) in layers_s1) else None
            postnorm_add(w, X, rX, gb + 5 * NCH, 0.5, then_pre=nxt)

        def mem_kv(w):
            X, rX = w.X[0], w.rX[0]
            mv = w.Y.rearrange("p c t -> p (c t)")[:, 0:2 * D].rearrange("p (tb f) -> p tb f", tb=2)
            P.dma("sp", lambda e: e.dma_start(out=mv, in_=mem_in[:, :].rearrange("(tb p) f -> p tb f", p=128)),
                  w.rY, writes=[w.rY])
            P.op("pool", lambda e: e.memset(X.rearrange("p c t -> p (c t)"), 1.0), writes=[rX])
            for c in range(NCH):
                ps, rps = next_ps()
                P.mm([lambda e, tb=tb, ps=ps, c=c: e.transpose(
                    ps[:, tb * 128:(tb + 1) * 128], mv[:, tb, c * 128:(c + 1) * 128], ident[:, :])
                    for tb in range(2)], reads=[w.rY, r_ident], writes=[rps])
                P.op("dve", lambda e, c=c, ps=ps: e.tensor_copy(out=X[:, c, 0:256], in_=ps[:, 0:256]),
                     reads=[rps], writes=[rX])
            for l in sorted(set(layers_s2)):
                prenorm(w, X, rX, l * NCH, gtile=mgain, r_g=r_mgain)
                s, rs_ = slot_load(w, WM[l][:, :], r_WM[l])
                sv = s.rearrange("p (c n) -> p c n", c=8)
                for mc in range(2):
                    ps, rps = next_ps()
                    P.mm([lambda e, c=c, ps=ps, mc=mc, sv=sv: e.matmul(
                        ps[:, 0:256], lhsT=sv[:, c, mc * 128:(mc + 1) * 128], rhs=w.H[:, c, 0:256],
                        start=(c == 0), stop=(c == NCH - 1)) for c in range(NCH)],
                        reads=[rs_, w.rH], writes=[rps])
                    P.op("dve", lambda e, ps=ps, mc=mc, l=l: e.tensor_copy(out=KC[0:64, l, mc, 0, :], in_=ps[0:64, 0:256]),
                         reads=[rps], writes=[r_const])
                    P.op("dve", lambda e, ps=ps, mc=mc, l=l: e.tensor_copy(out=KC[64:128, l, mc, 1, :], in_=ps[64:128, 0:256]),
                         reads=[rps], writes=[r_const])
                for kt in range(2):
                    ps, rps = next_ps()
                    P.mm([lambda e, c=c, ps=ps, kt=kt, sv=sv: e.matmul(
                        ps[:, 0:256], lhsT=w.H[:, c, kt * 128:(kt + 1) * 128], rhs=sv[:, c, 256:512],
                        start=(c == 0), stop=(c == NCH - 1)) for c in range(NCH)],
                        reads=[rs_, w.rH], writes=[rps])
                    P.op("dve", lambda e, ps=ps, kt=kt, l=l: e.tensor_copy(out=VC[:, l, kt, :], in_=ps[:, 0:256]),
                         reads=[rps], writes=[r_const])

        def s2_stage(l):
            arena.reset()
            NQT = TOK // 128
            Qsb = [arena.alloc([TOK], BF16) for _ in range(2)]
            rQ = [P.res("Qsb0"), P.res("Qsb1")]
            KMAX = 16 * 128 + TOK
            KA = [arena.alloc([KMAX], BF16) for _ in range(2)]
            KB = [arena.alloc([KMAX], BF16) for _ in range(2)]
            rK = [P.res("Ksb0"), P.res("Ksb1")]
            for i in range(2):
                P.op("pool", lambda e, i=i: e.memset(KA[i][64:128, :], 0.0), writes=[rK[i]])
                P.op("pool", lambda e, i=i: e.memset(KB[i][0:64, :], 0.0), writes=[rK[i]])
            VBLK = 16 + TOK // 128
            Vsb = [arena.alloc([VBLK * 128], BF16) for _ in range(2)]
            rV = [P.res("Vsb0"), P.res("Vsb1")]
            Oacc = [arena.alloc([TOK], BF16) for _ in range(3)]
            rOa = [P.res(f"Oacc{g}") for g in range(3)]
            Dacc = [arena.alloc([TOK], F32)]
            rDa = [P.res("Dacc0")]
            Ost = [arena.alloc([TOK], BF16) for _ in range(2)]
            rOst = [P.res("Ost0"), P.res("Ost1")]
            Pex = [arena.alloc([512], BF16) for _ in range(4)]
            rPex = [P.res(f"Pex{i}") for i in range(4)]
            PT = [arena.alloc([512], BF16) for _ in range(4)]
            rPT = [P.res(f"PT{i}") for i in range(4)]
            Dr = [arena.alloc([512], F32) for _ in range(2)]
            rDr = [P.res("Dr0"), P.res("Dr1")]
            cnt = {"q": 0, "kv": 0, "p": 0, "d": 0, "o": 0}

            def load_q(qchunk):
                i = cnt["q"] % 2
                cnt["q"] += 1
                P.dma("sp", lambda e, i=i: e.dma_start(out=Qsb[i], in_=QT_r[:, qchunk, :]), rQ[i],
                      reads=[r_QT], writes=[rQ[i]])
                return Qsb[i], rQ[i]

            def load_kv(kchunk, vgroup, r):
                i = cnt["kv"] % 2
                cnt["kv"] += 1
                npos = TOK // r
                nb = npos // 128
                kvA = KA[i][:, 0:r * (128 + npos)].rearrange("p (j n) -> p j n", j=r)
                kvB = KB[i][:, 0:r * (128 + npos)].rearrange("p (j n) -> p j n", j=r)
                kv = (kvA, kvB)
                vv = Vsb[i][:, 0:r * (nb + 1) * 128].rearrange("p (j b d) -> p j b d", j=r, b=nb + 1)
                koff = sum(K_RATES[:kchunk]) * 128
                hk = HR[0:HROWS // 2, :].rearrange("a b -> (a b)").rearrange("(p n) -> p n", p=128)
                hv = HR[HROWS // 2:HROWS, :].rearrange("a b -> (a b)").rearrange("(n d) -> n d", d=128)
                first_dma = True
                for hi, kz in enumerate(kv):
                    ps_ = slice(hi * 64, (hi + 1) * 64)
                    P.dma("sp", lambda e, kz=kz, ps_=ps_: e.dma_start(
                        out=kz[ps_, :, 0:128], in_=hk[ps_, koff:koff + r * 128].rearrange("p (j n) -> p j n", j=r)),
                        rK[i], reads=[r_HR], writes=[rK[i]], accumulate=not first_dma)
                    first_dma = False
                    P.dma("sp", lambda e, kz=kz, ps_=ps_: e.dma_start(
                        out=kz[ps_, :, 128:], in_=KT_r[ps_, kchunk, :].rearrange("p (j n) -> p j n", j=r)),
                        rK[i], reads=[r_KT], writes=[rK[i]], accumulate=True)
                P.dma("sp", lambda e: e.dma_start(
                    out=vv[:, :, 0, :], in_=hv[koff:koff + r * 128, :].rearrange("(j p) d -> p j d", p=128)),
                    rV[i], reads=[r_HR], writes=[rV[i]])
                for j in range(r):
                    P.dma("sp", lambda e, j=j: e.dma_start(
                        out=vv[:, j, 1:, :],
                        in_=VT_r[vgroup, j * npos:(j + 1) * npos, :].rearrange("(b p) d -> p b d", p=128)),
                        rV[i], reads=[r_VT], writes=[rV[i]], accumulate=True)
                return kv, rK[i], vv, rV[i]

            def run_chunk(nqt, Q, rQ_, key_fn, val_fn, E_fn, rKV, sink_for_group):
                def s1(qt):
                    qcols = slice(qt * 128, (qt + 1) * 128)
                    pss, rpss = next_ps()
                    P.mm([lambda e, hi=hi, pc=pc, pss=pss, qt=qt, qcols=qcols: e.matmul(
                        pss[:, (hi * 2 + pc) * 128:(hi * 2 + pc + 1) * 128],
                        lhsT=key_fn(qt, hi, pc), rhs=Q[:, qcols], start=True, stop=True)
                        for hi in range(2) for pc in range(2)], reads=[rQ_] + rKV, writes=[rpss])
                    i = cnt["p"] % 4
                    cnt["p"] += 1
                    E = E_fn(qt)
                    if E is None:
                        P.op("act", lambda e, i=i, pss=pss: e.activation(
                            out=PT[i], in_=pss[:, :], func=AF.Exp, scale=0.125), reads=[rpss], writes=[rPT[i]])
                    else:
                        P.op("act", lambda e, i=i, pss=pss: e.activation(
                            out=Pex[i], in_=pss[:, :], func=AF.Exp, scale=0.125), reads=[rpss], writes=[rPex[i]])
                        P.op("pool" if qt % 3 == 2 else "dve",
                             lambda e, i=i, E=E: e.tensor_tensor(out=PT[i], in0=Pex[i], in1=E, op=ALU.mult),
                             reads=[rPex[i], r_E], writes=[rPT[i]])
                    return i

                def s2(qt, i, qi, pso, rpso, psd, rpsd):
                    fns = []
                    for hi in range(2):
                        for pc in range(2):
                            fns.append(lambda e, hi=hi, pc=pc, i=i, qt=qt, qi=qi: e.matmul(
                                pso[hi * 64:(hi + 1) * 64, qi * 128:(qi + 1) * 128],
                                lhsT=val_fn(qt, hi, pc), rhs=PT[i][:, (hi * 2 + pc) * 128:(hi * 2 + pc + 1) * 128],
                                start=(pc == 0), stop=(pc == 1)))
                    P.mm(fns, reads=[rPT[i]] + rKV, writes=[rpso])
                    fns = []
                    for hi in range(2):
                        for pc in range(2):
                            fns.append(lambda e, hi=hi, pc=pc, i=i, qi=qi: e.matmul(
                                psd[hi * 64:(hi + 1) * 64, qi * 128:(qi + 1) * 128],
                                lhsT=ones[:, hi * 64:(hi + 1) * 64], rhs=PT[i][:, (hi * 2 + pc) * 128:(hi * 2 + pc + 1) * 128],
                                start=(pc == 0), stop=(pc == 1)))
                    P.mm(fns, reads=[rPT[i], r_ones], writes=[rpsd])

                LA = 2
                pend = [s1(q) for q in range(min(LA, nqt))]
                banks = None
                for qt in range(nqt):
                    if qt + LA < nqt:
                        pend.append(s1(qt + LA))
                    if qt % 4 == 0:
                        banks = next_ps() + next_ps()
                    s2(qt, pend.pop(0), qt % 4, *banks)
                    if qt % 4 == 3:
                        sink_for_group(qt - 3)(*banks)

            def finish_direct(sink_col, Ot, rOt, c0):
                def f(pso, rpso, psd, rpsd):
                    i = cnt["d"] % 2
                    cnt["d"] += 1
                    if sink_col is not None:
                        P.op("act", lambda e, i=i: e.activation(
                            out=Dr[i], in_=psd[:, :], func=AF.Ln, bias=sinke[:, sink_col:sink_col + 1], scale=1.0),
                            reads=[rpsd, r_sink], writes=[rDr[i]])
                    else:
                        P.op("act", lambda e, i=i: e.activation(out=Dr[i], in_=psd[:, :], func=AF.Ln),
                             reads=[rpsd], writes=[rDr[i]])
                    P.op("act", lambda e, i=i: e.activation(out=Dr[i], in_=Dr[i], func=AF.Exp, scale=-1.0),
                         reads=[rDr[i]], writes=[rDr[i]])
                    P.op("dve", lambda e, i=i, c0=c0: e.tensor_tensor(
                        out=Ot[:, c0:c0 + 512], in0=pso[:, :], in1=Dr[i], op=ALU.mult),
                        reads=[rpso, rDr[i]], writes=[rOt])
                return f

            kv_cache = {}
            for sc, (qch, kch, vg, r, md, cols, has_sink) in enumerate(SELF_CHUNKS):
                if ("s2_sw" in DBG_SKIP and has_sink) or ("s2_dil" in DBG_SKIP and not has_sink):
                    continue
                Q, rQ_ = load_q(qch)
                if (kch, vg) not in kv_cache:
                    kv_cache.clear()
                    kv_cache[(kch, vg)] = load_kv(kch, vg, r)
                kv, rKc, vv, rVc = kv_cache[(kch, vg)]
                npos = TOK // r
                nb = npos // 128

                def key_fn(qt, hi, pc, kv=kv, nb=nb):
                    j, b = divmod(qt, nb)
                    return kv[hi][:, j, b * 128 + pc * 128:b * 128 + pc * 128 + 128]

                def val_fn(qt, hi, pc, vv=vv, nb=nb):
                    j, b = divmod(qt, nb)
                    return vv[:, j, b + pc, hi * 64:(hi + 1) * 64]

                def E_fn(qt, sc=sc, nb=nb):
                    return (Efirst if qt % nb == 0 else Emat)[:, sc, :]

                if has_sink:
                    oi = cnt["o"] % 2
                    cnt["o"] += 1
                    run_chunk(NQT, Q, rQ_, key_fn, val_fn, E_fn, [rKc, rVc],
                              lambda qt0, sc=sc, oi=oi: finish_direct(l * 3 + sc, Ost[oi], rOst[oi], qt0 * 128))
                    P.dma("poolq", lambda e, oi=oi, qch=qch: e.dma_start(out=OT[:, qch, :], in_=Ost[oi]), rOst[oi],
                          reads=[rOst[oi]], writes=[r_OT], accumulate=True)
                else:
                    g = sc - 3
                    ov3 = Oacc[g].rearrange("p (n r) -> p r n", r=r)
                    dv3 = Dacc[0].rearrange("p (n r) -> p r n", r=r)

                    def sink_acc(pso, rpso, psd, rpsd, qt0=None):
                        pass

                    def dil_sink(qt0, nb=nb, ov3=ov3, dv3=dv3, g=g):
                        if nb >= 4:
                            j, b0 = divmod(qt0, nb)
                            od = ov3[:, j, b0 * 128:b0 * 128 + 512]
                            dd = dv3[:, j, b0 * 128:b0 * 128 + 512]
                            shp = None
                        else:
                            a = 4 // nb
                            j0 = qt0 // nb
                            od = ov3[:, j0:j0 + a, :]
                            dd = dv3[:, j0:j0 + a, :]
                            shp = a

                        def f(pso, rpso, psd, rpsd, od=od, dd=dd, shp=shp, g=g):
                            so = pso[:, :] if shp is None else pso[:, :].rearrange("p (a n) -> p a n", a=shp)
                            sd = psd[:, :] if shp is None else psd[:, :].rearrange("p (a n) -> p a n", a=shp)
                            P.op("act", lambda e: e.copy(out=od, in_=so), reads=[rpso], writes=[rOa[g]])
                            if g == 0:
                                P.op("dve", lambda e: e.tensor_copy(out=dd, in_=sd), reads=[rpsd], writes=[rDa[0]])
                            else:
                                P.op("dve", lambda e: e.tensor_tensor(out=dd, in0=dd, in1=sd, op=ALU.add),
                                     reads=[rpsd], writes=[rDa[0]])
                        return f

                    run_chunk(NQT, Q, rQ_, key_fn, val_fn, E_fn, [rKc, rVc], dil_sink)
            if "s2_dil" in DBG_SKIP:
                P.op("pool", lambda e: e.memset(Dacc[0], 1.0), writes=[rDa[0]])
                for g in range(3):
                    P.op("pool", lambda e, g=g: e.memset(Oacc[g], 1.0), writes=[rOa[g]])
            if "s2_comb" not in DBG_SKIP:
                P.op("dve", lambda e: e.reciprocal(out=Dacc[0], in_=Dacc[0]), reads=[rDa[0]], writes=[rDa[0]])
            for g in range(3 if "s2_comb" not in DBG_SKIP else 0):
                oi = cnt["o"] % 2
                cnt["o"] += 1
                P.op("dve", lambda e, g=g, oi=oi: e.tensor_tensor(out=Ost[oi], in0=Oacc[g], in1=Dacc[0], op=ALU.mult),
                     reads=[rOa[g], rDa[0]], writes=[rOst[oi]])
                P.dma("poolq", lambda e, oi=oi, g=g: e.dma_start(out=OT[:, 3 + g, :], in_=Ost[oi]), rOst[oi],
                      reads=[rOst[oi]], writes=[r_OT], accumulate=True)
            for mc in range(2 if "s2_mem" not in DBG_SKIP else 0):
                Q, rQ_ = load_q(6 + mc)

                def key_fn(qt, hi, pc, mc=mc):
                    return KC[:, l, mc, hi, pc * 128:(pc + 1) * 128]

                def val_fn(qt, hi, pc, mc=mc):
                    return VC[:, l, pc, mc * 128 + hi * 64:mc * 128 + (hi + 1) * 64]

                oi = cnt["o"] % 2
                cnt["o"] += 1
                run_chunk(NQT, Q, rQ_, key_fn, val_fn, lambda qt: None, [r_const],
                          lambda qt0, oi=oi: finish_direct(None, Ost[oi], rOst[oi], qt0 * 128))
                P.dma("poolq", lambda e, oi=oi, mc=mc: e.dma_start(out=OT[:, 6 + mc, :], in_=Ost[oi]), rOst[oi],
                      reads=[rOst[oi]], writes=[r_OT], accumulate=True)

        def halo_send(dst_buf):
            hk = dst_buf[0:HROWS // 2, :].rearrange("a b -> (a b)").rearrange("(p n) -> p n", p=128)
            hv = dst_buf[HROWS // 2:HROWS, :].rearrange("a b -> (a b)").rearrange("(n d) -> n d", d=128)
            for kch in range(4):
                r = K_RATES[kch]
                npos = TOK // r
                koff = sum(K_RATES[:kch]) * 128
                P.dma("sp", lambda e, kch=kch, r=r, npos=npos, koff=koff: e.dma_start(
                    out=hk[:, koff:koff + r * 128].rearrange("p (j n) -> p j n", j=r),
                    in_=KT_w[:, kch, :].rearrange("p (j n) -> p j n", j=r)[:, :, npos - 128:npos]),
                    r_HS, reads=[r_KT], writes=[r_HS], accumulate=True)
                P.dma("sp", lambda e, kch=kch, r=r, npos=npos, koff=koff: e.dma_start(
                    out=hv[koff:koff + r * 128, :].rearrange("(j p) d -> j p d", p=128),
                    in_=VT_w[kch, :, :].rearrange("(j n) d -> j n d", j=r)[:, npos - 128:npos, :]),
                    r_HS, reads=[r_VT], writes=[r_HS], accumulate=True)

        if not first:
            pass
        wts = alloc_token_stage()
        if layers_s2:
            mem_kv(wts)
        if first:
            per_tile = -(-len(cast_tasks) // max(NT - 1, 1))
            for t in range(NT):
                X, rX = wts.X[t % 2], wts.rX[t % 2]
                load_x_tokmajor(wts, t, X, rX)
                s1_body(wts, 0, t, X, rX)
                emit_casts(per_tile)
            emit_casts()
        for l in range(NL):
            if l not in layers_s2:
                continue
            if fused:
                halo_send(HS)
                r_cc = P.res("cc")
                P.dma("poolq", lambda e: e.collective_compute(
                    "AllGather", ALU.bypass, replica_groups=[[2 * i, 2 * i + 1] for i in range(n_cores // 2)],
                    ins=[HS[:, :].opt()], outs=[HR[:, :].opt()]), r_cc, reads=[r_HS], writes=[r_cc, r_HR], inc=1)
                P.op("pool", lambda e: e.memset(ccdummy[:, :], 0.0), reads=[r_cc], writes=[r_HR])
            P.barrier()
            if "s2" not in DBG_SKIP:
                s2_stage(l)
            P.barrier()
            wts = alloc_token_stage()
            for t in range(NT if "s3" not in DBG_SKIP else 0):
                X, rX = wts.X[t % 2], wts.rX[t % 2]
                s3_body(wts, l, t, X, rX)
                if l + 1 < NL and (l + 1) in layers_s1:
                    s1_body(wts, l + 1, t, X, rX, pre_done=True)
                elif l == NL - 1:
                    store_out_tokmajor(wts, t, X, rX)
        P.barrier(streams=("sp",))
        with nc.Block() as block:
            P.emit(block)
    return nc


_CACHE = {}


def _get_nc(TOK, mode, n_cores=N_CORES):
    key = (TOK, mode, n_cores)
    if key not in _CACHE:
        _CACHE[key] = build(TOK, mode, n_cores)
    return _CACHE[key]


def kernel(**inputs):
    x = np.asarray(inputs["x"], np.float32)
    mem = np.asarray(inputs["mem"], np.float32)
    B, S, _ = x.shape
    TOK = B * S // N_CORES
    halves = S // TOK
    sh = prep_shared(inputs)
    in_maps = []
    for c in range(N_CORES):
        b, hf = divmod(c, halves)
        m = dict(sh)
        m["x"] = np.ascontiguousarray(x[b, hf * TOK:(hf + 1) * TOK, :])
        m["mem"] = np.ascontiguousarray(mem[b])
        m["halo_valid"] = np.full((128, 1), 1.0 if hf > 0 else 0.0, np.float32)
        in_maps.append(m)
    nc = _get_nc(TOK, "fused")
    res = run_bass_kernel_spmd(nc, in_maps, core_ids=list(range(N_CORES)))
    out = np.empty((B, S, D), np.float32)
    for c in range(N_CORES):
        b, hf = divmod(c, halves)
        out[b, hf * TOK:(hf + 1) * TOK, :] = res.results[c]["out"]
    return out


def _halo_from(KT, VT, TOK):
    HROWS = 2 * 2816 * 128 // 1024
    hk = np.zeros((128, 2816), KT.dtype)
    hv = np.zeros((2816, 128), VT.dtype)
    for kch in range(4):
        r = K_RATES[kch]
        npos = TOK // r
        koff = sum(K_RATES[:kch]) * 128
        kk = KT[:, kch, :].reshape(128, r, npos)[:, :, npos - 128:]
        hk[:, koff:koff + r * 128] = kk.reshape(128, r * 128)
        vv = VT[kch].reshape(r, npos, 128)[:, npos - 128:, :]
        hv[koff:koff + r * 128, :] = vv.reshape(r * 128, 128)
    HR = np.zeros((2 * HROWS, 1024), KT.dtype)
    HR[0:HROWS // 2] = hk.reshape(HROWS // 2, 1024)
    HR[HROWS // 2:HROWS] = hv.reshape(HROWS // 2, 1024)
    return HR


def kernel_unfused(n_cores=N_CORES, debug=None, **inputs):
    x = np.asarray(inputs["x"], np.float32)
    mem = np.asarray(inputs["mem"], np.float32)
    B, S, _ = x.shape
    TOK = B * S // n_cores
    halves = S // TOK
    sh = prep_shared(inputs)
    base = []
    for c in range(n_cores):
        b, hf = divmod(c, halves)
        m = dict(sh)
        m["mem"] = np.ascontiguousarray(mem[b])
        m["halo_valid"] = np.full((128, 1), 1.0 if hf > 0 else 0.0, np.float32)
        base.append(m)
    cores = list(range(n_cores))

    def halos(res):
        hr = []
        for c in range(n_cores):
            b, hf = divmod(c, halves)
            src = c - 1 if hf > 0 else c
            hr.append(_halo_from(np.asarray(res[src]["KT_o"]), np.asarray(res[src]["VT_o"]), TOK))
        return hr

    ins = []
    for c in range(n_cores):
        b, hf = divmod(c, halves)
        m = dict(base[c])
        m["x"] = np.ascontiguousarray(x[b, hf * TOK:(hf + 1) * TOK, :])
        ins.append(m)
    ra = run_bass_kernel_spmd(_get_nc(TOK, "A", n_cores), ins, core_ids=cores).results
    if debug is not None:
        debug["A"] = ra
    hr = halos(ra)
    ins = []
    for c in range(n_cores):
        m = dict(base[c])
        m.update({"XS_i": ra[c]["XS_o"], "QT_i": ra[c]["QT_o"], "KT_i": ra[c]["KT_o"], "VT_i": ra[c]["VT_o"],
                  "HR": hr[c]})
        ins.append(m)
    rb = run_bass_kernel_spmd(_get_nc(TOK, "B", n_cores), ins, core_ids=cores).results
    if debug is not None:
        debug["B"] = rb
    hr = halos(rb)
    ins = []
    for c in range(n_cores):
        m = dict(base[c])
        m.update({"XS_i": rb[c]["XS_o"], "QT_i": rb[c]["QT_o"], "KT_i": rb[c]["KT_o"], "VT_i": rb[c]["VT_o"],
                  "HR": hr[c]})
        ins.append(m)
    rc = run_bass_kernel_spmd(_get_nc(TOK, "C", n_cores), ins, core_ids=cores).results
    if debug is not None:
        debug["C"] = rc
    out = np.empty((B, S, D), np.float32)
    for c in range(n_cores):
        b, hf = divmod(c, halves)
        out[b, hf * TOK:(hf + 1) * TOK, :] = rc[c]["out"]
    return out
```

```python
import math
import numpy as np
import ml_dtypes
import concourse.bass as bass
import concourse.mybir as mybir
from concourse.bass_utils import run_bass_kernel_spmd
from contextlib import ExitStack

F32 = mybir.dt.float32
BF16 = mybir.dt.bfloat16
AF = mybir.ActivationFunctionType
ALU = mybir.AluOpType

D = 1024
NCH = 8
TT = 512
FF = 2816
JH = 22
NL = 2
EPS = 1e-6
N_CORES = 8
DBG_SKIP = set()

COMPUTE = ("pe", "act", "dve", "pool")
STREAM_OF = {"pe": "pe", "act": "act", "dve": "dve", "pool": "pool",
             "sp": "sp", "actq": "act", "poolq": "pool"}


class Res:
    __slots__ = ("name", "w", "r", "dsem", "dcnt")

    def __init__(self, name):
        self.name = name
        self.w = []
        self.r = []
        self.dsem = None
        self.dcnt = 0


class Prog:
    def __init__(self, nc, es, n_dma_sems=92):
        self.nc = nc
        self.ops = {s: [] for s in ("pe", "act", "dve", "pool", "sp")}
        self.sem = {e: es.enter_context(nc.semaphore("sem_" + e)) for e in COMPUTE}
        self.free_dma = [es.enter_context(nc.semaphore(f"dsem{i}")) for i in range(n_dma_sems)]
        self.cnt = {e: 0 for e in COMPUTE}
        self.seen = {s: {} for s in self.ops}
        self.dma_res = []

    def res(self, name):
        return Res(name)

    def _dsem(self, r):
        if r.dsem is None:
            r.dsem = self.free_dma.pop()
            self.dma_res.append(r)
        return r.dsem

    def _waits(self, stream, evs, own=None):
        need = {}
        for ev in evs:
            k, v = ev
            if need.get(id(k), (None, -1))[1] < v:
                need[id(k)] = (k, v)
        seen = self.seen[stream]
        for kid, (k, v) in need.items():
            if own is not None and k is own and stream == "pe":
                continue
            if seen.get(kid, -1) >= v:
                continue
            seen[kid] = v
            self.ops[stream].append(("wait", k, v))

    def _deps(self, reads, writes, accumulate):
        evs = []
        for r in reads:
            evs.extend(r.w)
        for w in writes:
            if not accumulate:
                evs.extend(w.w)
            evs.extend(w.r)
        return evs

    def _record(self, ev, reads, writes, accumulate):
        for r in reads:
            r.r.append(ev)
        for w in writes:
            if accumulate:
                w.w.append(ev)
            else:
                w.w = [ev]
            w.r = []

    def op(self, eng, fn, reads=(), writes=()):
        stream = STREAM_OF[eng]
        sem = self.sem[eng]
        self._waits(stream, self._deps(reads, writes, False), own=sem)
        self.cnt[eng] += 1
        ev = (sem, self.cnt[eng])
        self.ops[stream].append(("inst", fn, sem, 1))
        self._record(ev, reads, writes, False)

    def mm(self, fns, reads=(), writes=()):
        sem = self.sem["pe"]
        self._waits("pe", self._deps(reads, writes, False), own=sem)
        for fn in fns[:-1]:
            self.ops["pe"].append(("inst", fn, None, 0))
        self.cnt["pe"] += 1
        ev = (sem, self.cnt["pe"])
        self.ops["pe"].append(("inst", fns[-1], sem, 1))
        self._record(ev, reads, writes, False)

    def dma(self, queue, fn, sb, reads=(), writes=(), accumulate=False, inc=16):
        stream = STREAM_OF[queue]
        self._waits(stream, self._deps(reads, writes, accumulate))
        k = self._dsem(sb)
        sb.dcnt += inc
        ev = (k, sb.dcnt)
        self.ops[stream].append(("inst", fn, k, inc))
        self._record(ev, reads, writes, accumulate)
        return ev

    def all_events(self, stream):
        evs = [(self.sem[e], self.cnt[e]) for e in COMPUTE if self.cnt[e] > 0]
        evs += [(r.dsem, r.dcnt) for r in self.dma_res
                if r.dcnt > 0 and (stream == "pool" or not r.name.startswith("cc"))]
        return evs

    def barrier(self, streams=("pe", "act", "dve", "pool", "sp")):
        for s in streams:
            self._waits(s, self.all_events(s))

    def emit(self, block):
        ops = self.ops

        def run(eng_obj, lst):
            for o in lst:
                if o[0] == "wait":
                    eng_obj.wait_ge(o[1], o[2])
                else:
                    ins = o[1](eng_obj)
                    if o[2] is not None:
                        ins.then_inc(o[2], o[3])

        @block.tensor
        def _(e):
            run(e, ops["pe"])

        @block.scalar
        def _(e):
            run(e, ops["act"])

        @block.vector
        def _(e):
            run(e, ops["dve"])

        @block.gpsimd
        def _(e):
            run(e, ops["pool"])

        @block.sync
        def _(e):
            run(e, ops["sp"])


class Arena:
    def __init__(self, ap_f32):
        self.ap = ap_f32
        self.n = ap_f32.shape[1]
        self.off = 0

    def reset(self):
        self.off = 0

    def alloc(self, free_shape, dtype):
        n = int(np.prod(free_shape))
        if dtype == BF16:
            assert n % 2 == 0
            n32 = n // 2
        else:
            n32 = n
        n32a = (n32 + 7) // 8 * 8
        assert self.off + n32a <= self.n, f"arena overflow {self.off}+{n32a}>{self.n}"
        v = self.ap[:, self.off:self.off + n32]
        self.off += n32a
        if dtype == BF16:
            v = v.bitcast(BF16)
        if len(free_shape) == 2:
            v = v.rearrange("p (a b) -> p a b", a=free_shape[0])
        elif len(free_shape) == 3:
            v = v.rearrange("p (a b c) -> p a b c", a=free_shape[0], b=free_shape[1])
        return v


QA_HEAD_ORDER = [0, 3, 1, 4, 2, 5]
SELF_CHUNKS = [
    (0, 0, 0, 1, 127, (0, 3), True),
    (1, 0, 0, 1, 127, (1, 4), True),
    (2, 0, 0, 1, 127, (2, 5), True),
    (3, 1, 1, 1, 128, (6, 7), False),
    (4, 2, 2, 4, 128, (8, 9), False),
    (5, 3, 3, 16, 128, (10, 11), False),
]
K_RATES = [1, 1, 4, 16]
Q_RATES = [1, 1, 1, 1, 4, 16, 1, 1]


def _t5_bucket(dist):
    dist = np.asarray(dist)
    me = 16
    dd = np.maximum(dist, 1).astype(np.float32)
    large = me + (np.log(dd / np.float32(me)) / np.float32(math.log(2048 / me))
                  * np.float32(32 - me)).astype(np.int32)
    large = np.minimum(large, 31)
    return np.where(dist < me, dist, large)


def _win_perm():
    qa = np.concatenate([np.arange(64 * h, 64 * h + 64) for h in QA_HEAD_ORDER])
    ka = np.arange(384, 512)
    va = np.arange(512, 640)
    qb = np.arange(640, 1024)
    kb = np.arange(1024, 1408)
    vb = np.arange(1408, 1792)
    qc = np.arange(1792, 2048)
    return np.concatenate([qa, qb, qc, ka, kb, va, vb])


def _static_bias_tables(rel_bias):
    kk = np.arange(128)[:, None]
    qq = np.arange(128)[None, :]
    bias = np.zeros((6, 128, 2, 2, 128), np.float32)
    mask = np.zeros((6, 128, 2, 2, 128), np.float32)
    for sc, (_, _, _, r, md, cols, _) in enumerate(SELF_CHUNKS):
        d_prev = qq + 128 - kk
        d_cur = qq - kk
        v_prev = (d_prev <= md)
        v_cur = (d_cur >= 0)
        for hi, col in enumerate(cols):
            tab = rel_bias[:, col]
            bias[sc, :, hi, 0, :] = np.where(v_prev, tab[_t5_bucket(d_prev * r)], 0.0)
            bias[sc, :, hi, 1, :] = np.where(v_cur, tab[_t5_bucket(np.maximum(d_cur, 0) * r)], 0.0)
            mask[sc, :, hi, 0, :] = v_prev
            mask[sc, :, hi, 1, :] = v_cur
    return bias.reshape(6, 128, 512), mask.reshape(6, 128, 512)


def _chunk_cols(v):
    v = np.asarray(v, np.float32)
    lead = v.shape[:-1]
    return np.ascontiguousarray(v.reshape(-1, NCH, 128).transpose(2, 0, 1).reshape(128, -1)), lead


def prep_shared(inp):
    sh = {}
    perm = _win_perm()
    sh["w_ffn1_in"] = np.ascontiguousarray(inp["w_ffn1_in"], np.float32)
    sh["w_ffn1_out"] = np.ascontiguousarray(inp["w_ffn1_out"], np.float32)
    sh["w_ffn2_in"] = np.ascontiguousarray(inp["w_ffn2_in"], np.float32)
    sh["w_ffn2_out"] = np.ascontiguousarray(inp["w_ffn2_out"], np.float32)
    sh["w_in"] = np.ascontiguousarray(np.asarray(inp["w_in"], np.float32)[:, :, perm])
    sh["w_gate"] = np.ascontiguousarray(inp["w_gate"], np.float32)
    rows_a = np.concatenate([np.arange(64 * h, 64 * h + 64) for h in QA_HEAD_ORDER])
    wbr = np.concatenate([np.asarray(inp["w_br_a"], np.float32)[:, rows_a, :],
                          np.asarray(inp["w_br_b"], np.float32),
                          np.asarray(inp["w_br_c"], np.float32)], axis=1)
    sh["w_br"] = np.ascontiguousarray(wbr)
    sh["w_o"] = np.ascontiguousarray(inp["w_o"], np.float32)
    sh["w_mem_kv"] = np.ascontiguousarray(inp["w_mem_kv"], np.float32)
    g, _ = _chunk_cols(inp["norm_gain"])
    sh["gains"] = g
    mg, _ = _chunk_cols(inp["mem_norm_gain"])
    sh["mgain"] = mg
    bg, _ = _chunk_cols(inp["b_gate"])
    sh["bgate"] = bg
    sinks = np.asarray(inp["sinks"], np.float32)
    sk = np.zeros((128, NL * 3), np.float32)
    for l in range(NL):
        for ci in range(3):
            sk[0:64, l * 3 + ci] = sinks[l, QA_HEAD_ORDER[2 * ci]]
            sk[64:128, l * 3 + ci] = sinks[l, QA_HEAD_ORDER[2 * ci + 1]]
    sh["sinks"] = sk
    bias, mask = _static_bias_tables(np.asarray(inp["rel_bias"], np.float32))
    sh["biasT"] = np.ascontiguousarray(bias.transpose(1, 0, 2))
    sh["maskT"] = np.ascontiguousarray(mask.transpose(1, 0, 2))
    sh["ident"] = np.eye(128, dtype=np.float32)
    return sh


def build(TOK, mode="fused", n_cores=N_CORES):
    NT = TOK // TT
    fused = mode == "fused"
    nc = bass.Bass("TRN2", target_bir_lowering=False)

    def din(name, shape, dt=F32):
        return nc.dram_tensor(name, list(shape), dt, kind="ExternalInput").ap()

    def dout(name, shape, dt=F32):
        return nc.dram_tensor(name, list(shape), dt, kind="ExternalOutput").ap()

    def dint(name, shape, dt=F32):
        return nc.dram_tensor(name, list(shape), dt).ap()

    def dscr(name, shape, dt, is_in, is_out):
        if fused:
            return dint(name, shape, dt), None
        i = din(name + "_i", shape, dt) if is_in else None
        o = dout(name + "_o", shape, dt) if is_out else None
        return i, o

    first = mode in ("fused", "A")
    last = mode in ("fused", "C")
    layers_s1 = {"fused": [0, 1], "A": [0], "B": [1], "C": []}[mode]
    layers_s2 = {"fused": [0, 1], "A": [], "B": [0], "C": [1]}[mode]

    x_in = din("x", [TOK, D]) if first else None
    mem_in = din("mem", [256, D])
    hv_in = din("halo_valid", [128, 1])
    w1i = [din("w_ffn1_in", [NL, D, 2 * FF]), din("w_ffn2_in", [NL, D, 2 * FF])]
    w1o = [din("w_ffn1_out", [NL, FF, D]), din("w_ffn2_out", [NL, FF, D])]
    w_in = din("w_in", [NL, D, 2048])
    w_gate = din("w_gate", [NL, D, 3 * D])
    w_br = din("w_br", [NL, D, D])
    w_o = din("w_o", [NL, D, D])
    w_mkv = din("w_mem_kv", [NL, D, 512])
    gains_in = din("gains", [128, NL * 6 * NCH])
    mgain_in = din("mgain", [128, NL * NCH])
    bgate_in = din("bgate", [128, NL * 3 * NCH])
    sinks_in = din("sinks", [128, NL * 3])
    biasT_in = din("biasT", [128, 6, 512])
    maskT_in = din("maskT", [128, 6, 512])
    ident_in = din("ident", [128, 128])
    out_ap = dout("out", [TOK, D]) if last else None

    HROWS = 2 * 2816 * 128 // 1024
    if fused:
        XS_r = XS_w = dint("XS", [NT, 128, NCH * TT], F32)
        QT_r = QT_w = dint("QT", [128, 8, TOK], BF16)
        KT_r = KT_w = dint("KT", [128, 4, TOK], BF16)
        VT_r = VT_w = dint("VT", [4, TOK, 128], BF16)
        HS = dint("HS", [HROWS, 1024], BF16)
        HR = dint("HR", [2 * HROWS, 1024], BF16)
    else:
        XS_r = din("XS_i", [NT, 128, NCH * TT], F32) if mode in ("B", "C") else None
        XS_w = dout("XS_o", [NT, 128, NCH * TT], F32) if mode in ("A", "B") else None
        QT_r = din("QT_i", [128, 8, TOK], BF16) if mode in ("B", "C") else None
        KT_r = din("KT_i", [128, 4, TOK], BF16) if mode in ("B", "C") else None
        VT_r = din("VT_i", [4, TOK, 128], BF16) if mode in ("B", "C") else None
        QT_w = dout("QT_o", [128, 8, TOK], BF16) if mode in ("A", "B") else None
        KT_w = dout("KT_o", [128, 4, TOK], BF16) if mode in ("A", "B") else None
        VT_w = dout("VT_o", [4, TOK, 128], BF16) if mode in ("A", "B") else None
        HS = None
        HR = din("HR", [2 * HROWS, 1024], BF16) if mode in ("B", "C") else None
    OT = dint("OT", [128, 8, TOK], BF16) if fused or not layers_s2 else dout("OT_dbg", [128, 8, TOK], BF16)
    W1 = [[dint(f"W1_{l}_{f}", [11, 128, 4096], BF16) for f in range(2)] for l in range(NL)]
    W2 = [[dint(f"W2_{l}_{f}", [8, 128, JH * 128], BF16) for f in range(2)] for l in range(NL)]
    WIN = [dint(f"WIN_{l}", [4, 128, 4096], BF16) for l in range(NL)]
    WG = [dint(f"WG_{l}", [6, 128, 4096], BF16) for l in range(NL)]
    WBR = [dint(f"WBR_{l}", [2, 128, 4096], BF16) for l in range(NL)]
    WO = [dint(f"WO_{l}", [2, 128, 4096], BF16) for l in range(NL)]
    WM = [dint(f"WM_{l}", [128, 4096], BF16) for l in range(NL)]

    es = ExitStack()
    with es:
        P = Prog(nc, es)

        def sb(name, shape, dt):
            return es.enter_context(nc.sbuf_tensor("sb_" + name, list(shape), dt))

        ident = sb("ident", [128, 128], F32)
        ones = sb("ones", [128, 128], BF16)
        gains = sb("gains", [128, NL * 6 * NCH], F32)
        mgain = sb("mgain", [128, NL * NCH], F32)
        bgate = sb("bgate", [128, NL * 3 * NCH], F32)
        sinke = sb("sinke", [128, NL * 3], F32)
        hval = sb("hval", [128, 1], F32)
        ccdummy = sb("ccdummy", [128, 8], F32)
        epsb = sb("epsb", [128, 2], F32)
        Emat = sb("Emat", [128, 6, 512], BF16)
        Efirst = sb("Efirst", [128, 6, 512], BF16)
        KC = sb("KC", [128, NL, 2, 2, 256], BF16)
        VC = sb("VC", [128, NL, 2, 256], BF16)
        psum = [es.enter_context(nc.psum_tensor(f"ps{i}", [128, 512], F32)) for i in range(8)]
        r_psum = [P.res(f"ps{i}") for i in range(8)]
        ps_i = [0]
        ARENA_W = 45200
        arena_t = sb("arena", [128, ARENA_W], F32)
        arena = Arena(arena_t[:, :])
        r_const = P.res("const")

        def next_ps():
            i = ps_i[0] % 7
            ps_i[0] += 1
            return psum[i], r_psum[i]

        def stat_ps():
            return psum[7], r_psum[7]

        def load_const(dst, src, nm):
            r = P.res(nm)
            P.dma("sp", lambda e, d=dst, s=src: e.dma_start(out=d, in_=s), r, writes=[r])
            return r

        r_ident = load_const(ident[:, :], ident_in[:, :], "ident")
        r_gains = load_const(gains[:, :], gains_in[:, :], "gains")
        r_mgain = load_const(mgain[:, :], mgain_in[:, :], "mgain")
        r_bgate = load_const(bgate[:, :], bgate_in[:, :], "bgate")
        r_sink = load_const(sinke[:, :], sinks_in[:, :], "sinks")
        r_hval = load_const(hval[:, :], hv_in[:, :], "hval")
        r_ones = P.res("ones")
        P.op("pool", lambda e: e.memset(epsb[:, 0:1], EPS), writes=[r_ones])
        P.op("pool", lambda e: e.memset(epsb[:, 1:2], EPS / 0.25), writes=[r_ones])
        P.op("pool", lambda e: e.memset(ones[:, :], 1.0), writes=[r_ones])
        P.op("pool", lambda e: e.memset(KC[:, :, :, :, :].rearrange("p a b c d -> p (a b c d)"), 0.0), writes=[r_const])
        P.op("act", lambda e: e.activation(out=sinke[:, :], in_=sinke[:, :], func=AF.Exp),
             reads=[r_sink], writes=[r_sink])
        r_E = P.res("E")
        if layers_s2:
            arena.reset()
            bt = arena.alloc([6, 512], F32)
            mt = arena.alloc([6, 512], F32)
            r_bt, r_mt = P.res("bt"), P.res("mt")
            P.dma("sp", lambda e: e.dma_start(out=bt, in_=biasT_in[:, :, :]), r_bt, writes=[r_bt])
            P.dma("sp", lambda e: e.dma_start(out=mt, in_=maskT_in[:, :, :]), r_mt, writes=[r_mt])
            P.op("act", lambda e: e.activation(out=bt, in_=bt, func=AF.Exp), reads=[r_bt], writes=[r_bt])
            P.op("dve", lambda e: e.tensor_tensor(out=Emat[:, :, :], in0=bt, in1=mt, op=ALU.mult),
                 reads=[r_bt, r_mt], writes=[r_E])
            P.op("dve", lambda e: e.tensor_copy(out=Efirst[:, :, :], in_=Emat[:, :, :]),
                 reads=[r_E], writes=[r_E])
            for hi in range(2):
                P.op("dve", lambda e, hi=hi: e.tensor_scalar(
                    out=Efirst[:, :, hi * 256:hi * 256 + 128], in0=Emat[:, :, hi * 256:hi * 256 + 128],
                    scalar1=hval[:, 0:1], scalar2=None, op0=ALU.mult),
                    reads=[r_E, r_hval], writes=[r_E])

        P.barrier()

        r_W1 = [[P.res(f"W1{l}{f}") for f in range(2)] for l in range(NL)]
        r_W2 = [[P.res(f"W2{l}{f}") for f in range(2)] for l in range(NL)]
        r_WIN = [P.res(f"WIN{l}") for l in range(NL)]
        r_WG = [P.res(f"WG{l}") for l in range(NL)]
        r_WBR = [P.res(f"WBR{l}") for l in range(NL)]
        r_WO = [P.res(f"WO{l}") for l in range(NL)]
        r_WM = [P.res(f"WM{l}") for l in range(NL)]

        cast_tasks = []

        def cast(dst, src, r):
            cast_tasks.append((dst, src, r))

        def emit_casts(n=None):
            k = len(cast_tasks) if n is None else min(n, len(cast_tasks))
            for _ in range(k):
                dst, src, r = cast_tasks.pop(0)
                P.dma("poolq", lambda e, d=dst, s=src: e.dma_start(out=d, in_=s), r, writes=[r], accumulate=True)

        def cast_w1(l, f):
            src = w1i[f][l].rearrange("(c p) n -> p c n", p=128)
            for blk in range(11):
                dst = W1[l][f][blk].rearrange("p (c ab n) -> p c ab n", c=8, ab=2)
                for ab in range(2):
                    c0 = ab * FF + blk * 256
                    cast(dst[:, :, ab, :], src[:, :, c0:c0 + 256], r_W1[l][f])

        def cast_w2(l, f):
            src = w1o[f][l].rearrange("(j p) n -> p j n", p=128)
            for m in range(8):
                dst = W2[l][f][m].rearrange("p (j n) -> p j n", j=JH)
                cast(dst, src[:, :, m * 128:(m + 1) * 128], r_W2[l][f])

        def cast_cols(dst_blocks, src2d, nblk, width, r):
            src = src2d.rearrange("(c p) n -> p c n", p=128)
            for b in range(nblk):
                dst = dst_blocks[b].rearrange("p (c n) -> p c n", c=8)
                cast(dst, src[:, :, b * width:(b + 1) * width], r)

        need_s1 = set(layers_s1)
        need_s3 = set(layers_s2)
        for l in range(NL):
            cast(WM[l].rearrange("p (c n) -> p c n", c=8),
                 w_mkv[l].rearrange("(c p) n -> p c n", p=128), r_WM[l])
        for l in range(NL):
            if l in need_s1:
                cast_w1(l, 0)
                cast_w2(l, 0)
                cast_cols(WIN[l], w_in[l], 4, 512, r_WIN[l])
                if l == 0 and fused:
                    n_first = len(cast_tasks)
            if l in need_s3:
                cast_cols(WG[l], w_gate[l], 6, 512, r_WG[l])
                cast_cols(WBR[l], w_br[l], 2, 512, r_WBR[l])
                cast_cols(WO[l], w_o[l], 2, 512, r_WO[l])
                cast_w1(l, 1)
                cast_w2(l, 1)
        if fused:
            emit_casts(n_first)
        else:
            emit_casts()

        class WS:
            pass

        def alloc_token_stage():
            arena.reset()
            w = WS()
            w.X = [arena.alloc([NCH, TT], F32) for _ in range(2)]
            w.rX = [[P.res(f"X{b}_{c}") for c in range(NCH)] for b in range(2)]
            w.Y = arena.alloc([NCH, TT], F32)
            w.rY = P.res("Y")
            w.H = arena.alloc([NCH, TT], BF16)
            w.rH = [P.res(f"H{c}") for c in range(NCH)]
            w.XG = arena.alloc([NCH, TT], BF16)
            w.rXG = [P.res(f"XG{c}") for c in range(NCH)]
            w.G = arena.alloc([JH, TT], BF16)
            w.rG = [P.res(f"G{j}") for j in range(JH)]
            w.slots = [arena.alloc([4096], BF16) for _ in range(5)]
            w.rS = [P.res(f"slot{i}") for i in range(5)]
            w.si = 0
            w.rs = [arena.alloc([TT], F32) for _ in range(2)]
            w.r_rs = [P.res("rs0"), P.res("rs1")]
            w.rsi = 0
            w.tmp = [arena.alloc([TT], F32) for _ in range(3)]
            w.r_tmp = [P.res(f"tmp{i}") for i in range(3)]
            w.tmi = 0
            w.sil = [arena.alloc([TT], BF16) for _ in range(3)]
            w.r_sil = [P.res(f"sil{i}") for i in range(3)]
            w.sli = 0
            w.gt = [arena.alloc([TT], BF16) for _ in range(3)]
            w.r_gt = [P.res(f"gt{i}") for i in range(3)]
            w.M = arena.alloc([NCH, TT], BF16)
            w.rM = P.res("M")
            w.O = arena.alloc([NCH, TT], BF16)
            w.rO = P.res("Osb")
            w.QK = arena.alloc([12, TT], BF16)
            w.rQK = [P.res(f"QK{i}") for i in range(12)]
            w.V = arena.alloc([4, 512], BF16)
            w.rV = P.res("Vst")
            return w

        def slot_load(w, src_ap, r_src):
            i = w.si % len(w.slots)
            w.si += 1
            s, r = w.slots[i], w.rS[i]
            n = src_ap.shape[1]
            P.dma("sp", lambda e, s=s, a=src_ap: e.dma_start(out=s[:, 0:n], in_=a), r, reads=[r_src], writes=[r])
            return s, r

        def stat_chunk(w, c, src, r_src):
            ps, rps = stat_ps()
            P.op("act", lambda e: e.activation(out=w.H[:, c, :], in_=src, func=AF.Square),
                 reads=[r_src], writes=[w.rH[c]])
            P.mm([lambda e: e.matmul(ps[:, :], lhsT=ones[:, :], rhs=w.H[:, c, :],
                                     start=(c == 0), stop=(c == NCH - 1))],
                 reads=[w.rH[c], r_ones], writes=[rps])

        def rstd_from_stats(w, alpha=1.0):
            ps, rps = stat_ps()
            i = w.rsi % 2
            w.rsi += 1
            rs, r_rs = w.rs[i], w.r_rs[i]
            a2 = float(alpha) ** 2
            P.op("act", lambda e: e.activation(out=rs, in_=ps[:, :], func=AF.Ln, bias=epsb[:, 0:1] if a2 == 1.0 else epsb[:, 1:2],
                                               scale=1.0 / (D * a2)), reads=[rps, r_ones], writes=[r_rs])
            P.op("act", lambda e: e.activation(out=rs, in_=rs, func=AF.Exp, scale=-0.5), reads=[r_rs], writes=[r_rs])
            return rs, r_rs

        def h_from(w, X, rX, gcol0, rs, r_rs, gtile=None, r_g=None):
            gtile = gains if gtile is None else gtile
            r_g = r_gains if r_g is None else r_g
            for c in range(NCH):
                P.op("dve", lambda e, c=c: e.scalar_tensor_tensor(
                    out=w.H[:, c, :], in0=X[:, c, :], scalar=gtile[:, gcol0 + c:gcol0 + c + 1], in1=rs,
                    op0=ALU.mult, op1=ALU.mult), reads=[rX[c], r_rs, r_g], writes=[w.rH[c]])

        def prenorm(w, X, rX, gcol0, gtile=None, r_g=None):
            for c in range(NCH):
                stat_chunk(w, c, X[:, c, :], rX[c])
            rs, r_rs = rstd_from_stats(w)
            h_from(w, X, rX, gcol0, rs, r_rs, gtile, r_g)

        POOL_A = (1, 3, 5)
        POOL_B = (2, 5)

        def postnorm_add(w, X, rX, gcol0, alpha, then_pre=None):
            rs, r_rs = rstd_from_stats(w, alpha)
            for c in range(NCH):
                i = w.tmi % 3
                w.tmi += 1
                t, rt = w.tmp[i], w.r_tmp[i]
                P.op("pool" if c in POOL_A else "dve", lambda e, c=c, t=t: e.tensor_tensor(
                    out=t, in0=w.Y[:, c, :], in1=rs, op=ALU.mult), reads=[w.rY, r_rs], writes=[rt])
                P.op("dve", lambda e, c=c, t=t: e.tensor_tensor(
                    out=X[:, c, :], in0=X[:, c, :], in1=t, op=ALU.add), reads=[rt, rX[c]], writes=[rX[c]])
                if then_pre is not None:
                    stat_chunk(w, c, X[:, c, :], rX[c])
                    P.op("act", lambda e, c=c: e.mul(out=w.XG[:, c, :], in_=X[:, c, :],
                                                     mul=gains[:, then_pre + c:then_pre + c + 1]),
                         reads=[rX[c], r_gains], writes=[w.rXG[c]])
            if then_pre is not None:
                rs2, r_rs2 = rstd_from_stats(w)
                for c in range(NCH):
                    P.op("pool" if c in POOL_B else "dve", lambda e, c=c: e.tensor_tensor(
                        out=w.H[:, c, :], in0=w.XG[:, c, :], in1=rs2, op=ALU.mult),
                        reads=[w.rXG[c], r_rs2], writes=[w.rH[c]])

        def ffn(w, l, f):
            ycol = (l * 6 + (1 if f == 0 else 5)) * NCH
            for blk in range(11):
                s, rs_ = slot_load(w, W1[l][f][blk], r_W1[l][f])
                sv = s.rearrange("p (c ab n) -> p c ab n", c=8, ab=2)
                for jj in range(2):
                    j = blk * 2 + jj
                    pa, rpa = next_ps()
                    P.mm([lambda e, c=c, pa=pa, jj=jj, sv=sv: e.matmul(
                        pa[:, :], lhsT=sv[:, c, 0, jj * 128:(jj + 1) * 128], rhs=w.H[:, c, :],
                        start=(c == 0), stop=(c == NCH - 1)) for c in range(NCH)],
                        reads=[rs_] + w.rH, writes=[rpa])
                    pb, rpb = next_ps()
                    P.mm([lambda e, c=c, pb=pb, jj=jj, sv=sv: e.matmul(
                        pb[:, :], lhsT=sv[:, c, 1, jj * 128:(jj + 1) * 128], rhs=w.H[:, c, :],
                        start=(c == 0), stop=(c == NCH - 1)) for c in range(NCH)],
                        reads=[rs_] + w.rH, writes=[rpb])
                    i = w.sli % 3
                    w.sli += 1
                    sl, rsl = w.sil[i], w.r_sil[i]
                    P.op("act", lambda e, sl=sl, pa=pa: e.activation(out=sl, in_=pa[:, :], func=AF.Silu),
                         reads=[rpa], writes=[rsl])
                    P.op("dve", lambda e, sl=sl, pb=pb, j=j: e.tensor_tensor(
                        out=w.G[:, j, :], in0=sl, in1=pb[:, :], op=ALU.mult),
                        reads=[rsl, rpb], writes=[w.rG[j]])
            for m in range(8):
                s, rs_ = slot_load(w, W2[l][f][m][:, :], r_W2[l][f])
                sv = s[:, 0:JH * 128].rearrange("p (j n) -> p j n", j=JH)
                py, rpy = next_ps()
                P.mm([lambda e, j=j, py=py, sv=sv: e.matmul(
                    py[:, :], lhsT=sv[:, j, :], rhs=w.G[:, j, :], start=(j == 0), stop=(j == JH - 1))
                    for j in range(JH)], reads=[rs_] + w.rG, writes=[rpy])
                P.op("act", lambda e, m=m, py=py: e.mul(out=w.Y[:, m, :], in_=py[:, :],
                                                          mul=gains[:, ycol + m:ycol + m + 1]),
                     reads=[rpy, r_gains], writes=[w.rY])
                stat_chunk(w, m, py[:, :], rpy)

        def in_proj(w, l, t):
            ev = 0
            for blk in range(3):
                s, rs_ = slot_load(w, WIN[l][blk], r_WIN[l])
                sv = s.rearrange("p (c n) -> p c n", c=8)
                for cc in range(4):
                    qi = blk * 4 + cc
                    rate = Q_RATES[qi] if qi < 8 else K_RATES[qi - 8]
                    ps, rps = next_ps()
                    P.mm([lambda e, c=c, ps=ps, cc=cc, sv=sv: e.matmul(
                        ps[:, :], lhsT=sv[:, c, cc * 128:(cc + 1) * 128], rhs=w.H[:, c, :],
                        start=(c == 0), stop=(c == NCH - 1)) for c in range(NCH)],
                        reads=[rs_] + w.rH, writes=[rps])
                    if rate == 1:
                        dst, src = w.QK[:, qi, :], ps[:, :]
                    else:
                        dst = w.QK[:, qi, :].rearrange("p (j i) -> p i j", j=rate)
                        src = ps[:, :].rearrange("p (i j) -> p i j", j=rate)
                    if ev % 2 == 0:
                        P.op("act", lambda e, dst=dst, src=src: e.copy(out=dst, in_=src),
                             reads=[rps], writes=[w.rQK[qi]])
                    else:
                        P.op("dve", lambda e, dst=dst, src=src: e.tensor_copy(out=dst, in_=src),
                             reads=[rps], writes=[w.rQK[qi]])
                    ev += 1
            def st(dst, src, rr, dres):
                P.dma("poolq", lambda e, d=dst, s=src: e.dma_start(out=d, in_=s), rr[0],
                      reads=rr, writes=[dres], accumulate=True)

            tsl = slice(t * TT, (t + 1) * TT)
            st(QT_w[:, 0:4, tsl], w.QK[:, 0:4, :], w.rQK[0:4], r_QT)
            for qi in (4, 5):
                r = Q_RATES[qi]
                st(QT_w[:, qi, :].rearrange("p (j n) -> p j n", j=r)[:, :, t * (TT // r):(t + 1) * (TT // r)],
                   w.QK[:, qi, :].rearrange("p (j i) -> p j i", j=r), [w.rQK[qi]], r_QT)
            st(QT_w[:, 6:8, tsl], w.QK[:, 6:8, :], w.rQK[6:8], r_QT)
            st(KT_w[:, 0:2, tsl], w.QK[:, 8:10, :], w.rQK[8:10], r_KT)
            for ki in (2, 3):
                r = K_RATES[ki]
                st(KT_w[:, ki, :].rearrange("p (j n) -> p j n", j=r)[:, :, t * (TT // r):(t + 1) * (TT // r)],
                   w.QK[:, 8 + ki, :].rearrange("p (j i) -> p j i", j=r), [w.rQK[8 + ki]], r_KT)
            s, rs_ = slot_load(w, WIN[l][3], r_WIN[l])
            sv = s.rearrange("p (c n) -> p c n", c=8)
            for tb in range(4):
                ps, rps = next_ps()
                P.mm([lambda e, c=c, ps=ps, tb=tb, sv=sv: e.matmul(
                    ps[:, :], lhsT=w.H[:, c, tb * 128:(tb + 1) * 128], rhs=sv[:, c, :],
                    start=(c == 0), stop=(c == NCH - 1)) for c in range(NCH)],
                    reads=[rs_] + w.rH, writes=[rps])
                P.op("dve", lambda e, tb=tb, ps=ps: e.tensor_copy(out=w.V[:, tb, :], in_=ps[:, :]),
                     reads=[rps], writes=[w.rV])
            for g in range(4):
                r = K_RATES[g]
                src = w.V[:, :, g * 128:(g + 1) * 128]
                if r == 1:
                    dst = VT_w[g, t * TT:(t + 1) * TT, :].rearrange("(tb p) d -> p tb d", p=128)
                    st(dst, src, [w.rV], r_VT)
                else:
                    npr = TOK // r
                    for j in range(r):
                        dst = VT_w[g, j * npr + t * (TT // r):j * npr + (t + 1) * (TT // r), :] \
                            .rearrange("(tb i) d -> i tb d", tb=4)
                        st(dst, w.V[j:128:r, :, g * 128:(g + 1) * 128], [w.rV], r_VT)

        r_QT, r_KT, r_VT, r_OT = P.res("QT"), P.res("KT"), P.res("VT"), P.res("OT")
        r_XS = [P.res(f"XS{t}") for t in range(NT)]
        r_HS, r_HR = P.res("HS"), P.res("HR")
        r_out = P.res("out")

        def load_x_tokmajor(w, t, X, rX):
            xin = w.Y
            xv = xin.rearrange("p c t -> p (c t)").rearrange("p (tb f) -> p tb f", tb=4)
            P.dma("sp", lambda e: e.dma_start(
                out=xv, in_=x_in[t * TT:(t + 1) * TT, :].rearrange("(tb p) f -> p tb f", p=128)),
                w.rY, writes=[w.rY])
            for c in range(NCH):
                ps, rps = next_ps()
                P.mm([lambda e, tb=tb, ps=ps, c=c: e.transpose(
                    ps[:, tb * 128:(tb + 1) * 128], xv[:, tb, c * 128:(c + 1) * 128], ident[:, :])
                    for tb in range(4)], reads=[w.rY, r_ident], writes=[rps])
                if c % 2 == 0:
                    P.op("act", lambda e, c=c, ps=ps: e.copy(out=X[:, c, :], in_=ps[:, :]), reads=[rps], writes=[rX[c]])
                else:
                    P.op("dve", lambda e, c=c, ps=ps: e.tensor_copy(out=X[:, c, :], in_=ps[:, :]),
                         reads=[rps], writes=[rX[c]])

        def store_out_tokmajor(w, t, X, rX):
            xo = w.Y
            xv = xo.rearrange("p c t -> p (c t)").rearrange("p (tb f) -> p tb f", tb=4)
            for tb in range(4):
                for hf in range(2):
                    ps, rps = next_ps()
                    P.mm([lambda e, cc=cc, ps=ps, tb=tb, hf=hf: e.transpose(
                        ps[:, cc * 128:(cc + 1) * 128], X[:, hf * 4 + cc, tb * 128:(tb + 1) * 128], ident[:, :])
                        for cc in range(4)], reads=rX + [r_ident], writes=[rps])
                    if hf == 0:
                        P.op("act", lambda e, ps=ps, tb=tb: e.copy(out=xv[:, tb, 0:512], in_=ps[:, :]),
                             reads=[rps], writes=[w.rY])
                    else:
                        P.op("dve", lambda e, ps=ps, tb=tb: e.tensor_copy(out=xv[:, tb, 512:1024], in_=ps[:, :]),
                             reads=[rps], writes=[w.rY])
            P.dma("poolq", lambda e: e.dma_start(
                out=out_ap[t * TT:(t + 1) * TT, :].rearrange("(tb p) f -> p tb f", p=128), in_=xv),
                w.rY, reads=[w.rY], writes=[r_out], accumulate=True)

        def s1_body(w, l, t, X, rX, pre_done=False):
            gb = (l * 6) * NCH
            if not pre_done:
                prenorm(w, X, rX, gb + 0 * NCH)
            ffn(w, l, 0)
            postnorm_add(w, X, rX, gb + 1 * NCH, 0.5, then_pre=gb + 2 * NCH)
            P.dma("poolq", lambda e: e.dma_start(out=XS_w[t], in_=X.rearrange("p c t -> p (c t)")),
                  rX[0], reads=rX, writes=[r_XS[t]])
            in_proj(w, l, t)

        def mixing(w, l, t, X, rX):
            ycol = (l * 6 + 3) * NCH
            s_o = None
            for c in range(NCH):
                hf, cc = divmod(c, 4)
                if cc == 0:
                    s_brh = slot_load(w, WBR[l][hf], r_WBR[l])
                    s_g = [slot_load(w, WG[l][br * 2 + hf], r_WG[l]) for br in range(3)]
                brv = s_brh[0].rearrange("p (k n) -> p k n", k=8)
                kr = [(0, 3), (3, 6), (6, 8)]
                mts = []
                for br in range(3):
                    gv = s_g[br][0].rearrange("p (k n) -> p k n", k=8)
                    pg, rpg = next_ps()
                    P.mm([lambda e, k=k, pg=pg, gv=gv, cc=cc: e.matmul(
                        pg[:, :], lhsT=gv[:, k, cc * 128:(cc + 1) * 128], rhs=w.H[:, k, :],
                        start=(k == 0), stop=(k == NCH - 1)) for k in range(NCH)],
                        reads=[s_g[br][1]] + w.rH, writes=[rpg])
                    pb, rpb = next_ps()
                    k0, k1 = kr[br]
                    P.mm([lambda e, k=k, pb=pb, brv=brv, cc=cc, k0=k0, k1=k1: e.matmul(
                        pb[:, :], lhsT=brv[:, k, cc * 128:(cc + 1) * 128], rhs=w.O[:, k, :],
                        start=(k == k0), stop=(k == k1 - 1)) for k in range(k0, k1)],
                        reads=[s_brh[1], w.rO], writes=[rpb])
                    gt, rgt = w.gt[br], w.r_gt[br]
                    bcol = (l * 3 + br) * NCH + c
                    P.op("act", lambda e, gt=gt, pg=pg, bcol=bcol: e.activation(
                        out=gt, in_=pg[:, :], func=AF.Sigmoid, bias=bgate[:, bcol:bcol + 1], scale=1.0),
                        reads=[rpg, r_bgate], writes=[rgt])
                    i = w.tmi % 3
                    w.tmi += 1
                    mt_, rmt = w.tmp[i], w.r_tmp[i]
                    P.op("dve", lambda e, mt_=mt_, gt=gt, pb=pb: e.tensor_tensor(
                        out=mt_, in0=gt, in1=pb[:, :], op=ALU.mult), reads=[rgt, rpb], writes=[rmt])
                    mts.append((mt_, rmt))
                P.op("dve", lambda e, a=mts[0][0], b=mts[1][0]: e.tensor_tensor(out=a, in0=a, in1=b, op=ALU.add),
                     reads=[mts[1][1]], writes=[mts[0][1]])
                P.op("dve", lambda e, a=mts[0][0], b=mts[2][0], c=c: e.tensor_tensor(
                    out=w.M[:, c, :], in0=a, in1=b, op=ALU.add),
                    reads=[mts[0][1], mts[2][1]], writes=[w.rM])
            for m in range(8):
                hf, mm_ = divmod(m, 4)
                if mm_ == 0:
                    s_o = slot_load(w, WO[l][hf], r_WO[l])
                ov = s_o[0].rearrange("p (k n) -> p k n", k=8)
                py, rpy = next_ps()
                P.mm([lambda e, k=k, py=py, ov=ov, mm_=mm_: e.matmul(
                    py[:, :], lhsT=ov[:, k, mm_ * 128:(mm_ + 1) * 128], rhs=w.M[:, k, :],
                    start=(k == 0), stop=(k == NCH - 1)) for k in range(NCH)],
                    reads=[s_o[1], w.rM], writes=[rpy])
                P.op("act", lambda e, m=m, py=py: e.mul(out=w.Y[:, m, :], in_=py[:, :],
                                                          mul=gains[:, ycol + m:ycol + m + 1]),
                     reads=[rpy, r_gains], writes=[w.rY])
                stat_chunk(w, m, py[:, :], rpy)

        def s3_load_x(w, t):
            X, rX = w.X[t % 2], w.rX[t % 2]
            P.dma("sp", lambda e: e.dma_start(out=X.rearrange("p c t -> p (c t)"), in_=XS_r[t]),
                  rX[0], reads=[r_XS[t]], writes=rX)

        def s3_body(w, l, t, X, rX):
            gb = (l * 6) * NCH
            if t == 0:
                s3_load_x(w, 0)
            if t + 1 < NT:
                s3_load_x(w, t + 1)
            P.dma("sp", lambda e: e.dma_start(out=w.O, in_=OT[:, :, t * TT:(t + 1) * TT]),
                  w.rO, reads=[r_OT], writes=[w.rO])
            prenorm(w, X, rX, gb + 2 * NCH)
            mixing(w, l, t, X, rX)
            postnorm_add(w, X, rX, gb + 3 * NCH, 1.0, then_pre=gb + 4 * NCH)
            ffn(w, l, 1)
            nxt = ((l + 1) * 6) * NCH if (l + 1 < NL and (l + 1) in layers_s1) else None
            postnorm_add(w, X, rX, gb + 5 * NCH, 0.5, then_pre=nxt)

        def mem_kv(w):
            X, rX = w.X[0], w.rX[0]
            mv = w.Y.rearrange("p c t -> p (c t)")[:, 0:2 * D].rearrange("p (tb f) -> p tb f", tb=2)
            P.dma("sp", lambda e: e.dma_start(out=mv, in_=mem_in[:, :].rearrange("(tb p) f -> p tb f", p=128)),
                  w.rY, writes=[w.rY])
            P.op("pool", lambda e: e.memset(X.rearrange("p c t -> p (c t)"), 1.0), writes=rX)
            for c in range(NCH):
                ps, rps = next_ps()
                P.mm([lambda e, tb=tb, ps=ps, c=c: e.transpose(
                    ps[:, tb * 128:(tb + 1) * 128], mv[:, tb, c * 128:(c + 1) * 128], ident[:, :])
                    for tb in range(2)], reads=[w.rY, r_ident], writes=[rps])
                P.op("dve", lambda e, c=c, ps=ps: e.tensor_copy(out=X[:, c, 0:256], in_=ps[:, 0:256]),
                     reads=[rps], writes=[rX[c]])
            for l in sorted(set(layers_s2)):
                prenorm(w, X, rX, l * NCH, gtile=mgain, r_g=r_mgain)
                s, rs_ = slot_load(w, WM[l][:, :], r_WM[l])
                sv = s.rearrange("p (c n) -> p c n", c=8)
                for mc in range(2):
                    ps, rps = next_ps()
                    P.mm([lambda e, c=c, ps=ps, mc=mc, sv=sv: e.matmul(
                        ps[:, 0:256], lhsT=sv[:, c, mc * 128:(mc + 1) * 128], rhs=w.H[:, c, 0:256],
                        start=(c == 0), stop=(c == NCH - 1)) for c in range(NCH)],
                        reads=[rs_] + w.rH, writes=[rps])
                    P.op("dve", lambda e, ps=ps, mc=mc, l=l: e.tensor_copy(out=KC[0:64, l, mc, 0, :], in_=ps[0:64, 0:256]),
                         reads=[rps], writes=[r_const])
                    P.op("dve", lambda e, ps=ps, mc=mc, l=l: e.tensor_copy(out=KC[64:128, l, mc, 1, :], in_=ps[64:128, 0:256]),
                         reads=[rps], writes=[r_const])
                for kt in range(2):
                    ps, rps = next_ps()
                    P.mm([lambda e, c=c, ps=ps, kt=kt, sv=sv: e.matmul(
                        ps[:, 0:256], lhsT=w.H[:, c, kt * 128:(kt + 1) * 128], rhs=sv[:, c, 256:512],
                        start=(c == 0), stop=(c == NCH - 1)) for c in range(NCH)],
                        reads=[rs_] + w.rH, writes=[rps])
                    P.op("dve", lambda e, ps=ps, kt=kt, l=l: e.tensor_copy(out=VC[:, l, kt, :], in_=ps[:, 0:256]),
                         reads=[rps], writes=[r_const])

        def s2_stage(l):
            arena.reset()
            NQT = TOK // 128
            Qsb = [arena.alloc([TOK], BF16) for _ in range(2)]
            rQ = [P.res("Qsb0"), P.res("Qsb1")]
            KMAX = 16 * 128 + TOK
            KA = [arena.alloc([KMAX], BF16) for _ in range(2)]
            KB = [arena.alloc([KMAX], BF16) for _ in range(2)]
            rK = [P.res("Ksb0"), P.res("Ksb1")]
            for i in range(2):
                P.op("pool", lambda e, i=i: e.memset(KA[i][64:128, :], 0.0), writes=[rK[i]])
                P.op("pool", lambda e, i=i: e.memset(KB[i][0:64, :], 0.0), writes=[rK[i]])
            VBLK = 16 + TOK // 128
            Vsb = [arena.alloc([VBLK * 128], BF16) for _ in range(2)]
            rV = [P.res("Vsb0"), P.res("Vsb1")]
            Oacc = [arena.alloc([TOK], BF16) for _ in range(3)]
            rOa = [P.res(f"Oacc{g}") for g in range(3)]
            Dacc = [arena.alloc([TOK], F32)]
            rDa = [P.res("Dacc0")]
            Ost = [arena.alloc([TOK], BF16) for _ in range(2)]
            rOst = [P.res("Ost0"), P.res("Ost1")]
            Pex = [arena.alloc([512], BF16) for _ in range(4)]
            rPex = [P.res(f"Pex{i}") for i in range(4)]
            PT = [arena.alloc([512], BF16) for _ in range(4)]
            rPT = [P.res(f"PT{i}") for i in range(4)]
            Dr = [arena.alloc([512], F32) for _ in range(2)]
            rDr = [P.res("Dr0"), P.res("Dr1")]
            cnt = {"q": 0, "kv": 0, "p": 0, "d": 0, "o": 0}

            def load_q(qchunk):
                i = cnt["q"] % 2
                cnt["q"] += 1
                P.dma("sp", lambda e, i=i: e.dma_start(out=Qsb[i], in_=QT_r[:, qchunk, :]), rQ[i],
                      reads=[r_QT], writes=[rQ[i]])
                return Qsb[i], rQ[i]

            def load_kv(kchunk, vgroup, r):
                i = cnt["kv"] % 2
                cnt["kv"] += 1
                npos = TOK // r
                nb = npos // 128
                kvA = KA[i][:, 0:r * (128 + npos)].rearrange("p (j n) -> p j n", j=r)
                kvB = KB[i][:, 0:r * (128 + npos)].rearrange("p (j n) -> p j n", j=r)
                kv = (kvA, kvB)
                vv = Vsb[i][:, 0:r * (nb + 1) * 128].rearrange("p (j b d) -> p j b d", j=r, b=nb + 1)
                koff = sum(K_RATES[:kchunk]) * 128
                hk = HR[0:HROWS // 2, :].rearrange("a b -> (a b)").rearrange("(p n) -> p n", p=128)
                hv = HR[HROWS // 2:HROWS, :].rearrange("a b -> (a b)").rearrange("(n d) -> n d", d=128)
                first_dma = True
                for hi, kz in enumerate(kv):
                    ps_ = slice(hi * 64, (hi + 1) * 64)
                    P.dma("sp", lambda e, kz=kz, ps_=ps_: e.dma_start(
                        out=kz[ps_, :, 0:128], in_=hk[ps_, koff:koff + r * 128].rearrange("p (j n) -> p j n", j=r)),
                        rK[i], reads=[r_HR], writes=[rK[i]], accumulate=not first_dma)
                    first_dma = False
                    P.dma("sp", lambda e, kz=kz, ps_=ps_: e.dma_start(
                        out=kz[ps_, :, 128:], in_=KT_r[ps_, kchunk, :].rearrange("p (j n) -> p j n", j=r)),
                        rK[i], reads=[r_KT], writes=[rK[i]], accumulate=True)
                P.dma("sp", lambda e: e.dma_start(
                    out=vv[:, :, 0, :], in_=hv[koff:koff + r * 128, :].rearrange("(j p) d -> p j d", p=128)),
                    rV[i], reads=[r_HR], writes=[rV[i]])
                for j in range(r):
                    P.dma("sp", lambda e, j=j: e.dma_start(
                        out=vv[:, j, 1:, :],
                        in_=VT_r[vgroup, j * npos:(j + 1) * npos, :].rearrange("(b p) d -> p b d", p=128)),
                        rV[i], reads=[r_VT], writes=[rV[i]], accumulate=True)
                return kv, rK[i], vv, rV[i]

            def run_chunk(nqt, Q, rQ_, key_fn, val_fn, E_fn, rKV, sink_for_group):
                def s1(qt):
                    qcols = slice(qt * 128, (qt + 1) * 128)
                    pss, rpss = next_ps()
                    P.mm([lambda e, hi=hi, pc=pc, pss=pss, qt=qt, qcols=qcols: e.matmul(
                        pss[:, (hi * 2 + pc) * 128:(hi * 2 + pc + 1) * 128],
                        lhsT=key_fn(qt, hi, pc), rhs=Q[:, qcols], start=True, stop=True)
                        for hi in range(2) for pc in range(2)], reads=[rQ_] + rKV, writes=[rpss])
                    i = cnt["p"] % 4
                    cnt["p"] += 1
                    E = E_fn(qt)
                    if E is None:
                        P.op("act", lambda e, i=i, pss=pss: e.activation(
                            out=PT[i], in_=pss[:, :], func=AF.Exp, scale=0.125), reads=[rpss], writes=[rPT[i]])
                    else:
                        P.op("act", lambda e, i=i, pss=pss: e.activation(
                            out=Pex[i], in_=pss[:, :], func=AF.Exp, scale=0.125), reads=[rpss], writes=[rPex[i]])
                        P.op("pool" if qt % 3 == 2 else "dve",
                             lambda e, i=i, E=E: e.tensor_tensor(out=PT[i], in0=Pex[i], in1=E, op=ALU.mult),
                             reads=[rPex[i], r_E], writes=[rPT[i]])
                    return i

                def s2(qt, i, qi, pso, rpso, psd, rpsd):
                    fns = []
                    for hi in range(2):
                        for pc in range(2):
                            fns.append(lambda e, hi=hi, pc=pc, i=i, qt=qt, qi=qi: e.matmul(
                                pso[hi * 64:(hi + 1) * 64, qi * 128:(qi + 1) * 128],
                                lhsT=val_fn(qt, hi, pc), rhs=PT[i][:, (hi * 2 + pc) * 128:(hi * 2 + pc + 1) * 128],
                                start=(pc == 0), stop=(pc == 1)))
                    P.mm(fns, reads=[rPT[i]] + rKV, writes=[rpso])
                    fns = []
                    for hi in range(2):
                        for pc in range(2):
                            fns.append(lambda e, hi=hi, pc=pc, i=i, qi=qi: e.matmul(
                                psd[hi * 64:(hi + 1) * 64, qi * 128:(qi + 1) * 128],
                                lhsT=ones[:, hi * 64:(hi + 1) * 64], rhs=PT[i][:, (hi * 2 + pc) * 128:(hi * 2 + pc + 1) * 128],
                                start=(pc == 0), stop=(pc == 1)))
                    P.mm(fns, reads=[rPT[i], r_ones], writes=[rpsd])

                LA = 2
                pend = [s1(q) for q in range(min(LA, nqt))]
                banks = None
                for qt in range(nqt):
                    if qt + LA < nqt:
                        pend.append(s1(qt + LA))
                    if qt % 4 == 0:
                        banks = next_ps() + next_ps()
                    s2(qt, pend.pop(0), qt % 4, *banks)
                    if qt % 4 == 3:
                        sink_for_group(qt - 3)(*banks)

            def finish_direct(sink_col, Ot, rOt, c0):
                def f(pso, rpso, psd, rpsd):
                    i = cnt["d"] % 2
                    cnt["d"] += 1
                    if sink_col is not None:
                        P.op("act", lambda e, i=i: e.activation(
                            out=Dr[i], in_=psd[:, :], func=AF.Ln, bias=sinke[:, sink_col:sink_col + 1], scale=1.0),
                            reads=[rpsd, r_sink], writes=[rDr[i]])
                    else:
                        P.op("act", lambda e, i=i: e.activation(out=Dr[i], in_=psd[:, :], func=AF.Ln),
                             reads=[rpsd], writes=[rDr[i]])
                    P.op("act", lambda e, i=i: e.activation(out=Dr[i], in_=Dr[i], func=AF.Exp, scale=-1.0),
                         reads=[rDr[i]], writes=[rDr[i]])
                    P.op("dve", lambda e, i=i, c0=c0: e.tensor_tensor(
                        out=Ot[:, c0:c0 + 512], in0=pso[:, :], in1=Dr[i], op=ALU.mult),
                        reads=[rpso, rDr[i]], writes=[rOt])
                return f

            kv_cache = {}
            for sc, (qch, kch, vg, r, md, cols, has_sink) in enumerate(SELF_CHUNKS):
                if ("s2_sw" in DBG_SKIP and has_sink) or ("s2_dil" in DBG_SKIP and not has_sink):
                    continue
                Q, rQ_ = load_q(qch)
                if (kch, vg) not in kv_cache:
                    kv_cache.clear()
                    kv_cache[(kch, vg)] = load_kv(kch, vg, r)
                kv, rKc, vv, rVc = kv_cache[(kch, vg)]
                npos = TOK // r
                nb = npos // 128

                def key_fn(qt, hi, pc, kv=kv, nb=nb):
                    j, b = divmod(qt, nb)
                    return kv[hi][:, j, b * 128 + pc * 128:b * 128 + pc * 128 + 128]

                def val_fn(qt, hi, pc, vv=vv, nb=nb):
                    j, b = divmod(qt, nb)
                    return vv[:, j, b + pc, hi * 64:(hi + 1) * 64]

                def E_fn(qt, sc=sc, nb=nb):
                    return (Efirst if qt % nb == 0 else Emat)[:, sc, :]

                if has_sink:
                    oi = cnt["o"] % 2
                    cnt["o"] += 1
                    run_chunk(NQT, Q, rQ_, key_fn, val_fn, E_fn, [rKc, rVc],
                              lambda qt0, sc=sc, oi=oi: finish_direct(l * 3 + sc, Ost[oi], rOst[oi], qt0 * 128))
                    P.dma("poolq", lambda e, oi=oi, qch=qch: e.dma_start(out=OT[:, qch, :], in_=Ost[oi]), rOst[oi],
                          reads=[rOst[oi]], writes=[r_OT], accumulate=True)
                else:
                    g = sc - 3
                    ov3 = Oacc[g].rearrange("p (n r) -> p r n", r=r)
                    dv3 = Dacc[0].rearrange("p (n r) -> p r n", r=r)

                    def sink_acc(pso, rpso, psd, rpsd, qt0=None):
                        pass

                    def dil_sink(qt0, nb=nb, ov3=ov3, dv3=dv3, g=g):
                        if nb >= 4:
                            j, b0 = divmod(qt0, nb)
                            od = ov3[:, j, b0 * 128:b0 * 128 + 512]
                            dd = dv3[:, j, b0 * 128:b0 * 128 + 512]
                            shp = None
                        else:
                            a = 4 // nb
                            j0 = qt0 // nb
                            od = ov3[:, j0:j0 + a, :]
                            dd = dv3[:, j0:j0 + a, :]
                            shp = a

                        def f(pso, rpso, psd, rpsd, od=od, dd=dd, shp=shp, g=g):
                            so = pso[:, :] if shp is None else pso[:, :].rearrange("p (a n) -> p a n", a=shp)
                            sd = psd[:, :] if shp is None else psd[:, :].rearrange("p (a n) -> p a n", a=shp)
                            P.op("act", lambda e: e.copy(out=od, in_=so), reads=[rpso], writes=[rOa[g]])
                            if g == 0:
                                P.op("dve", lambda e: e.tensor_copy(out=dd, in_=sd), reads=[rpsd], writes=[rDa[0]])
                            else:
                                P.op("dve", lambda e: e.tensor_tensor(out=dd, in0=dd, in1=sd, op=ALU.add),
                                     reads=[rpsd], writes=[rDa[0]])
                        return f

                    run_chunk(NQT, Q, rQ_, key_fn, val_fn, E_fn, [rKc, rVc], dil_sink)
            if "s2_dil" in DBG_SKIP:
                P.op("pool", lambda e: e.memset(Dacc[0], 1.0), writes=[rDa[0]])
                for g in range(3):
                    P.op("pool", lambda e, g=g: e.memset(Oacc[g], 1.0), writes=[rOa[g]])
            if "s2_comb" not in DBG_SKIP:
                P.op("dve", lambda e: e.reciprocal(out=Dacc[0], in_=Dacc[0]), reads=[rDa[0]], writes=[rDa[0]])
            for g in range(3 if "s2_comb" not in DBG_SKIP else 0):
                oi = cnt["o"] % 2
                cnt["o"] += 1
                P.op("dve", lambda e, g=g, oi=oi: e.tensor_tensor(out=Ost[oi], in0=Oacc[g], in1=Dacc[0], op=ALU.mult),
                     reads=[rOa[g], rDa[0]], writes=[rOst[oi]])
                P.dma("poolq", lambda e, oi=oi, g=g: e.dma_start(out=OT[:, 3 + g, :], in_=Ost[oi]), rOst[oi],
                      reads=[rOst[oi]], writes=[r_OT], accumulate=True)
            for mc in range(2 if "s2_mem" not in DBG_SKIP else 0):
                Q, rQ_ = load_q(6 + mc)

                def key_fn(qt, hi, pc, mc=mc):
                    return KC[:, l, mc, hi, pc * 128:(pc + 1) * 128]

                def val_fn(qt, hi, pc, mc=mc):
                    return VC[:, l, pc, mc * 128 + hi * 64:mc * 128 + (hi + 1) * 64]

                oi = cnt["o"] % 2
                cnt["o"] += 1
                run_chunk(NQT, Q, rQ_, key_fn, val_fn, lambda qt: None, [r_const],
                          lambda qt0, oi=oi: finish_direct(None, Ost[oi], rOst[oi], qt0 * 128))
                P.dma("poolq", lambda e, oi=oi, mc=mc: e.dma_start(out=OT[:, 6 + mc, :], in_=Ost[oi]), rOst[oi],
                      reads=[rOst[oi]], writes=[r_OT], accumulate=True)

        def halo_send(dst_buf):
            hk = dst_buf[0:HROWS // 2, :].rearrange("a b -> (a b)").rearrange("(p n) -> p n", p=128)
            hv = dst_buf[HROWS // 2:HROWS, :].rearrange("a b -> (a b)").rearrange("(n d) -> n d", d=128)
            for kch in range(4):
                r = K_RATES[kch]
                npos = TOK // r
                koff = sum(K_RATES[:kch]) * 128
                P.dma("sp", lambda e, kch=kch, r=r, npos=npos, koff=koff: e.dma_start(
                    out=hk[:, koff:koff + r * 128].rearrange("p (j n) -> p j n", j=r),
                    in_=KT_w[:, kch, :].rearrange("p (j n) -> p j n", j=r)[:, :, npos - 128:npos]),
                    r_HS, reads=[r_KT], writes=[r_HS], accumulate=True)
                P.dma("sp", lambda e, kch=kch, r=r, npos=npos, koff=koff: e.dma_start(
                    out=hv[koff:koff + r * 128, :].rearrange("(j p) d -> j p d", p=128),
                    in_=VT_w[kch, :, :].rearrange("(j n) d -> j n d", j=r)[:, npos - 128:npos, :]),
                    r_HS, reads=[r_VT], writes=[r_HS], accumulate=True)

        if not first:
            pass
        wts = alloc_token_stage()
        if layers_s2:
            mem_kv(wts)
        if first:
            per_tile = -(-len(cast_tasks) // max(NT - 1, 1))
            for t in range(NT):
                X, rX = wts.X[t % 2], wts.rX[t % 2]
                load_x_tokmajor(wts, t, X, rX)
                s1_body(wts, 0, t, X, rX)
                emit_casts(per_tile)
            emit_casts()
        for l in range(NL):
            if l not in layers_s2:
                continue
            if fused:
                halo_send(HS)
                r_cc = P.res("cc")
                P.dma("poolq", lambda e: e.collective_compute(
                    "AllGather", ALU.bypass, replica_groups=[[2 * i, 2 * i + 1] for i in range(n_cores // 2)],
                    ins=[HS[:, :].opt()], outs=[HR[:, :].opt()]), r_cc, reads=[r_HS], writes=[r_cc, r_HR], inc=1)
                P.op("pool", lambda e: e.memset(ccdummy[:, :], 0.0), reads=[r_cc], writes=[r_HR])
            P.barrier()
            if "s2" not in DBG_SKIP:
                s2_stage(l)
            P.barrier()
            wts = alloc_token_stage()
            for t in range(NT if "s3" not in DBG_SKIP else 0):
                X, rX = wts.X[t % 2], wts.rX[t % 2]
                s3_body(wts, l, t, X, rX)
                if l + 1 < NL and (l + 1) in layers_s1:
                    s1_body(wts, l + 1, t, X, rX, pre_done=True)
                elif l == NL - 1:
                    store_out_tokmajor(wts, t, X, rX)
        P.barrier(streams=("sp",))
        with nc.Block() as block:
            P.emit(block)
    return nc


_CACHE = {}


def _get_nc(TOK, mode, n_cores=N_CORES):
    key = (TOK, mode, n_cores)
    if key not in _CACHE:
        _CACHE[key] = build(TOK, mode, n_cores)
    return _CACHE[key]


def kernel(**inputs):
    x = np.asarray(inputs["x"], np.float32)
    mem = np.asarray(inputs["mem"], np.float32)
    B, S, _ = x.shape
    TOK = B * S // N_CORES
    halves = S // TOK
    sh = prep_shared(inputs)
    in_maps = []
    for c in range(N_CORES):
        b, hf = divmod(c, halves)
        m = dict(sh)
        m["x"] = np.ascontiguousarray(x[b, hf * TOK:(hf + 1) * TOK, :])
        m["mem"] = np.ascontiguousarray(mem[b])
        m["halo_valid"] = np.full((128, 1), 1.0 if hf > 0 else 0.0, np.float32)
        in_maps.append(m)
    nc = _get_nc(TOK, "fused")
    res = run_bass_kernel_spmd(nc, in_maps, core_ids=list(range(N_CORES)))
    out = np.empty((B, S, D), np.float32)
    for c in range(N_CORES):
        b, hf = divmod(c, halves)
        out[b, hf * TOK:(hf + 1) * TOK, :] = res.results[c]["out"]
    return out


def _halo_from(KT, VT, TOK):
    HROWS = 2 * 2816 * 128 // 1024
    hk = np.zeros((128, 2816), KT.dtype)
    hv = np.zeros((2816, 128), VT.dtype)
    for kch in range(4):
        r = K_RATES[kch]
        npos = TOK // r
        koff = sum(K_RATES[:kch]) * 128
        kk = KT[:, kch, :].reshape(128, r, npos)[:, :, npos - 128:]
        hk[:, koff:koff + r * 128] = kk.reshape(128, r * 128)
        vv = VT[kch].reshape(r, npos, 128)[:, npos - 128:, :]
        hv[koff:koff + r * 128, :] = vv.reshape(r * 128, 128)
    HR = np.zeros((2 * HROWS, 1024), KT.dtype)
    HR[0:HROWS // 2] = hk.reshape(HROWS // 2, 1024)
    HR[HROWS // 2:HROWS] = hv.reshape(HROWS // 2, 1024)
    return HR


def kernel_unfused(n_cores=N_CORES, debug=None, **inputs):
    x = np.asarray(inputs["x"], np.float32)
    mem = np.asarray(inputs["mem"], np.float32)
    B, S, _ = x.shape
    TOK = B * S // n_cores
    halves = S // TOK
    sh = prep_shared(inputs)
    base = []
    for c in range(n_cores):
        b, hf = divmod(c, halves)
        m = dict(sh)
        m["mem"] = np.ascontiguousarray(mem[b])
        m["halo_valid"] = np.full((128, 1), 1.0 if hf > 0 else 0.0, np.float32)
        base.append(m)
    cores = list(range(n_cores))

    def halos(res):
        hr = []
        for c in range(n_cores):
            b, hf = divmod(c, halves)
            src = c - 1 if hf > 0 else c
            hr.append(_halo_from(np.asarray(res[src]["KT_o"]), np.asarray(res[src]["VT_o"]), TOK))
        return hr

    ins = []
    for c in range(n_cores):
        b, hf = divmod(c, halves)
        m = dict(base[c])
        m["x"] = np.ascontiguousarray(x[b, hf * TOK:(hf + 1) * TOK, :])
        ins.append(m)
    ra = run_bass_kernel_spmd(_get_nc(TOK, "A", n_cores), ins, core_ids=cores).results
    if debug is not None:
        debug["A"] = ra
    hr = halos(ra)
    ins = []
    for c in range(n_cores):
        m = dict(base[c])
        m.update({"XS_i": ra[c]["XS_o"], "QT_i": ra[c]["QT_o"], "KT_i": ra[c]["KT_o"], "VT_i": ra[c]["VT_o"],
                  "HR": hr[c]})
        ins.append(m)
    rb = run_bass_kernel_spmd(_get_nc(TOK, "B", n_cores), ins, core_ids=cores).results
    if debug is not None:
        debug["B"] = rb
    hr = halos(rb)
    ins = []
    for c in range(n_cores):
        m = dict(base[c])
        m.update({"XS_i": rb[c]["XS_o"], "QT_i": rb[c]["QT_o"], "KT_i": rb[c]["KT_o"], "VT_i": rb[c]["VT_o"],
                  "HR": hr[c]})
        ins.append(m)
    rc = run_bass_kernel_spmd(_get_nc(TOK, "C", n_cores), ins, core_ids=cores).results
    if debug is not None:
        debug["C"] = rc
    out = np.empty((B, S, D), np.float32)
    for c in range(n_cores):
        b, hf = divmod(c, halves)
        out[b, hf * TOK:(hf + 1) * TOK, :] = rc[c]["out"]
    return out
```
